# Optimizing a Trainium2 kernel written in Bass

```python
import jax, jax.numpy as jnp
from jax import lax
import numpy as np

D_MODEL = 2048
BATCH = 4
SEQ = 4096
DEPTH = 4

CHUNK = 64
LEFT_CHUNKS = 8
BAND_CHUNKS = LEFT_CHUNKS + 1
Q_BLOCK = 128

A_HEAD_DIM = 128
A_HEADS = D_MODEL // (2 * A_HEAD_DIM)
A_WIDTH = A_HEADS * A_HEAD_DIM
REL_CLIP = 128

B_NOPE = 128
B_ROPE = 64
B_V = 128
B_HEADS = D_MODEL // (2 * B_V)
B_WIDTH = B_HEADS * B_V
Q_LORA = 768
KV_LORA = 512
ROPE_THETA = 10000.0

MIX_WIDTH = A_WIDTH + B_WIDTH
IN_SIZES = (A_WIDTH, A_WIDTH, A_WIDTH, Q_LORA, KV_LORA, B_ROPE)
IN_WIDTH = sum(IN_SIZES)
IN_SPLITS = tuple(int(v) for v in np.cumsum(IN_SIZES)[:-1])

N_MEM = 256
X_HEADS = 4
X_HEAD_DIM = 128
X_WIDTH = X_HEADS * X_HEAD_DIM

D_FF = 256 * (-(-8 * D_MODEL // (3 * 256)))

EPS = 1e-6

kernel_name = "hybrid_chunked_relpos_mla_memory_encoder"


def rms_norm(x, g):
    xf = x.astype(jnp.float32)
    y = xf * lax.rsqrt(jnp.mean(xf * xf, axis=-1, keepdims=True) + EPS)
    return y.astype(x.dtype) * g


def softmax_f32(s):
    return jax.nn.softmax(s.astype(jnp.float32), axis=-1)


def rope_tables(positions):
    half = B_ROPE // 2
    inv = ROPE_THETA ** (-jnp.arange(half, dtype=jnp.float32) / half)
    ang = positions.astype(jnp.float32)[..., None] * inv
    return jnp.cos(ang), jnp.sin(ang)


def apply_rope(x, cos, sin):
    half = x.shape[-1] // 2
    x1, x2 = x[..., :half], x[..., half:]
    c = cos.astype(x.dtype)
    s = sin.astype(x.dtype)
    return jnp.concatenate([x1 * c - x2 * s, x1 * s + x2 * c], axis=-1)


def chunked_relpos_attention(q, k, v, rel_bias):
    b, s, h, dh = q.shape
    nc = s // CHUNK
    qc = q.reshape(b, nc, CHUNK, h, dh)
    pad = ((0, 0), (LEFT_CHUNKS * CHUNK, 0), (0, 0), (0, 0))
    kp = jnp.pad(k, pad).reshape(b, nc + LEFT_CHUNKS, CHUNK, h, dh)
    vp = jnp.pad(v, pad).reshape(b, nc + LEFT_CHUNKS, CHUNK, h, dh)
    kb = jnp.concatenate([kp[:, j:j + nc] for j in range(BAND_CHUNKS)], axis=2)
    vb = jnp.concatenate([vp[:, j:j + nc] for j in range(BAND_CHUNKS)], axis=2)
    scores = jnp.einsum('bnqhd,bnkhd->bhnqk', qc, kb).astype(jnp.float32) * (dh ** -0.5)
    qpos = LEFT_CHUNKS * CHUNK + jnp.arange(CHUNK)
    kpos = jnp.arange(BAND_CHUNKS * CHUNK)
    rel = jnp.clip(qpos[:, None] - kpos[None, :], -REL_CLIP, REL_CLIP) + REL_CLIP
    bias = rel_bias[:, rel].astype(jnp.float32)
    kglob = jnp.arange(nc)[:, None] * CHUNK + kpos[None, :] - LEFT_CHUNKS * CHUNK
    valid = kglob >= 0
    scores = scores + bias[None, :, None]
    scores = jnp.where(valid[None, None, :, None, :], scores, -jnp.inf)
    p = softmax_f32(scores).astype(v.dtype)
    o = jnp.einsum('bhnqk,bnkhd->bnqhd', p, vb)
    return o.reshape(b, s, h * dh)


def mla_attention(q_lat, kv_lat, k_rope, cos, sin, q_norm, kv_norm, w_uq, w_ukv):
    b, s, _ = q_lat.shape
    q = (rms_norm(q_lat, q_norm) @ w_uq).reshape(b, s, B_HEADS, B_NOPE + B_ROPE)
    q_nope = q[..., :B_NOPE]
    q_pe = apply_rope(q[..., B_NOPE:], cos[:, :, None], sin[:, :, None])
    kv = (rms_norm(kv_lat, kv_norm) @ w_ukv).reshape(b, s, B_HEADS, B_NOPE + B_V)
    k_nope, v = kv[..., :B_NOPE], kv[..., B_NOPE:]
    k_pe = apply_rope(k_rope, cos, sin)
    nb = s // Q_BLOCK
    qn_blocks = q_nope.reshape(b, nb, Q_BLOCK, B_HEADS, B_NOPE).transpose(1, 0, 2, 3, 4)
    qr_blocks = q_pe.reshape(b, nb, Q_BLOCK, B_HEADS, B_ROPE).transpose(1, 0, 2, 3, 4)
    k_chunk = jnp.arange(s) // CHUNK
    scale = (B_NOPE + B_ROPE) ** -0.5

    def block(args):
        i, qn, qr = args
        sc = (jnp.einsum('bqhd,bkhd->bhqk', qn, k_nope)
              + jnp.einsum('bqhr,bkr->bhqk', qr, k_pe)).astype(jnp.float32) * scale
        q_chunk = (i * Q_BLOCK + jnp.arange(Q_BLOCK)) // CHUNK
        allowed = k_chunk[None, :] <= q_chunk[:, None]
        sc = jnp.where(allowed[None, None], sc, -jnp.inf)
        p = softmax_f32(sc).astype(v.dtype)
        return jnp.einsum('bhqk,bkhd->bqhd', p, v)

    o = lax.map(block, (jnp.arange(nb), qn_blocks, qr_blocks))
    return o.transpose(1, 0, 2, 3, 4).reshape(b, s, B_WIDTH)


def memory_cross_attention(h, mem_n, w_xq, w_xkv, w_xo):
    b, s, _ = h.shape
    m = mem_n.shape[1]
    q = (h @ w_xq).reshape(b, s, X_HEADS, X_HEAD_DIM)
    kv = (mem_n @ w_xkv).reshape(b, m, 2, X_HEADS, X_HEAD_DIM)
    k, v = kv[:, :, 0], kv[:, :, 1]
    sc = jnp.einsum('bqhd,bmhd->bhqm', q, k).astype(jnp.float32) * (X_HEAD_DIM ** -0.5)
    p = softmax_f32(sc).astype(v.dtype)
    o = jnp.einsum('bhqm,bmhd->bqhd', p, v).reshape(b, s, X_WIDTH)
    return o @ w_xo


def swiglu(h, w_gate, w_up, w_down):
    return (jax.nn.silu(h @ w_gate) * (h @ w_up)) @ w_down


def setup_inputs(seed: int = 0) -> dict:
    key = jax.random.key(seed)
    ks = jax.random.split(key, 24)
    f32 = jnp.float32

    def dense(k, shape):
        return jax.random.normal(k, shape, f32) * (shape[-2] ** -0.5)

    def gain(k, shape):
        return 1.0 + 0.02 * jax.random.normal(k, shape, f32)

    x = jax.random.normal(ks[0], (BATCH, SEQ, D_MODEL), f32)
    mem = jax.random.normal(ks[1], (BATCH, N_MEM, D_MODEL), f32)
    offset = jax.random.randint(ks[2], (BATCH, 1), 0, 4096, dtype=jnp.int32)
    positions = (offset + jnp.arange(SEQ, dtype=jnp.int32)[None, :]).astype(jnp.int32)
    return {
        "x": x,
        "mem": mem,
        "positions": positions,
        "norm_mix": gain(ks[3], (DEPTH, D_MODEL)),
        "w_in": dense(ks[4], (DEPTH, D_MODEL, IN_WIDTH)),
        "rel_bias": 0.1 * jax.random.normal(ks[5], (DEPTH, A_HEADS, 2 * REL_CLIP + 1), f32),
        "q_norm": gain(ks[6], (DEPTH, Q_LORA)),
        "kv_norm": gain(ks[7], (DEPTH, KV_LORA)),
        "w_uq": dense(ks[8], (DEPTH, Q_LORA, B_HEADS * (B_NOPE + B_ROPE))),
        "w_ukv": dense(ks[9], (DEPTH, KV_LORA, B_HEADS * (B_NOPE + B_V))),
        "w_out": dense(ks[10], (DEPTH, MIX_WIDTH, D_MODEL)),
        "norm_mem": gain(ks[11], (DEPTH, D_MODEL)),
        "mem_norm": gain(ks[12], (D_MODEL,)),
        "w_xq": dense(ks[13], (DEPTH, D_MODEL, X_WIDTH)),
        "w_xkv": dense(ks[14], (DEPTH, D_MODEL, 2 * X_WIDTH)),
        "w_xo": dense(ks[15], (DEPTH, X_WIDTH, D_MODEL)),
        "norm_ffn": gain(ks[16], (DEPTH, D_MODEL)),
        "w_gate": dense(ks[17], (DEPTH, D_MODEL, D_FF)),
        "w_up": dense(ks[18], (DEPTH, D_MODEL, D_FF)),
        "w_down": dense(ks[19], (DEPTH, D_FF, D_MODEL)),
        "norm_final": gain(ks[20], (D_MODEL,)),
    }


def reference(x, mem, positions, norm_mix, w_in, rel_bias, q_norm, kv_norm, w_uq, w_ukv,
              w_out, norm_mem, mem_norm, w_xq, w_xkv, w_xo, norm_ffn, w_gate, w_up,
              w_down, norm_final):
    b, s, _ = x.shape
    cos, sin = rope_tables(positions)
    mem_n = rms_norm(mem, mem_norm)
    for l in range(DEPTH):
        h = rms_norm(x, norm_mix[l])
        proj = h @ w_in[l]
        qa, ka, va, q_lat, kv_lat, k_rope = jnp.split(proj, IN_SPLITS, axis=-1)
        shp = (b, s, A_HEADS, A_HEAD_DIM)
        oa = chunked_relpos_attention(qa.reshape(shp), ka.reshape(shp), va.reshape(shp),
                                      rel_bias[l])
        ob = mla_attention(q_lat, kv_lat, k_rope, cos, sin, q_norm[l], kv_norm[l],
                           w_uq[l], w_ukv[l])
        x = x + jnp.concatenate([oa, ob], axis=-1) @ w_out[l]
        h = rms_norm(x, norm_mem[l])
        x = x + memory_cross_attention(h, mem_n, w_xq[l], w_xkv[l], w_xo[l])
        h = rms_norm(x, norm_ffn[l])
        x = x + swiglu(h, w_gate[l], w_up[l], w_down[l])
    return rms_norm(x, norm_final)
```

```python
import contextlib
import math
import numpy as np
import concourse.bass as bass
import concourse.mybir as mybir
from concourse.bass_utils import run_bass_kernel_spmd

F32 = mybir.dt.float32
BF16 = mybir.dt.bfloat16
I32 = mybir.dt.int32
AF = mybir.ActivationFunctionType
ALU = mybir.AluOpType

ENGS = ("pe", "act", "dve", "pool", "sp")

D_MODEL = 2048
CHUNK = 64
A_HEADS = 8
B_HEADS = 8
Q_LORA = 768
KV_LORA = 512
N_MEM = 256
X_HEADS = 4
D_FF = 5632
EPS = 1e-6
REL_CLIP = 128
ROPE_THETA = 10000.0
W_IN_R = 4608
NG_L = 58


class Op:
    __slots__ = ("eng", "fn", "deps", "is_dma", "sig", "sem", "val", "pre_wait", "idx", "inc")

    def __init__(self, eng, fn, is_dma):
        self.eng = eng
        self.fn = fn
        self.is_dma = bool(is_dma)
        self.inc = 1 if is_dma == "cc" else 16
        self.deps = []
        self.sig = False
        self.sem = None
        self.val = 0
        self.pre_wait = None


class Prog:
    def __init__(self, nc, sem_wrap=30000):
        self.nc = nc
        self.ops = {e: [] for e in ENGS}
        self.last_w = {}
        self.readers = {}
        self.n_dma_sems = {"sp": 28, "pool": 20, "act": 8, "dve": 4, "pe": 4}
        self.sem_wrap = sem_wrap
        self.stack = contextlib.ExitStack()

    def sbuf(self, name, shape, dtype):
        return self.stack.enter_context(self.nc.sbuf_tensor("sb_" + name, list(shape), dtype))

    def psum(self, name, shape, dtype):
        return self.stack.enter_context(self.nc.psum_tensor(name, list(shape), dtype))

    def op(self, eng, fn, reads=(), writes=(), dma=False):
        o = Op(eng, fn, dma)
        o.idx = len(self.ops[eng])
        lw = self.last_w
        rd = self.readers
        best = {}
        dmas = {}

        def add(d):
            if d.is_dma:
                dmas[id(d)] = d
            else:
                b = best.get(d.eng)
                if b is None or d.idx > b.idx:
                    best[d.eng] = d

        for r in reads:
            for w in lw.get(r, ()):
                add(w)
        for r in writes:
            ws = lw.get(r)
            rs = rd.get(r)
            if rs:
                for x in rs:
                    add(x)
                if ws:
                    for w in ws:
                        add(w)
                lw[r] = [o]
                rd[r] = []
            elif ws and dma and all(w.is_dma for w in ws):
                ws.append(o)
            else:
                if ws:
                    for w in ws:
                        add(w)
                lw[r] = [o]
                rd[r] = []
        for r in reads:
            rd.setdefault(r, []).append(o)
        o.deps = [d for d in list(best.values()) + list(dmas.values()) if d is not o]
        self.ops[eng].append(o)
        return o

    @staticmethod
    def _skip(d, o):
        return d.eng == o.eng and d.eng == "pe" and not o.is_dma and not d.is_dma

    def emit(self, final_waits=()):
        nc = self.nc
        for e in ENGS:
            for o in self.ops[e]:
                for d in o.deps:
                    if d.is_dma or self._skip(d, o):
                        continue
                    d.sig = True
        sems = {}

        def get_sem(name):
            if name not in sems:
                sems[name] = self.stack.enter_context(nc.semaphore(name))
            return sems[name]

        for e in ENGS:
            cnt = 0
            dma_i = 0
            cc_i = 0
            dma_vals = {}
            dma_last = {}
            for o in self.ops[e]:
                if o.is_dma:
                    if o.inc == 1:
                        name = f"cc_{e}"
                    else:
                        name = f"d_{e}_{dma_i % self.n_dma_sems[e]}"
                        dma_i += 1
                    o.sem = get_sem(name)
                    o.val = dma_vals.get(name, 0) + o.inc
                    dma_vals[name] = o.val
                    o.pre_wait = dma_last.get(name)
                    dma_last[name] = o
                elif o.sig:
                    o.sem = get_sem(f"c_{e}_{cnt // self.sem_wrap}")
                    o.val = cnt % self.sem_wrap + 1
                    cnt += 1
        self.n_waits = 0
        self.n_ins = 0
        with nc.Block() as block:
            def run(e, h):
                waited = {}

                def wait(sem, val):
                    k = id(sem)
                    if waited.get(k, 0) >= val:
                        return
                    waited[k] = val
                    h.wait_ge(sem, val)
                    self.n_waits += 1

                for o in self.ops[e]:
                    if o.pre_wait is not None:
                        wait(o.pre_wait.sem, o.pre_wait.val)
                    for d in o.deps:
                        if d.is_dma:
                            wait(d.sem, d.val)
                        elif d.sig and not self._skip(d, o):
                            wait(d.sem, d.val)
                    ins = o.fn(h)
                    self.n_ins += 1
                    if o.is_dma:
                        ins.then_inc(o.sem, o.inc)
                    elif o.sig:
                        ins.then_inc(o.sem, 1)
                if e == "sp":
                    for o in final_waits:
                        wait(o.sem, o.val)

            if self.ops["pe"]:
                block.tensor(lambda h: run("pe", h))
            if self.ops["act"]:
                block.scalar(lambda h: run("act", h))
            if self.ops["dve"]:
                block.vector(lambda h: run("dve", h))
            if self.ops["pool"]:
                block.gpsimd(lambda h: run("pool", h))
            block.sync(lambda h: run("sp", h))

    def close(self):
        self.stack.close()


class Cfg:
    def __init__(self, T=2048, CTX=2048, depth=4, layer0=0, first=True, last=True, n_cores=8):
        self.T = T
        self.CTX = CTX
        self.CA = min(512, CTX)
        self.depth = depth
        self.n_cores = n_cores


def build_program(cfg):
    T, CTX, CA, DEPTH = cfg.T, cfg.CTX, cfg.CA, cfg.depth
    NSUB = 128
    NST = T // 512
    nc = bass.Bass("TRN2", target_bir_lowering=False)
    P = Prog(nc)

    def din(name, shape, dt=F32):
        return nc.dram_tensor(name, list(shape), dt, kind="ExternalInput").ap()

    def dscr(name, shape, dt=BF16):
        return nc.dram_tensor(name, list(shape), dt, kind="Internal").ap()

    xT_in = din("xT", [D_MODEL, T])
    memT_in = din("memT", [D_MODEL, N_MEM])
    pos_in = din("pos", [1, T], I32)
    consts_in = din("consts", [128, 8])
    gains_in = din("gains", [128, NG_L * DEPTH + 32])
    flag_in = din("flag", [128, 1])
    mask01_in = din("mask01", [128, 640])
    mlamask_in = din("mlamask", [128, 4 * 512])
    biasT_in = din("biasT", [DEPTH * 8 * 128, 640])
    w_in_d = din("w_in", [DEPTH * D_MODEL, W_IN_R])
    w_uq_d = din("w_uq", [DEPTH * Q_LORA, 2048])
    w_ukv_d = din("w_ukv", [DEPTH * KV_LORA, 2048])
    w_out_d = din("w_out", [DEPTH * D_MODEL, D_MODEL])
    w_xq_d = din("w_xq", [DEPTH * D_MODEL, 512])
    w_xkv_d = din("w_xkv", [DEPTH * D_MODEL, 1024])
    w_xo_d = din("w_xo", [DEPTH * 512, D_MODEL])
    w_gate_d = din("w_gate", [DEPTH * D_MODEL, D_FF])
    w_up_d = din("w_up", [DEPTH * D_MODEL, D_FF])
    w_down_d = din("w_down", [DEPTH * D_FF, D_MODEL])
    outT = nc.dram_tensor("outT", [D_MODEL, T], F32, kind="ExternalOutput").ap()

    XR = dscr("XR", [D_MODEL, T], F32)
    ROPE = dscr("ROPE", [4 * 128, T], F32)
    QA = dscr("QA", [8 * 128, T])
    KA = dscr("KA", [8 * 128, T])
    VA = dscr("VA", [T, 1024])
    QN = dscr("QN", [8 * 128, T])
    QPE = dscr("QPE", [4 * 128, T])
    OT = dscr("OT", [D_MODEL, T])
    TB = T // 1024
    chunks = []
    for base in (0, T):
        for s_ in range(0, T, 1024):
            chunks.append((base + s_, 1024))
    chunks.append((2 * T, T // 8 + 512))
    chunks.append((2 * T + T // 8 + 512, 512))
    S_t = [dscr(f"SND{i}", [n_, 1024]) for i, (s_, n_) in enumerate(chunks)]
    G_t = [dscr(f"GTH{i}", [2 * n_, 1024]) for i, (s_, n_) in enumerate(chunks)] if CTX else None

    def xrows(kind, r0, n):
        for i, (s_, n_) in enumerate(chunks):
            if s_ <= r0 and r0 + n <= s_ + n_:
                buf = S_t[i] if kind == "own" else G_t[i]
                return buf[r0 - s_:r0 - s_ + n, :]
        raise AssertionError((r0, n))

    def xKN(kind, h):
        return xrows(kind, h * 128 * TB, 128 * TB).rearrange("(a b) c -> a (b c)", b=TB)

    def xVB(kind, tok0, n):
        return xrows(kind, T + tok0, n)

    def xKPE(kind):
        return xrows(kind, 2 * T, T // 8).rearrange("(a b) c -> a (b c)", b=TB)

    def xKAT(kind):
        return xrows(kind, 2 * T + T // 8, 512).rearrange("r (s c) -> (r s) c", s=2)

    def xVAT(kind):
        return xrows(kind, 2 * T + T // 8 + 512, 512)

    WS = 6144
    wslot = [P.sbuf(f"wslot{i}", [128, WS], BF16) for i in range(4)]
    ain = P.sbuf("ain", [128, 16 * 1024], BF16)
    xs = [P.sbuf(f"xs{i}", [128, 16 * NSUB], F32) for i in range(2)]
    arena = P.sbuf("arena", [128, 11264], F32)
    sqb = P.sbuf("sqb", [128, 32 * NSUB], BF16)
    sql = [P.sbuf(f"sql{i}", [128, 512], BF16) for i in range(3)]
    rstd = [P.sbuf(f"rstd{i}", [128, 512], F32) for i in range(2)]
    NTMP, NPT, NXR, NSTG = 3, 4, 6, 4
    tmpf = [P.sbuf(f"tmpf{i}", [128, 512], F32) for i in range(NTMP)]
    stg = [P.sbuf(f"stg{i}", [128, 512], BF16) for i in range(NSTG)]
    pt = [P.sbuf(f"pt{i}", [128, 640], BF16) for i in range(NPT)]
    xres = [P.sbuf(f"xres{i}", [128, 512], F32) for i in range(NXR)]
    gains = P.sbuf("gains", [128, NG_L * DEPTH + 32], F32)
    consts = P.sbuf("consts", [128, 8], F32)
    flag = P.sbuf("flag", [128, 1], F32)
    ones_bf = P.sbuf("ones_bf", [128, 128], BF16)
    flagones = P.sbuf("flagones", [128, 128], BF16)
    mask01 = P.sbuf("mask01", [128, 640], F32)
    mlamask = P.sbuf("mlamask", [128, 4 * 512], BF16)
    memn = P.sbuf("memn", [128, 16 * 256], BF16)
    kx = P.sbuf("kx", [128, 4 * 256], BF16)
    vx = P.sbuf("vx", [128, 2 * 512], BF16)
    ebias = [P.sbuf(f"ebias{i}", [128, 640], BF16) for i in range(2)]
    ps = P.psum("ps", [128, 8, 512], F32)

    state = {"w": 0, "stg": 0, "tmpf": 0, "pt": 0, "sql": 0, "xres": 0, "rstd": 0, "ev": 0, "apair": 0,
             "xsi": 0, "hb": 0, "grp": 0, "sqi": 0}

    def rot(key, n):
        v = state[key]
        state[key] = (v + 1) % n
        return v

    def gbank(allowed):
        k = "bank_" + str(allowed)
        i = state.get(k, 0)
        state[k] = i + 1
        return allowed[i % len(allowed)]

    ALLB = [0, 1, 2, 3, 4, 5, 6, 7]
    LOWB = [0, 1, 2, 3]

    def ar(lo, hi):
        return [f"ar{g}" for g in range(lo // 512, (hi + 511) // 512)]

    def ainq(q):
        return [f"ain{q}a", f"ain{q}b"]

    AIN_H = [[f"ain{q}a" for q in range(4)], [f"ain{q}b" for q in range(4)]]
    AIN_ALL = AIN_H[0] + AIN_H[1]

    def xnames(t0, ntok, pref="X"):
        tiles = sorted(set([t0 // 512, (t0 + ntok - 1) // 512]))
        return [f"{pref}{c}_{t}" for c in range(16) for t in tiles]

    def dma(eng, out, in_, reads, writes):
        return P.op(eng, lambda e: e.dma_start(out=out, in_=in_), reads=reads, writes=writes, dma=True)

    def evac_copy(out, in_, reads, writes, scale=None, eng=None):
        e_ = eng or ("act" if rot("ev", 2) == 0 else "dve")
        if e_ == "act":
            if scale is None:
                return P.op("act", lambda e: e.copy(out=out, in_=in_), reads, writes)
            return P.op("act", lambda e: e.mul(out=out, in_=in_, mul=scale), reads, writes)
        if scale is None:
            return P.op("dve", lambda e: e.tensor_copy(out=out, in_=in_), reads, writes)
        return P.op("dve", lambda e: e.tensor_scalar(out=out, in0=in_, scalar1=scale, scalar2=None,
                                                    op0=ALU.mult), reads, writes)

    bgq = []
    BG_EVERY = 4

    def bg_run():
        it = bgq.pop(0)
        (it[1] if isinstance(it, tuple) else it)()

    BG_MM = 96

    def bg_step(n_mm=32):
        state["grp"] += n_mm
        if bgq and state["grp"] >= BG_MM:
            state["grp"] = 0
            bg_run()

    def bg_flush():
        while bgq:
            bg_run()

    def bg_flush_tag(tag):
        while any(isinstance(it, tuple) and it[0] == tag for it in bgq):
            bg_run()

    def wgroups(KC, c0, ncols, cap=512):
        gw_max = min(cap, (WS // KC) // 128 * 128)
        out = []
        c = c0
        while c < c0 + ncols:
            gw = min(gw_max, c0 + ncols - c)
            out.append((c, gw))
            c += gw
        return out

    pending_cc = []

    def load_w(W_ap, row0, KC, c, gw):
        state["nload"] = state.get("nload", 0) + 1
        if pending_cc and state["nload"] % 2 == 0:
            pending_cc.pop(0)()
        s = rot("w", 4)
        dst = wslot[s][:, 0:KC * gw].rearrange("p (k f) -> p k f", k=KC)
        src = W_ap[row0:row0 + KC * 128, c:c + gw].rearrange("(k p) f -> p k f", p=128)
        dma("pool", dst, src, reads=[], writes=[f"w{s}"])
        return s, dst

    def linear_fm(a_view, a_res, KC, tw, nt, W_ap, row0, c0, ncols, consume, banks=ALLB, pre=None, PF=2):
        groups = [(c, gw, f, j) for (c, gw) in wgroups(KC, c0, ncols) for f in range(0, gw, 128) for j in range(nt)]
        if pre:
            for g in groups[:PF]:
                pre(g[0] + g[2], g[3])
        cur = None
        for gi, (c, gw, f, j) in enumerate(groups):
            if cur is None or cur[0] != c:
                cur = (c,) + load_w(W_ap, row0, KC, c, gw)
            s, wv = cur[1], cur[2]
            if pre and gi + PF < len(groups):
                g2 = groups[gi + PF]
                pre(g2[0] + g2[2], g2[3])
            b = gbank(banks)
            pv = ps[:, b, 0:tw]
            for k in range(KC):
                P.op("pe", (lambda e, pv=pv, wv=wv, k=k, f=f, j=j:
                            e.matmul(pv, wv[:, k, f:f + 128], a_view[:, k, j * tw:(j + 1) * tw],
                                     start=(k == 0), stop=(k == KC - 1))),
                     reads=[f"w{s}"] + a_res, writes=[f"ps{b}"])
            consume(c + f, j, pv, f"ps{b}")
            bg_step(KC)

    def linear_tm(a_view, a_res, KC, ntt, W_ap, row0, c0, ncols, consume, banks=ALLB):
        for (c, gw) in wgroups(KC, c0, ncols):
            s, wv = load_w(W_ap, row0, KC, c, gw)
            for tt in range(ntt):
                b = gbank(banks)
                pv = ps[:, b, 0:gw]
                for k in range(KC):
                    P.op("pe", (lambda e, pv=pv, wv=wv, k=k, tt=tt:
                                e.matmul(pv, a_view[:, k, tt * 128:(tt + 1) * 128], wv[:, k, :],
                                         start=(k == 0), stop=(k == KC - 1))),
                         reads=[f"w{s}"] + a_res, writes=[f"ps{b}"])
                consume(tt, c, gw, pv, f"ps{b}")
                bg_step(KC)

    def rstd_from_psum(pv, psres, n_feat, w):
        r = rot("rstd", 2)
        rv = rstd[r][:, 0:w]
        P.op("act", lambda e: e.activation(out=rv, in_=pv, func=AF.Sqrt, bias=consts[:, 5:6], scale=1.0 / n_feat),
             reads=[psres, "consts"], writes=[f"rstd{r}"])
        P.op("dve", lambda e: e.reciprocal(out=rv, in_=rv), reads=[f"rstd{r}"], writes=[f"rstd{r}"])
        return rv, f"rstd{r}"

    final_ops = []

    def norm_pieces(x_ap, t0, ntok, gcol, dst_fn, dst_res_fn, out_f32_dma=None, pref="X", hbm_dst=None):
        nsub = ntok // NSUB
        st_ = {}

        def stage_a(sub):
            xi = rot("xsi", 2)
            xv = xs[xi][:].rearrange("p (c t) -> p c t", c=16)
            a0 = t0 + sub * NSUB
            src_ = x_ap[:, a0:a0 + NSUB].rearrange("(c p) t -> p c t", p=128)
            dma("sp", xv, src_, reads=xnames(a0, NSUB, pref), writes=[f"xs{xi}"])
            sqi = rot("sqi", 2)
            for c in range(16):
                qv = sqb[:, (sqi * 16 + c) * NSUB:(sqi * 16 + c + 1) * NSUB]
                P.op("act", lambda e, qv=qv, c=c: e.activation(out=qv, in_=xv[:, c, :], func=AF.Square),
                     reads=[f"xs{xi}"], writes=[f"sq{sqi}_{c}"])
            st_[sub] = (xi, xv, a0, sqi)

        def stage_b(sub):
            xi, xv, a0, sqi = st_.pop(sub)
            b_ = gbank(ALLB)
            pv = ps[:, b_, 0:NSUB]
            for c in range(16):
                qv = sqb[:, (sqi * 16 + c) * NSUB:(sqi * 16 + c + 1) * NSUB]
                P.op("pe", lambda e, qv=qv, c=c: e.matmul(pv, ones_bf[:], qv, start=(c == 0), stop=(c == 15)),
                     reads=[f"sq{sqi}_{c}", "ones"], writes=[f"ps{b_}"])
            rv, rres = rstd_from_psum(pv, f"ps{b_}", D_MODEL, NSUB)
            for c in range(16):
                if hbm_dst is not None:
                    dv = xs[xi][:].bitcast(BF16)[:, c * NSUB:(c + 1) * NSUB]
                    P.op("dve", (lambda e, dv=dv, c=c:
                                 e.scalar_tensor_tensor(out=dv, in0=xv[:, c, :], scalar=gains[:, gcol + c:gcol + c + 1],
                                                        in1=rv, op0=ALU.mult, op1=ALU.mult)),
                         reads=[f"xs{xi}", rres, "gains"], writes=[f"xs{xi}"])
                elif out_f32_dma is None:
                    dv = dst_fn(c, sub * NSUB, NSUB)
                    P.op("dve", (lambda e, dv=dv, c=c:
                                 e.scalar_tensor_tensor(out=dv, in0=xv[:, c, :], scalar=gains[:, gcol + c:gcol + c + 1],
                                                        in1=rv, op0=ALU.mult, op1=ALU.mult)),
                         reads=[f"xs{xi}", rres, "gains"], writes=dst_res_fn(c, sub * NSUB))
                else:
                    P.op("dve", (lambda e, c=c:
                                 e.scalar_tensor_tensor(out=xv[:, c, :], in0=xv[:, c, :],
                                                        scalar=gains[:, gcol + c:gcol + c + 1],
                                                        in1=rv, op0=ALU.mult, op1=ALU.mult)),
                         reads=[f"xs{xi}", rres, "gains"], writes=[f"xs{xi}"])
            if out_f32_dma is not None:
                dst = out_f32_dma[:, a0:a0 + NSUB].rearrange("(c p) t -> p c t", p=128)
                final_ops.append(dma("sp", dst, xv, reads=[f"xs{xi}"], writes=["OUT"]))
            if hbm_dst is not None:
                H_ap, hname = hbm_dst
                dma("sp", H_ap[:, a0:a0 + NSUB].rearrange("(c p) t -> p c t", p=128),
                    xs[xi][:].bitcast(BF16)[:, 0:16 * NSUB].rearrange("p (c t) -> p c t", c=16),
                    reads=[f"xs{xi}"], writes=[hname])

        pieces = []
        for p_ in range(nsub + 1):
            def piece(p_=p_):
                if p_ < nsub:
                    stage_a(p_)
                if p_ >= 1:
                    stage_b(p_ - 1)
            pieces.append(piece)
        return pieces

    ain_v16 = ain[:].rearrange("p (c t) -> p c t", c=16)

    def norm512(x_ap, t0, gcol, hb):
        return norm_pieces(x_ap, t0, 512, gcol,
                           lambda c, s0, w: ain_v16[:, c, hb * 512 + s0:hb * 512 + s0 + w],
                           lambda c, s0: [f"ain{c // 4}{'ab'[hb]}"])

    def norm1024(x_ap, t0, gcol):
        return norm_pieces(x_ap, t0, 1024, gcol,
                           lambda c, s0, w: ain_v16[:, c, s0:s0 + w],
                           lambda c, s0: [f"ain{c // 4}{'ab'[s0 // 512]}"])

    dma("sp", gains[:], gains_in, [], ["gains"])
    dma("sp", consts[:], consts_in, [], ["consts"])
    dma("sp", flag[:], flag_in, [], ["flag"])
    dma("sp", mask01[:], mask01_in, [], ["mask01"])
    dma("pool", mlamask[:], mlamask_in, [], ["mlamask"])
    P.op("dve", lambda e: e.memset(ones_bf[:], 1.0), [], ["ones"])
    P.op("dve", lambda e: e.tensor_scalar(out=flagones[:], in0=ones_bf[:], scalar1=flag[:, 0:1], scalar2=None,
                                          op0=ALU.mult), ["ones", "flag"], ["flagones"])
    QSC_B = (128 + 64) ** -0.5
    QSC_A = 128.0 ** -0.5
    ROPET0 = 8704
    ropet = arena[:, ROPET0:ROPET0 + 2048]
    ROPET_RES = ar(ROPET0, ROPET0 + 2048)
    PR = ar(0, 2560)
    for t0 in range(0, T, 512):
        posi = arena[:, 0:512].bitcast(I32)
        posf = arena[:, 512:1024]
        ang = arena[:, 1024:1536]
        fr = arena[:, 1536:2048]
        dma("sp", posi, pos_in[0:1, t0:t0 + 512].partition_broadcast(128), [], PR)
        P.op("dve", lambda e: e.tensor_copy(out=posf, in_=posi), PR, PR)
        P.op("dve", lambda e: e.tensor_scalar(out=ang, in0=posf, scalar1=consts[:, 0:1], scalar2=None, op0=ALU.mult),
             PR + ["consts"], PR)
        for which in range(2):
            off = 0.25 if which == 0 else 0.0
            zi = arena[:, 0:512].bitcast(I32)
            zf = arena[:, 512:1024]
            mk = arena[:, 2048:2560]
            P.op("dve", lambda e, off=off: e.tensor_scalar(out=fr, in0=ang, scalar1=1.0 / (2 * math.pi), scalar2=off,
                                                           op0=ALU.mult, op1=ALU.add), PR, PR)
            P.op("dve", lambda e: e.tensor_copy(out=zi, in_=fr), PR, PR)
            P.op("dve", lambda e: e.tensor_copy(out=zf, in_=zi), PR, PR)
            P.op("dve", lambda e: e.tensor_tensor(out=fr, in0=fr, in1=zf, op=ALU.subtract), PR, PR)
            P.op("dve", lambda e: e.tensor_scalar(out=mk, in0=fr, scalar1=0.5, scalar2=None, op0=ALU.is_gt), PR, PR)
            P.op("dve", lambda e: e.tensor_tensor(out=fr, in0=fr, in1=mk, op=ALU.subtract), PR, PR)
            P.op("dve", lambda e: e.tensor_scalar(out=mk, in0=fr, scalar1=-0.5, scalar2=None, op0=ALU.is_lt), PR, PR)
            P.op("dve", lambda e: e.tensor_tensor(out=fr, in0=fr, in1=mk, op=ALU.add), PR, PR)
            tv = ropet[:, which * 512:(which + 1) * 512]
            P.op("act", lambda e, tv=tv: e.activation(out=tv, in_=fr, func=AF.Sin, scale=2 * math.pi * 0.999999),
                 PR + ["consts"], ROPET_RES)
            if which == 1:
                P.op("dve", lambda e, tv=tv: e.tensor_scalar(out=tv, in0=tv, scalar1=consts[:, 1:2], scalar2=None,
                                                             op0=ALU.mult), ROPET_RES + ["consts"], ROPET_RES)
            tq = ropet[:, (2 + which) * 512:(3 + which) * 512]
            P.op("dve", lambda e, tv=tv, tq=tq: e.tensor_scalar(out=tq, in0=tv, scalar1=QSC_B, scalar2=None,
                                                                op0=ALU.mult), ROPET_RES, ROPET_RES)
        for r in range(4):
            dma("sp", ROPE[r * 128:(r + 1) * 128, t0:t0 + 512], ropet[:, r * 512:(r + 1) * 512], ROPET_RES, ["ROPE"])
    GM = NG_L * DEPTH
    memn_v = memn[:].rearrange("p (c t) -> p c t", c=16)
    for pc in norm_pieces(memT_in, 0, N_MEM, GM, lambda c, s0, w: memn_v[:, c, s0:s0 + w], lambda c, s0: ["memn"], pref="M"):
        pc()

    def load_ropet(t0):
        dma("sp", ropet.rearrange("p (r t) -> p r t", r=4),
            ROPE[:, t0:t0 + 512].rearrange("(r p) t -> p r t", p=128), ["ROPE"], ROPET_RES)
    CS2 = ropet[:, 0:512]
    SS2 = ropet[:, 512:1024]
    CSq = ropet[:, 1024:1536]
    SSq = ropet[:, 1536:2048]

    HN = dscr("HN", [D_MODEL, T])
    HE = dscr("HE", [D_MODEL, T])
    HF = dscr("HF", [D_MODEL, T])

    def norm_to(H_ap, hname, x_ap, t0, gcol):
        return norm_pieces(x_ap, t0, 1024, gcol, None, None, hbm_dst=(H_ap, hname))

    def load_ain(H_ap, hname, t0):
        dma("sp", ain_v16, H_ap[:, t0:t0 + 1024].rearrange("(c p) t -> p c t", p=128), [hname], AIN_ALL)
    NSTT = T // 1024
    ain_full = ain_v16
    bgq.extend(norm_to(HN, "HN", xT_in, 0, 0))
    bg_flush()

    for l in range(DEPTH):
        G0 = NG_L * l
        x_src = xT_in if l == 0 else XR

        kx_v = kx[:].rearrange("p (h m) -> p h m", h=4)
        vx_v = vx[:].rearrange("p (t c) -> p t c", t=2)

        def cons_kx(col, j, pv, pres):
            evac_copy(kx_v[:, col // 128, :], pv, [pres], ["kx"])
        linear_fm(memn_v, ["memn"], 16, 256, 1, w_xkv_d, l * D_MODEL, 0, 512, cons_kx)

        def cons_vx(tt, c, gw, pv, pres):
            evac_copy(vx_v[:, tt, c - 512:c - 512 + gw], pv, [pres], ["vx"])
        linear_tm(memn_v, ["memn"], 16, 2, w_xkv_d, l * D_MODEL, 512, 512, cons_vx)

        def latent_norm(lat, latres, nch, gcol, dstv, dres, j):
            b = gbank(ALLB)
            pv = ps[:, b, :]
            sl = slice(j * 512, (j + 1) * 512)
            for c in range(nch):
                q = rot("sql", 3)
                P.op("act", lambda e, q=q, c=c: e.activation(out=sql[q][:], in_=lat[:, c, sl], func=AF.Square),
                     reads=latres, writes=[f"sql{q}"])
                P.op("pe", lambda e, q=q, c=c: e.matmul(pv, ones_bf[:], sql[q][:], start=(c == 0), stop=(c == nch - 1)),
                     reads=[f"sql{q}", "ones"], writes=[f"ps{b}"])
            rv, rres = rstd_from_psum(pv, f"ps{b}", nch * 128, 512)
            for c in range(nch):
                P.op("dve", (lambda e, c=c: e.scalar_tensor_tensor(out=dstv[:, c, sl], in0=lat[:, c, sl],
                                                                   scalar=gains[:, gcol + c:gcol + c + 1], in1=rv,
                                                                   op0=ALU.mult, op1=ALU.mult)),
                     latres + [rres, "gains"], dres)

        kvlat = arena[:, 0:4096].rearrange("p (c t) -> p c t", c=4)
        kvn_b = arena[:, 4096:6144].bitcast(BF16).rearrange("p (c t) -> p c t", c=4)
        krA = arena[:, 6144:7168]
        krB = arena[:, 7168:8192]
        rope1 = arena[:, 8192:10240]
        R_KVLAT, R_KVN, R_KRA, R_KRB, R_ROPE1 = ar(0, 4096), ar(4096, 6144), ar(6144, 7168), ar(7168, 8192), ar(8192, 10240)
        CS2, SS2 = rope1[:, 0:1024], rope1[:, 1024:2048]
        for st in range(NSTT):
            t0 = st * 1024
            last = (st == NSTT - 1)
            load_ain(HN, "HN", t0)
            if not last:
                bgq.extend(norm_to(HN, "HN", x_src, t0 + 1024, G0 + 0))
            state["grp"] = 0
            dma("sp", rope1.rearrange("p (r t) -> p r t", r=2),
                ROPE[0:256, t0:t0 + 1024].rearrange("(r p) t -> p r t", p=128), ["ROPE"], R_ROPE1)

            def cons_in1(col, j, pv, pres, t0=t0, last=last):
                tt0 = t0 + j * 512
                if col < 2048:
                    s = rot("stg", NSTG)
                    evac_copy(stg[s][:], pv, [pres], [f"stg{s}"])
                    dma("sp", KA[col - 1024:col - 1024 + 128, tt0:tt0 + 512], stg[s][:], [f"stg{s}"], ["KA"])
                    if last and j == 1 and CTX:
                        dma("sp", xKAT("own")[col - 1024:col - 1024 + 128, :], stg[s][:], [f"stg{s}"], ["KAT"])
                elif col < 4352:
                    ci = (col - 3840) // 128
                    evac_copy(kvlat[:, ci, j * 512:(j + 1) * 512], pv, [pres], ar(ci * 1024 + j * 512, ci * 1024 + j * 512 + 512))
                elif col < 4480:
                    evac_copy(krA[:, j * 512:(j + 1) * 512], pv, [pres], ar(6144 + j * 512, 6144 + j * 512 + 512))
                else:
                    evac_copy(krB[:, j * 512:(j + 1) * 512], pv, [pres], ar(7168 + j * 512, 7168 + j * 512 + 512))

            linear_fm(ain_full, AIN_ALL, 16, 512, 2, w_in_d, l * D_MODEL, 3840, W_IN_R - 3840, cons_in1)

            def krope_piece(j, t0=t0):
                f0 = rot("tmpf", NTMP)
                f1 = rot("tmpf", NTMP)
                sl = slice(j * 512, (j + 1) * 512)
                P.op("dve", lambda e, f0=f0, sl=sl: e.tensor_tensor(out=tmpf[f0][:], in0=krA[:, sl], in1=CS2[:, sl], op=ALU.mult),
                     R_KRA + R_ROPE1, [f"tmpf{f0}"])
                P.op("dve", lambda e, f1=f1, sl=sl: e.tensor_tensor(out=tmpf[f1][:], in0=krB[:, sl], in1=SS2[:, sl], op=ALU.mult),
                     R_KRB + R_ROPE1, [f"tmpf{f1}"])
                s = rot("stg", NSTG)
                P.op("dve", lambda e, s=s, f0=f0, f1=f1: e.tensor_tensor(out=stg[s][:], in0=tmpf[f0][:], in1=tmpf[f1][:],
                                                                         op=ALU.add),
                     [f"tmpf{f0}", f"tmpf{f1}"], [f"stg{s}"])
                dma("sp", xKPE("own")[:, t0 + j * 512:t0 + (j + 1) * 512], stg[s][:], [f"stg{s}"], ["KPE"])
            for j in range(2):
                bgq.append(("lat", lambda j=j: krope_piece(j)))
                bgq.append(("lat", lambda j=j: latent_norm(kvlat, R_KVLAT, 4, G0 + 54, kvn_b, R_KVN, j)))
            state["grp"] = 0
            linear_fm(ain_full, AIN_ALL, 16, 512, 2, w_in_d, l * D_MODEL, 1024, 1024, cons_in1)

            def cons_va(tt, c, gw, pv, pres, t0=t0, last=last):
                s = rot("stg", NSTG)
                evac_copy(stg[s][:, 0:gw], pv, [pres], [f"stg{s}"])
                r0 = t0 + tt * 128
                dma("sp", VA[r0:r0 + 128, c - 2048:c - 2048 + gw], stg[s][:, 0:gw], [f"stg{s}"], ["VA"])
                if last and tt >= 4 and CTX:
                    dma("sp", xVAT("own")[(tt - 4) * 128:(tt - 3) * 128, c - 2048:c - 2048 + gw], stg[s][:, 0:gw],
                        [f"stg{s}"], ["VAT"])
            linear_tm(ain_full, AIN_ALL, 16, 8, w_in_d, l * D_MODEL, 2048, 1024, cons_va)
            bg_flush_tag("lat")
            if last:
                bg_flush()
                load_ain(HN, "HN", 0)

            def cons_kn(col, j, pv, pres, t0=t0):
                s = rot("stg", NSTG)
                evac_copy(stg[s][:], pv, [pres], [f"stg{s}"])
                dma("sp", xKN("own", col // 128)[:, t0 + j * 512:t0 + (j + 1) * 512], stg[s][:], [f"stg{s}"], ["KN"])
            linear_fm(kvn_b, R_KVN, 4, 512, 2, w_ukv_d, l * KV_LORA, 0, 1024, cons_kn)

            def cons_vb(tt, c, gw, pv, pres, t0=t0):
                s = rot("stg", NSTG)
                evac_copy(stg[s][:, 0:gw], pv, [pres], [f"stg{s}"])
                r0 = t0 + tt * 128
                dma("sp", xVB("own", r0, 128)[:, c - 1024:c - 1024 + gw], stg[s][:, 0:gw], [f"stg{s}"], ["VB"])
            linear_tm(kvn_b, R_KVN, 4, 8, w_ukv_d, l * KV_LORA, 1024, 1024, cons_vb)
            bg_flush()
        if CTX:
            groups_ = [[2 * i, 2 * i + 1] for i in range(cfg.n_cores // 2)]
            def issue_cc(ci):
                P.op("pool", lambda e, ci=ci: e.collective_compute("AllGather", ALU.bypass, replica_groups=groups_,
                                                                   ins=[S_t[ci]], outs=[G_t[ci]]),
                     ["KN", "VB", "KPE", "KAT", "VAT"], ["GATH"], dma="cc")
            issue_cc(0)
            for ci in range(1, len(chunks)):
                pending_cc.append(lambda ci=ci: issue_cc(ci))

        qlat = arena[:, 0:6144].rearrange("p (c t) -> p c t", c=6)
        qn_b = arena[:, 6144:9216].bitcast(BF16).rearrange("p (c t) -> p c t", c=6)
        rope2 = arena[:, 9216:11264]
        R_QLAT, R_QN, R_ROPE2 = ar(0, 6144), ar(6144, 9216), ar(9216, 11264)
        CSq, SSq = rope2[:, 0:1024], rope2[:, 1024:2048]
        for st in range(NSTT):
            t0 = st * 1024
            last = (st == NSTT - 1)
            dma("sp", rope2.rearrange("p (r t) -> p r t", r=2),
                ROPE[256:512, t0:t0 + 1024].rearrange("(r p) t -> p r t", p=128), ["ROPE"], R_ROPE2)

            def cons_in2(col, j, pv, pres, t0=t0):
                tt0 = t0 + j * 512
                if col < 1024:
                    s = rot("stg", NSTG)
                    evac_copy(stg[s][:], pv, [pres], [f"stg{s}"], scale=QSC_A)
                    dma("sp", QA[col:col + 128, tt0:tt0 + 512], stg[s][:], [f"stg{s}"], ["QA"])
                else:
                    ci = (col - 3072) // 128
                    evac_copy(qlat[:, ci, j * 512:(j + 1) * 512], pv, [pres], ar(ci * 1024 + j * 512, ci * 1024 + j * 512 + 512))
            linear_fm(ain_full, AIN_ALL, 16, 512, 2, w_in_d, l * D_MODEL, 3072, 768, cons_in2)
            for j in range(2):
                bgq.append(("lat", lambda j=j: latent_norm(qlat, R_QLAT, 6, G0 + 48, qn_b, R_QN, j)))
            state["grp"] = 0
            linear_fm(ain_full, AIN_ALL, 16, 512, 2, w_in_d, l * D_MODEL, 0, 1024, cons_in2)
            bg_flush_tag("lat")
            if not last:
                dma("sp", ain_full, HN[:, t0 + 1024:t0 + 2048].rearrange("(c p) t -> p c t", p=128), ["HN"], AIN_ALL)
            qra_slot = {}

            def cons_uq(col, j, pv, pres, t0=t0):
                tt0 = t0 + j * 512
                sl = slice(j * 512, (j + 1) * 512)
                if col < 1024:
                    s = rot("stg", NSTG)
                    evac_copy(stg[s][:], pv, [pres], [f"stg{s}"], scale=QSC_B)
                    dma("sp", QN[col:col + 128, tt0:tt0 + 512], stg[s][:], [f"stg{s}"], ["QN"])
                    return
                p_ = (col - 1024) // 256
                if ((col - 1024) // 128) % 2 == 0:
                    fa = rot("tmpf", NTMP)
                    qra_slot[(p_, j)] = fa
                    P.op("dve", lambda e, fa=fa, pv=pv: e.tensor_tensor(out=tmpf[fa][:], in0=pv, in1=CSq[:, sl], op=ALU.mult),
                         [pres] + R_ROPE2, [f"tmpf{fa}"])
                else:
                    fa = qra_slot.pop((p_, j))
                    f0 = rot("tmpf", NTMP)
                    P.op("dve", lambda e, f0=f0, pv=pv: e.tensor_tensor(out=tmpf[f0][:], in0=pv, in1=SSq[:, sl], op=ALU.mult),
                         [pres] + R_ROPE2, [f"tmpf{f0}"])
                    s = rot("stg", NSTG)
                    P.op("dve", lambda e, s=s, f0=f0, fa=fa: e.tensor_tensor(out=stg[s][:], in0=tmpf[f0][:],
                                                                             in1=tmpf[fa][:], op=ALU.add),
                         [f"tmpf{f0}", f"tmpf{fa}"], [f"stg{s}"])
                    dma("sp", QPE[p_ * 128:(p_ + 1) * 128, tt0:tt0 + 512], stg[s][:], [f"stg{s}"], ["QPE"])
            linear_fm(qn_b, R_QN, 6, 512, 2, w_uq_d, l * Q_LORA, 0, 2048, cons_uq)
            bg_flush()

        while pending_cc:
            pending_cc.pop(0)()
        NKA = (CA + T) // 128
        NC4 = CA // 128
        ebf = arena[:, 0:640]
        R_EBF = ar(0, 640)
        OT_ROW = lambda r: [f"OT{r}_{i}" for i in range(NST)]
        A0 = 1024
        bsets = [
            dict(q=ain[:, 0:T], k=ain[:, 4096:4096 + CA + T],
                 v=ain[:, 8192:8192 + NKA * 128].rearrange("p (k d) -> p k d", d=128), o=ain[:, 12288:12288 + T],
                 rq=ainq(0), rk=ainq(1), rv=ainq(2), ro=ainq(3)),
            dict(q=arena[:, A0:A0 + T // 2].bitcast(BF16), k=arena[:, A0 + 2048:A0 + 2048 + (CA + T) // 2].bitcast(BF16),
                 v=arena[:, A0 + 4096:A0 + 4096 + NKA * 64].bitcast(BF16).rearrange("p (k d) -> p k d", d=128),
                 o=arena[:, A0 + 6144:A0 + 6144 + T // 2].bitcast(BF16),
                 rq=ar(A0, A0 + 2048), rk=ar(A0 + 2048, A0 + 4096), rv=ar(A0 + 4096, A0 + 6144), ro=ar(A0 + 6144, A0 + 8192)),
        ]

        def b_load(h):
            S_ = bsets[h % 2]
            dma("sp", S_["q"], QA[h * 128:(h + 1) * 128, :], ["QA"], S_["rq"])
            if CA:
                dma("sp", S_["k"][:, 0:CA], xKAT("ctx")[h * 128:(h + 1) * 128, :], ["GATH"], S_["rk"])
                dma("sp", S_["v"][:, 0:NC4, :], xVAT("ctx")[:, h * 128:(h + 1) * 128].rearrange("(k p) d -> p k d", p=128),
                    ["GATH"], S_["rv"])
            dma("sp", S_["k"][:, CA:CA + T], KA[h * 128:(h + 1) * 128, :], ["KA"], S_["rk"])
            dma("sp", S_["v"][:, NC4:NKA, :], VA[:, h * 128:(h + 1) * 128].rearrange("(k p) d -> p k d", p=128),
                ["VA"], S_["rv"])
            if CA:
                P.op("dve", lambda e, S_=S_: e.tensor_scalar(out=S_["v"][:, 0:NC4, :], in0=S_["v"][:, 0:NC4, :],
                                                            scalar1=flag[:, 0:1], scalar2=None, op0=ALU.mult),
                     S_["rv"] + ["flag"], S_["rv"])
            eb = ebias[h % 2]
            r0 = (l * 8 + h) * 128
            dma("sp", ebf, biasT_in[r0:r0 + 128, :], [], R_EBF)
            P.op("act", lambda e: e.activation(out=ebf, in_=ebf, func=AF.Exp), R_EBF, R_EBF)
            P.op("dve", lambda e, eb=eb: e.tensor_tensor(out=eb[:], in0=ebf, in1=mask01[:], op=ALU.mult),
                 R_EBF + ["mask01"], [f"ebias{h % 2}"])

        DEPTH_P = 2
        b_load(0)
        items = []
        for h in range(A_HEADS):
            for j in range(T // 128):
                items.append((h, j))
        binfo = {}

        def b_qk(idx):
            h, j = items[idx]
            S_ = bsets[h % 2]
            if j == 2 * DEPTH_P and h + 1 < A_HEADS:
                b_load(h + 1)
            qv = S_["q"][:, j * 128:(j + 1) * 128]
            kts = [kt for kt in range(5) if j + NC4 - 4 + kt >= 0]
            pair = rot("apair", 2)
            bA, bB = 2 * pair, 2 * pair + 1
            for kt in kts:
                g = j + NC4 - 4 + kt
                pv = ps[:, bA, kt * 128:(kt + 1) * 128] if kt < 4 else ps[:, bB, 0:128]
                pres = f"ps{bA}" if kt < 4 else f"ps{bB}"
                P.op("pe", lambda e, pv=pv, g=g, qv=qv, S_=S_: e.matmul(pv, S_["k"][:, g * 128:(g + 1) * 128], qv,
                                                                        start=True, stop=True),
                     S_["rk"] + S_["rq"], [pres])
            pi = rot("pt", NPT)
            lo = kts[0] * 128
            eb = ebias[h % 2]
            if lo < 512:
                P.op("act", lambda e, pi=pi, lo=lo, bA=bA:
                     e.activation(out=pt[pi][:, lo:512], in_=ps[:, bA, lo:512], func=AF.Exp),
                     [f"ps{bA}"], [f"pt{pi}"])
            P.op("act", lambda e, pi=pi, bB=bB: e.activation(out=pt[pi][:, 512:640], in_=ps[:, bB, 0:128],
                                                             func=AF.Exp), [f"ps{bB}"], [f"pt{pi}"])
            P.op("dve", lambda e, pi=pi, lo=lo, eb=eb: e.tensor_tensor(out=pt[pi][:, lo:640], in0=pt[pi][:, lo:640],
                                                                       in1=eb[:, lo:640], op=ALU.mult),
                 [f"pt{pi}", f"ebias{h % 2}"], [f"pt{pi}"])
            binfo[idx] = (pi, kts)

        def b_pv(idx):
            h, j = items[idx]
            S_ = bsets[h % 2]
            pi, kts = binfo.pop(idx)
            jg, jj = j // 4, j % 4
            par = (h * (T // 512) + jg) % 2
            bo, bd = 4 + par, 6 + par
            for i_, kt in enumerate(kts):
                g = j + NC4 - 4 + kt
                isctx = g < NC4
                pcol = pt[pi][:, kt * 128:(kt + 1) * 128]
                P.op("pe", lambda e, g=g, pcol=pcol, i_=i_, n=len(kts), S_=S_:
                     e.matmul(ps[:, bo, jj * 128:(jj + 1) * 128], S_["v"][:, g, :], pcol,
                              start=(i_ == 0), stop=(i_ == n - 1)),
                     S_["rv"] + [f"pt{pi}"], [f"ps{bo}"])
                P.op("pe", lambda e, pcol=pcol, i_=i_, n=len(kts), isctx=isctx:
                     e.matmul(ps[:, bd, jj * 128:(jj + 1) * 128], (flagones if isctx else ones_bf)[:], pcol,
                              start=(i_ == 0), stop=(i_ == n - 1)),
                     ["ones", "flagones", f"pt{pi}"], [f"ps{bd}"])
            if jj == 3:
                f0 = rot("tmpf", NTMP)
                P.op("act", lambda e, f0=f0: e.activation(out=tmpf[f0][:], in_=ps[:, bd, :], func=AF.Ln),
                     [f"ps{bd}"], [f"tmpf{f0}"])
                P.op("act", lambda e, f0=f0: e.activation(out=tmpf[f0][:], in_=tmpf[f0][:], func=AF.Exp, scale=-1.0),
                     [f"tmpf{f0}"], [f"tmpf{f0}"])
                P.op("dve", lambda e, f0=f0, S_=S_: e.tensor_tensor(out=S_["o"][:, jg * 512:(jg + 1) * 512],
                                                                    in0=ps[:, bo, :], in1=tmpf[f0][:], op=ALU.mult),
                     [f"ps{bo}", f"tmpf{f0}"], S_["ro"])
                if jg == T // 512 - 1:
                    dma("sp", OT[h * 128:(h + 1) * 128, :], S_["o"], S_["ro"], OT_ROW(h))

        for step in range(len(items) + DEPTH_P):
            if step < len(items):
                b_qk(step)
            if step >= DEPTH_P:
                b_pv(step - DEPTH_P)

        NKB = (CTX + T) // 128
        NCC = CTX // 128
        SK = CTX + T
        kpeA = ain[:, 0:SK]
        kpeB = ain[:, 4096:4096 + SK]
        C0 = 1024
        csets = [
            dict(k=ain[:, 8192:8192 + SK], v=ain[:, 12288:12288 + NKB * 128].rearrange("p (k d) -> p k d", d=128),
                 rk=ainq(2), rv=ainq(3)),
            dict(k=arena[:, C0:C0 + SK // 2].bitcast(BF16),
                 v=arena[:, C0 + 2048:C0 + 2048 + NKB * 64].bitcast(BF16).rearrange("p (k d) -> p k d", d=128),
                 rk=ar(C0, C0 + 2048), rv=ar(C0 + 2048, C0 + 4096)),
        ]
        qnt = [arena[:, 0:256].bitcast(BF16), arena[:, 256:512].bitcast(BF16)]
        qpt = [arena[:, 512:768].bitcast(BF16), arena[:, 768:1024].bitcast(BF16)]
        R_QNT = [["ar0q0"], ["ar0q1"]]
        R_QPT = [["ar1q0"], ["ar1q1"]]
        kpe2 = arena[:, C0 + 4096:C0 + 4096 + SK // 2].bitcast(BF16)
        R_KPE2 = ar(C0 + 4096, C0 + 4096 + SK // 2)
        R_C_AR = ar(0, 1024)
        SUBN = R_QNT[0] + R_QNT[1] + R_QPT[0] + R_QPT[1]
        P.op("dve", lambda e: e.memset(arena[:, 0:8], 0.0), [], R_C_AR + SUBN)
        if CTX:
            dma("sp", kpe2[:, 0:CTX], xKPE("ctx"), ["GATH"], R_KPE2)
        dma("sp", kpe2[:, CTX:SK], xKPE("own"), ["KPE"], R_KPE2)
        P.op("dve", lambda e: e.tensor_scalar(out=kpeA, in0=kpe2, scalar1=consts[:, 2:3], scalar2=None, op0=ALU.mult),
             R_KPE2 + ["consts"], ainq(0))
        P.op("dve", lambda e: e.tensor_scalar(out=kpeB, in0=kpe2, scalar1=consts[:, 3:4], scalar2=None, op0=ALU.mult),
             R_KPE2 + ["consts"], ainq(1))

        def c_load(h):
            S_ = csets[h % 2]
            if CTX:
                dma("sp", S_["k"][:, 0:CTX], xKN("ctx", h), ["GATH"], S_["rk"])
                for t_ in range(0, T, 1024):
                    dma("sp", S_["v"][:, t_ // 128:t_ // 128 + 8, :],
                        xVB("ctx", t_, 1024)[:, h * 128:(h + 1) * 128].rearrange("(k p) d -> p k d", p=128),
                        ["GATH"], S_["rv"])
            dma("sp", S_["k"][:, CTX:SK], xKN("own", h), ["KN"], S_["rk"])
            for t_ in range(0, T, 1024):
                dma("sp", S_["v"][:, NCC + t_ // 128:NCC + t_ // 128 + 8, :],
                    xVB("own", t_, 1024)[:, h * 128:(h + 1) * 128].rearrange("(k p) d -> p k d", p=128),
                    ["VB"], S_["rv"])
            if CTX:
                P.op("dve", lambda e, S_=S_: e.tensor_scalar(out=S_["v"][:, 0:NCC, :], in0=S_["v"][:, 0:NCC, :],
                                                            scalar1=flag[:, 0:1], scalar2=None, op0=ALU.mult),
                     S_["rv"] + ["flag"], S_["rv"])

        def c_qload(h, i):
            qi = (h * NST + i) % 2
            dma("sp", qnt[qi], QN[h * 128:(h + 1) * 128, i * 512:(i + 1) * 512], ["QN"], R_QNT[qi])
            dma("sp", qpt[qi], QPE[(h // 2) * 128:(h // 2 + 1) * 128, i * 512:(i + 1) * 512], ["QPE"], R_QPT[qi])

        c_load(0)
        c_qload(0, 0)
        citems = []
        for h in range(B_HEADS):
            for i in range(NST):
                ktl = list(range(NCC)) + [NCC + o for o in range(4 * i + 4)]
                for i_, g in enumerate(ktl):
                    citems.append((h, i, g, i_, len(ktl)))
        cinfo = {}

        def c_qk(idx):
            h, i, g, i_, n = citems[idx]
            S_ = csets[h % 2]
            if i_ == 0:
                if i + 1 < NST:
                    c_qload(h, i + 1)
                elif h + 1 < B_HEADS:
                    c_qload(h + 1, 0)
            if i == 0 and i_ == DEPTH_P + 1 and h + 1 < B_HEADS:
                c_load(h + 1)
            qi = (h * NST + i) % 2
            kpeX, kpres = (kpeA, ainq(0)) if h % 2 == 0 else (kpeB, ainq(1))
            b = gbank(LOWB)
            P.op("pe", lambda e, b=b, g=g, qi=qi, S_=S_: e.matmul(ps[:, b, :], S_["k"][:, g * 128:(g + 1) * 128], qnt[qi],
                                                                  start=True, stop=False),
                 S_["rk"] + R_QNT[qi], [f"ps{b}"])
            P.op("pe", lambda e, b=b, g=g, qi=qi, kpeX=kpeX: e.matmul(ps[:, b, :], kpeX[:, g * 128:(g + 1) * 128],
                                                                      qpt[qi], start=False, stop=True),
                 kpres + R_QPT[qi], [f"ps{b}"])
            pi = rot("pt", NPT)
            pv = pt[pi][:, 0:512]
            P.op("act", lambda e, pv=pv, b=b: e.activation(out=pv, in_=ps[:, b, :], func=AF.Exp),
                 [f"ps{b}"], [f"pt{pi}"])
            own = g - NCC
            if own >= 4 * i:
                r = own - 4 * i
                P.op("dve", lambda e, pv=pv, r=r: e.tensor_tensor(out=pv, in0=pv, in1=mlamask[:, r * 512:(r + 1) * 512],
                                                                  op=ALU.mult),
                     [f"pt{pi}", "mlamask"], [f"pt{pi}"])
            cinfo[idx] = pi

        def c_pv(idx):
            h, i, g, i_, n = citems[idx]
            S_ = csets[h % 2]
            pi = cinfo.pop(idx)
            pv = pt[pi][:, 0:512]
            qi = (h * NST + i) % 2
            bo, bd = 4 + qi, 6 + qi
            isctx = g < NCC
            P.op("pe", lambda e, g=g, pv=pv, S_=S_: e.matmul(ps[:, bo, :], S_["v"][:, g, :], pv,
                                                             start=(i_ == 0), stop=(i_ == n - 1)),
                 S_["rv"] + [f"pt{pi}"], [f"ps{bo}"])
            P.op("pe", lambda e, pv=pv: e.matmul(ps[:, bd, :], (flagones if isctx else ones_bf)[:], pv,
                                                 start=(i_ == 0), stop=(i_ == n - 1)),
                 ["ones", "flagones", f"pt{pi}"], [f"ps{bd}"])
            if i_ == n - 1:
                f0 = rot("tmpf", NTMP)
                P.op("act", lambda e, f0=f0: e.activation(out=tmpf[f0][:], in_=ps[:, bd, :], func=AF.Ln),
                     [f"ps{bd}"], [f"tmpf{f0}"])
                P.op("act", lambda e, f0=f0: e.activation(out=tmpf[f0][:], in_=tmpf[f0][:], func=AF.Exp, scale=-1.0),
                     [f"tmpf{f0}"], [f"tmpf{f0}"])
                s = rot("stg", NSTG)
                P.op("dve", lambda e, f0=f0, s=s: e.tensor_tensor(out=stg[s][:], in0=ps[:, bo, :], in1=tmpf[f0][:],
                                                                  op=ALU.mult),
                     [f"ps{bo}", f"tmpf{f0}"], [f"stg{s}"])
                dma("sp", OT[(8 + h) * 128:(9 + h) * 128, i * 512:(i + 1) * 512], stg[s][:], [f"stg{s}"], [f"OT{8 + h}_{i}"])

        for step in range(len(citems) + DEPTH_P):
            if step < len(citems):
                c_qk(step)
            if step >= DEPTH_P:
                c_pv(step - DEPTH_P)
        P.op("dve", lambda e: e.memset(arena[:, 0:8], 0.0), SUBN, R_C_AR)

        def make_resid(x_read, t0):
            slots = {}

            def pre(col, j):
                xr = rot("xres", NXR)
                tt0 = t0 + j * 512
                nm = f"X{col // 128}_{tt0 // 512}"
                dma("sp", xres[xr][:], x_read[col:col + 128, tt0:tt0 + 512], [nm], [f"xres{xr}"])
                slots[(col, j)] = xr

            def cons(col, j, pv, pres):
                xr = slots.pop((col, j))
                tt0 = t0 + j * 512
                nm = f"X{col // 128}_{tt0 // 512}"
                P.op("dve", lambda e, xr=xr, pv=pv: e.tensor_tensor(out=xres[xr][:], in0=pv, in1=xres[xr][:], op=ALU.add),
                     [pres, f"xres{xr}"], [f"xres{xr}"])
                dma("sp", XR[col:col + 128, tt0:tt0 + 512], xres[xr][:], [f"xres{xr}"], [nm])
            return cons, pre

        ain2 = arena[:, 0:8192].bitcast(BF16).rearrange("p (c t) -> p c t", c=16)
        dbufs = [(ain_full, AIN_ALL), (ain2, ar(0, 8192))]

        def d_load(st, k):
            dma("sp", dbufs[k][0], OT[:, st * 1024:(st + 1) * 1024].rearrange("(c p) t -> p c t", p=128),
                [f"OT{c}_{i}" for c in range(16) for i in (2 * st, 2 * st + 1)], dbufs[k][1])
        d_load(0, 0)
        e0_queued = False
        for st in range(NSTT):
            t0 = st * 1024
            k = st % 2
            if st + 1 < NSTT:
                d_load(st + 1, 1 - k)
            if st == NSTT - 1 and NSTT > 1:
                bgq.extend(norm_to(HE, "HE", XR, 0, G0 + 16))
                e0_queued = True
                state["grp"] = 0
            cons, pre = make_resid(x_src, t0)
            linear_fm(dbufs[k][0], dbufs[k][1], 16, 512, 2, w_out_d, l * D_MODEL, 0, D_MODEL, cons, pre=pre)
            bg_flush()
        if not e0_queued:
            bgq.extend(norm_to(HE, "HE", XR, 0, G0 + 16))
            bg_flush()

        qx = arena[:, 8192:10240].bitcast(BF16).rearrange("p (h t) -> p h t", h=4)
        ox = arena[:, 0:2048].bitcast(BF16).rearrange("p (h t) -> p h t", h=4)
        R_QX, R_OX = ar(8192, 10240), ar(0, 2048)
        for st in range(NSTT):
            t0 = st * 1024

            load_ain(HE, "HE", t0)
            defer = None
            if st + 1 < NSTT:
                bgq.extend(norm_to(HE, "HE", XR, t0 + 1024, G0 + 16))
            elif NSTT > 1:
                bgq.extend(norm_to(HF, "HF", XR, 0, G0 + 32))
            else:
                defer = norm_to(HF, "HF", XR, 0, G0 + 32)
            state["grp"] = 0

            def cons_qx(col, j, pv, pres):
                evac_copy(qx[:, col // 128, j * 512:(j + 1) * 512], pv, [pres], R_QX, scale=QSC_A)
            linear_fm(ain_full, AIN_ALL, 16, 512, 2, w_xq_d, l * D_MODEL, 0, 512, cons_qx, banks=LOWB)
            xinfo = {}

            def x_qk(idx):
                j, h, mt = idx // 8, (idx // 2) % 4, idx % 2
                b = gbank(LOWB)
                P.op("pe", lambda e, b=b, h=h, mt=mt, j=j: e.matmul(ps[:, b, :], kx_v[:, h, mt * 128:(mt + 1) * 128],
                                                                    qx[:, h, j * 512:(j + 1) * 512], start=True, stop=True),
                     ["kx"] + R_QX, [f"ps{b}"])
                pi = rot("pt", NPT)
                pv = pt[pi][:, 0:512]
                P.op("act", lambda e, pv=pv, b=b: e.activation(out=pv, in_=ps[:, b, :], func=AF.Exp),
                     [f"ps{b}"], [f"pt{pi}"])
                xinfo[idx] = pi

            def x_pv(idx):
                j, h, mt = idx // 8, (idx // 2) % 4, idx % 2
                pi = xinfo.pop(idx)
                pv = pt[pi][:, 0:512]
                par = (j * 4 + h) % 2
                bo, bd = 4 + par, 6 + par
                P.op("pe", lambda e, pv=pv, h=h, mt=mt: e.matmul(ps[:, bo, :], vx_v[:, mt, h * 128:(h + 1) * 128],
                                                                 pv, start=(mt == 0), stop=(mt == 1)),
                     ["vx", f"pt{pi}"], [f"ps{bo}"])
                P.op("pe", lambda e, pv=pv, mt=mt: e.matmul(ps[:, bd, :], ones_bf[:], pv, start=(mt == 0), stop=(mt == 1)),
                     ["ones", f"pt{pi}"], [f"ps{bd}"])
                if mt == 1:
                    f0 = rot("tmpf", NTMP)
                    P.op("act", lambda e, f0=f0: e.activation(out=tmpf[f0][:], in_=ps[:, bd, :], func=AF.Ln),
                         [f"ps{bd}"], [f"tmpf{f0}"])
                    P.op("act", lambda e, f0=f0: e.activation(out=tmpf[f0][:], in_=tmpf[f0][:], func=AF.Exp, scale=-1.0),
                         [f"tmpf{f0}"], [f"tmpf{f0}"])
                    P.op("dve", lambda e, f0=f0, h=h, j=j: e.tensor_tensor(out=ox[:, h, j * 512:(j + 1) * 512], in0=ps[:, bo, :],
                                                                           in1=tmpf[f0][:], op=ALU.mult),
                         [f"ps{bo}", f"tmpf{f0}"], R_OX)
            for step in range(16 + DEPTH_P):
                if step < 16:
                    x_qk(step)
                if step >= DEPTH_P:
                    x_pv(step - DEPTH_P)
            cons, pre = make_resid(XR, t0)
            linear_fm(ox, R_OX, 4, 512, 2, w_xo_d, l * 512, 0, D_MODEL, cons, pre=pre, PF=4)
            if defer:
                bgq.extend(defer)
            bg_flush()

        NFH = D_FF // 2 // 128
        actv = arena[:, 0:11264].bitcast(BF16).rearrange("p (c t) -> p c t", c=NFH)
        R_ACT = ar(0, 11264)
        NF = T // 1024
        deferF = None
        for st in range(NF):
            t0 = st * 1024
            load_ain(HF, "HF", t0)
            if st + 1 < NF:
                bgq.extend(norm_to(HF, "HF", XR, t0 + 1024, G0 + 32))
            elif l + 1 < DEPTH:
                if NF > 1:
                    bgq.extend(norm_to(HN, "HN", XR, 0, NG_L * (l + 1)))
                else:
                    deferF = norm_to(HN, "HN", XR, 0, NG_L * (l + 1))
            state["grp"] = 0
            for half in range(2):
                c_base = half * (D_FF // 2)
                for (c, gw) in wgroups(16, c_base, D_FF // 2, cap=384):
                    sg, wg = load_w(w_gate_d, l * D_MODEL, 16, c, gw)
                    su, wu = load_w(w_up_d, l * D_MODEL, 16, c, gw)
                    for f in range(0, gw, 128):
                        fc = (c + f - c_base) // 128
                        for j in range(2):
                            bg = gbank(ALLB)
                            bu = gbank(ALLB)
                            for k in range(16):
                                P.op("pe", lambda e, bg=bg, wg=wg, k=k, f=f, j=j:
                                     e.matmul(ps[:, bg, :], wg[:, k, f:f + 128], ain_v16[:, k, j * 512:(j + 1) * 512],
                                              start=(k == 0), stop=(k == 15)),
                                     [f"w{sg}"] + AIN_H[j], [f"ps{bg}"])
                            for k in range(16):
                                P.op("pe", lambda e, bu=bu, wu=wu, k=k, f=f, j=j:
                                     e.matmul(ps[:, bu, :], wu[:, k, f:f + 128], ain_v16[:, k, j * 512:(j + 1) * 512],
                                              start=(k == 0), stop=(k == 15)),
                                     [f"w{su}"] + AIN_H[j], [f"ps{bu}"])
                            f0 = rot("tmpf", NTMP)
                            P.op("act", lambda e, f0=f0, bg=bg: e.activation(out=tmpf[f0][:], in_=ps[:, bg, :], func=AF.Silu),
                                 [f"ps{bg}"], [f"tmpf{f0}"])
                            P.op("dve", lambda e, f0=f0, bu=bu, fc=fc, j=j:
                                 e.tensor_tensor(out=actv[:, fc, j * 512:(j + 1) * 512], in0=ps[:, bu, :], in1=tmpf[f0][:],
                                                 op=ALU.mult),
                                 [f"ps{bu}", f"tmpf{f0}"], [f"ar{fc}"])
                cons, pre = make_resid(XR, t0)
                linear_fm(actv, R_ACT, NFH, 512, 2, w_down_d, l * D_FF + c_base, 0, D_MODEL, cons, pre=pre)
            if deferF:
                bgq.extend(deferF)
                deferF = None
            bg_flush()

    GF = NG_L * DEPTH + 16
    for t0 in range(0, T, 512):
        for pc in norm_pieces(XR, t0, 512, GF, None, None, out_f32_dma=outT):
            pc()

    P.emit(final_waits=final_ops)
    P.close()
    return nc, P


def _fm_cols(g):
    return np.ascontiguousarray(g.reshape(-1, 128).T)


def prepare_shared(inputs, depth, layer0=0):
    f32 = np.float32
    L = slice(layer0, layer0 + depth)
    w_in = np.asarray(inputs["w_in"])[L]
    kr = w_in[:, :, 4352:4416]
    kr_sw = np.concatenate([kr[:, :, 32:], kr[:, :, :32]], axis=-1)
    w_in_r = np.concatenate([w_in[:, :, :4352], kr, kr, kr_sw, kr_sw], axis=-1)
    w_uq = np.asarray(inputs["w_uq"])[L].reshape(depth, Q_LORA, 8, 192)
    nope = w_uq[..., :128].reshape(depth, Q_LORA, 1024)
    rope = w_uq[..., 128:]
    rope_sw = np.concatenate([rope[..., 32:], rope[..., :32]], axis=-1)
    ra = rope.reshape(depth, Q_LORA, 4, 128)
    rb_ = rope_sw.reshape(depth, Q_LORA, 4, 128)
    w_uq_r = np.concatenate([nope, np.stack([ra, rb_], axis=3).reshape(depth, Q_LORA, 1024)], axis=-1)
    w_ukv = np.asarray(inputs["w_ukv"])[L].reshape(depth, KV_LORA, 8, 256)
    w_ukv_r = np.concatenate([w_ukv[..., :128].reshape(depth, KV_LORA, 1024),
                              w_ukv[..., 128:].reshape(depth, KV_LORA, 1024)], axis=-1)
    gcols = []
    for l in range(layer0, layer0 + depth):
        gcols += [_fm_cols(np.asarray(inputs["norm_mix"])[l]), _fm_cols(np.asarray(inputs["norm_mem"])[l]),
                  _fm_cols(np.asarray(inputs["norm_ffn"])[l]), _fm_cols(np.asarray(inputs["q_norm"])[l]),
                  _fm_cols(np.asarray(inputs["kv_norm"])[l])]
    gcols += [_fm_cols(np.asarray(inputs["mem_norm"])), _fm_cols(np.asarray(inputs["norm_final"]))]
    gains = np.concatenate(gcols, axis=1).astype(f32)
    kk = np.arange(640)[:, None]
    qq = np.arange(128)[None, :]
    rel = np.clip(512 + qq - kk, -REL_CLIP, REL_CLIP) + REL_CLIP
    rb = np.asarray(inputs["rel_bias"])[L]
    bt = rb[:, :, rel]
    bt = bt.reshape(depth, 8, 5, 128, 128).transpose(0, 1, 3, 2, 4).reshape(depth * 8 * 128, 640)
    cq = 8 + qq // 64
    ck = kk // 64
    m01 = ((ck >= cq - 8) & (ck <= cq)).astype(f32)
    m01 = m01.reshape(5, 128, 128).transpose(1, 0, 2).reshape(128, 640)
    kq = (np.arange(128)[:, None] // 64)
    qc = (np.arange(512)[None, :] // 64)
    mm = [((2 * r + kq) <= qc).astype(f32) for r in range(4)]
    mlamask = np.concatenate(mm, axis=1)
    half = 32
    inv = (ROPE_THETA ** (-np.arange(half, dtype=f32) / half)).astype(f32)
    consts = np.zeros((128, 8), f32)
    consts[:, 0] = np.tile(inv, 4)
    consts[:, 1] = np.tile(np.concatenate([-np.ones(32, f32), np.ones(32, f32)]), 2)
    consts[:64, 2] = 1.0
    consts[64:, 3] = 1.0
    consts[:, 4] = -math.pi
    consts[:, 5] = EPS

    def flat(a):
        a = np.asarray(a)
        return np.ascontiguousarray(a.reshape(-1, a.shape[-1]), dtype=f32)

    return {
        "consts": consts, "gains": gains, "mask01": np.ascontiguousarray(m01), "mlamask": np.ascontiguousarray(mlamask),
        "biasT": np.ascontiguousarray(bt, dtype=f32),
        "w_in": flat(w_in_r), "w_uq": flat(w_uq_r), "w_ukv": flat(w_ukv_r),
        "w_out": flat(np.asarray(inputs["w_out"])[L]), "w_xq": flat(np.asarray(inputs["w_xq"])[L]),
        "w_xkv": flat(np.asarray(inputs["w_xkv"])[L]), "w_xo": flat(np.asarray(inputs["w_xo"])[L]),
        "w_gate": flat(np.asarray(inputs["w_gate"])[L]), "w_up": flat(np.asarray(inputs["w_up"])[L]),
        "w_down": flat(np.asarray(inputs["w_down"])[L]),
    }


_CACHE = {}


def kernel(**inputs):
    x = np.asarray(inputs["x"])
    mem = np.asarray(inputs["mem"])
    pos = np.asarray(inputs["positions"])
    B, S, D = x.shape
    NH = 2
    T = S // NH
    n_cores = B * NH
    depth = int(np.asarray(inputs["w_in"]).shape[0])
    cfg = Cfg(T=T, CTX=T, depth=depth, n_cores=n_cores)
    key = (T, T, depth, n_cores)
    if key not in _CACHE:
        _CACHE[key] = build_program(cfg)[0]
    nc = _CACHE[key]
    shared = prepare_shared(inputs, depth)
    in_maps = []
    for c in range(n_cores):
        b, hf = c // NH, c % NH
        m = dict(shared)
        m["xT"] = np.ascontiguousarray(x[b, hf * T:(hf + 1) * T].T)
        m["memT"] = np.ascontiguousarray(mem[b].T)
        m["pos"] = np.ascontiguousarray(pos[b:b + 1, hf * T:(hf + 1) * T]).astype(np.int32)
        m["flag"] = np.full((128, 1), 1.0 if hf > 0 else 0.0, np.float32)
        in_maps.append(m)
    res = run_bass_kernel_spmd(nc, in_maps, core_ids=list(range(n_cores)))
    out = np.empty((B, S, D), np.float32)
    for c in range(n_cores):
        b, hf = c // NH, c % NH
        out[b, hf * T:(hf + 1) * T] = res.results[c]["outT"].T
    return out
```

```python
import contextlib
import math
import numpy as np
import concourse.bass as bass
import concourse.mybir as mybir
from concourse.bass_utils import run_bass_kernel_spmd

F32 = mybir.dt.float32
BF16 = mybir.dt.bfloat16
I32 = mybir.dt.int32
AF = mybir.ActivationFunctionType
ALU = mybir.AluOpType

ENGS = ("pe", "act", "dve", "pool", "sp")

D_MODEL = 2048
CHUNK = 64
A_HEADS = 8
B_HEADS = 8
Q_LORA = 768
KV_LORA = 512
N_MEM = 256
X_HEADS = 4
D_FF = 5632
EPS = 1e-6
REL_CLIP = 128
ROPE_THETA = 10000.0
W_IN_R = 4608
NG_L = 58


class Op:
    __slots__ = ("eng", "fn", "deps", "is_dma", "sig", "sem", "val", "pre_wait", "idx", "inc")

    def __init__(self, eng, fn, is_dma):
        self.eng = eng
        self.fn = fn
        self.is_dma = bool(is_dma)
        self.inc = 1 if is_dma == "cc" else 16
        self.deps = []
        self.sig = False
        self.sem = None
        self.val = 0
        self.pre_wait = None


class Prog:
    def __init__(self, nc, sem_wrap=30000):
        self.nc = nc
        self.ops = {e: [] for e in ENGS}
        self.last_w = {}
        self.readers = {}
        self.n_dma_sems = {"sp": 28, "pool": 20, "act": 8, "dve": 4, "pe": 4}
        self.sem_wrap = sem_wrap
        self.stack = contextlib.ExitStack()

    def sbuf(self, name, shape, dtype):
        return self.stack.enter_context(self.nc.sbuf_tensor("sb_" + name, list(shape), dtype))

    def psum(self, name, shape, dtype):
        return self.stack.enter_context(self.nc.psum_tensor(name, list(shape), dtype))

    def op(self, eng, fn, reads=(), writes=(), dma=False):
        o = Op(eng, fn, dma)
        o.idx = len(self.ops[eng])
        lw = self.last_w
        rd = self.readers
        best = {}
        dmas = {}

        def add(d):
            if d.is_dma:
                dmas[id(d)] = d
            else:
                b = best.get(d.eng)
                if b is None or d.idx > b.idx:
                    best[d.eng] = d

        for r in reads:
            for w in lw.get(r, ()):
                add(w)
        for r in writes:
            ws = lw.get(r)
            rs = rd.get(r)
            if rs:
                for x in rs:
                    add(x)
                if ws:
                    for w in ws:
                        add(w)
                lw[r] = [o]
                rd[r] = []
            elif ws and dma and all(w.is_dma for w in ws):
                ws.append(o)
            else:
                if ws:
                    for w in ws:
                        add(w)
                lw[r] = [o]
                rd[r] = []
        for r in reads:
            rd.setdefault(r, []).append(o)
        o.deps = [d for d in list(best.values()) + list(dmas.values()) if d is not o]
        self.ops[eng].append(o)
        return o

    @staticmethod
    def _skip(d, o):
        return d.eng == o.eng and d.eng == "pe" and not o.is_dma and not d.is_dma

    def emit(self, final_waits=()):
        nc = self.nc
        for e in ENGS:
            for o in self.ops[e]:
                for d in o.deps:
                    if d.is_dma or self._skip(d, o):
                        continue
                    d.sig = True
        sems = {}

        def get_sem(name):
            if name not in sems:
                sems[name] = self.stack.enter_context(nc.semaphore(name))
            return sems[name]

        for e in ENGS:
            cnt = 0
            dma_i = 0
            cc_i = 0
            dma_vals = {}
            dma_last = {}
            for o in self.ops[e]:
                if o.is_dma:
                    if o.inc == 1:
                        name = f"cc_{e}"
                    else:
                        name = f"d_{e}_{dma_i % self.n_dma_sems[e]}"
                        dma_i += 1
                    o.sem = get_sem(name)
                    o.val = dma_vals.get(name, 0) + o.inc
                    dma_vals[name] = o.val
                    o.pre_wait = dma_last.get(name)
                    dma_last[name] = o
                elif o.sig:
                    o.sem = get_sem(f"c_{e}_{cnt // self.sem_wrap}")
                    o.val = cnt % self.sem_wrap + 1
                    cnt += 1
        self.n_waits = 0
        self.n_ins = 0
        with nc.Block() as block:
            def run(e, h):
                waited = {}

                def wait(sem, val):
                    k = id(sem)
                    if waited.get(k, 0) >= val:
                        return
                    waited[k] = val
                    h.wait_ge(sem, val)
                    self.n_waits += 1

                for o in self.ops[e]:
                    if o.pre_wait is not None:
                        wait(o.pre_wait.sem, o.pre_wait.val)
                    for d in o.deps:
                        if d.is_dma:
                            wait(d.sem, d.val)
                        elif d.sig and not self._skip(d, o):
                            wait(d.sem, d.val)
                    ins = o.fn(h)
                    self.n_ins += 1
                    if o.is_dma:
                        ins.then_inc(o.sem, o.inc)
                    elif o.sig:
                        ins.then_inc(o.sem, 1)
                if e == "sp":
                    for o in final_waits:
                        wait(o.sem, o.val)

            if self.ops["pe"]:
                block.tensor(lambda h: run("pe", h))
            if self.ops["act"]:
                block.scalar(lambda h: run("act", h))
            if self.ops["dve"]:
                block.vector(lambda h: run("dve", h))
            if self.ops["pool"]:
                block.gpsimd(lambda h: run("pool", h))
            block.sync(lambda h: run("sp", h))

    def close(self):
        self.stack.close()


class Cfg:
    def __init__(self, T=2048, CTX=2048, depth=4, layer0=0, first=True, last=True, n_cores=8):
        self.T = T
        self.CTX = CTX
        self.CA = min(512, CTX)
        self.depth = depth
        self.n_cores = n_cores


def build_program(cfg):
    T, CTX, CA, DEPTH = cfg.T, cfg.CTX, cfg.CA, cfg.depth
    NSUB = 128
    NST = T // 512
    nc = bass.Bass("TRN2", target_bir_lowering=False)
    P = Prog(nc)

    def din(name, shape, dt=F32):
        return nc.dram_tensor(name, list(shape), dt, kind="ExternalInput").ap()

    def dscr(name, shape, dt=BF16):
        return nc.dram_tensor(name, list(shape), dt, kind="Internal").ap()

    xT_in = din("xT", [D_MODEL, T])
    memT_in = din("memT", [D_MODEL, N_MEM])
    pos_in = din("pos", [1, T], I32)
    consts_in = din("consts", [128, 8])
    gains_in = din("gains", [128, NG_L * DEPTH + 32])
    flag_in = din("flag", [128, 1])
    mask01_in = din("mask01", [128, 640])
    mlamask_in = din("mlamask", [128, 4 * 512])
    biasT_in = din("biasT", [DEPTH * 8 * 128, 640])
    w_in_d = din("w_in", [DEPTH * D_MODEL, W_IN_R])
    w_uq_d = din("w_uq", [DEPTH * Q_LORA, 2048])
    w_ukv_d = din("w_ukv", [DEPTH * KV_LORA, 2048])
    w_out_d = din("w_out", [DEPTH * D_MODEL, D_MODEL])
    w_xq_d = din("w_xq", [DEPTH * D_MODEL, 512])
    w_xkv_d = din("w_xkv", [DEPTH * D_MODEL, 1024])
    w_xo_d = din("w_xo", [DEPTH * 512, D_MODEL])
    w_gate_d = din("w_gate", [DEPTH * D_MODEL, D_FF])
    w_up_d = din("w_up", [DEPTH * D_MODEL, D_FF])
    w_down_d = din("w_down", [DEPTH * D_FF, D_MODEL])
    outT = nc.dram_tensor("outT", [D_MODEL, T], F32, kind="ExternalOutput").ap()

    XR = dscr("XR", [D_MODEL, T], F32)
    ROPE = dscr("ROPE", [4 * 128, T], F32)
    QA = dscr("QA", [8 * 128, T])
    KA = dscr("KA", [8 * 128, T])
    VA = dscr("VA", [T, 1024])
    QN = dscr("QN", [8 * 128, T])
    QPE = dscr("QPE", [4 * 128, T])
    OT = dscr("OT", [D_MODEL, T])
    TB = T // 1024
    chunks = []
    for base in (0, T):
        for s_ in range(0, T, 1024):
            chunks.append((base + s_, 1024))
    chunks.append((2 * T, T // 8 + 512))
    chunks.append((2 * T + T // 8 + 512, 512))
    S_t = [dscr(f"SND{i}", [n_, 1024]) for i, (s_, n_) in enumerate(chunks)]
    G_t = [dscr(f"GTH{i}", [2 * n_, 1024]) for i, (s_, n_) in enumerate(chunks)] if CTX else None

    def xrows(kind, r0, n):
        for i, (s_, n_) in enumerate(chunks):
            if s_ <= r0 and r0 + n <= s_ + n_:
                buf = S_t[i] if kind == "own" else G_t[i]
                return buf[r0 - s_:r0 - s_ + n, :]
        raise AssertionError((r0, n))

    def xKN(kind, h):
        return xrows(kind, h * 128 * TB, 128 * TB).rearrange("(a b) c -> a (b c)", b=TB)

    def xVB(kind, tok0, n):
        return xrows(kind, T + tok0, n)

    def xKPE(kind):
        return xrows(kind, 2 * T, T // 8).rearrange("(a b) c -> a (b c)", b=TB)

    def xKAT(kind):
        return xrows(kind, 2 * T + T // 8, 512).rearrange("r (s c) -> (r s) c", s=2)

    def xVAT(kind):
        return xrows(kind, 2 * T + T // 8 + 512, 512)

    WS = 6144
    wslot = [P.sbuf(f"wslot{i}", [128, WS], BF16) for i in range(4)]
    ain = P.sbuf("ain", [128, 16 * 1024], BF16)
    xs = [P.sbuf(f"xs{i}", [128, 16 * NSUB], F32) for i in range(2)]
    arena = P.sbuf("arena", [128, 11264], F32)
    sqb = P.sbuf("sqb", [128, 16 * NSUB], BF16)
    sql = [P.sbuf(f"sql{i}", [128, 512], BF16) for i in range(3)]
    rstd = [P.sbuf(f"rstd{i}", [128, 512], F32) for i in range(2)]
    NTMP, NPT, NXR, NSTG = 3, 4, 8, 4
    tmpf = [P.sbuf(f"tmpf{i}", [128, 512], F32) for i in range(NTMP)]
    stg = [P.sbuf(f"stg{i}", [128, 512], BF16) for i in range(NSTG)]
    pt = [P.sbuf(f"pt{i}", [128, 640], BF16) for i in range(NPT)]
    xres = [P.sbuf(f"xres{i}", [128, 512], F32) for i in range(NXR)]
    gains = P.sbuf("gains", [128, NG_L * DEPTH + 32], F32)
    consts = P.sbuf("consts", [128, 8], F32)
    flag = P.sbuf("flag", [128, 1], F32)
    ones_bf = P.sbuf("ones_bf", [128, 128], BF16)
    flagones = P.sbuf("flagones", [128, 128], BF16)
    mask01 = P.sbuf("mask01", [128, 640], F32)
    mlamask = P.sbuf("mlamask", [128, 4 * 512], BF16)
    memn = P.sbuf("memn", [128, 16 * 256], BF16)
    kx = P.sbuf("kx", [128, 4 * 256], BF16)
    vx = P.sbuf("vx", [128, 2 * 512], BF16)
    ebias = [P.sbuf(f"ebias{i}", [128, 640], BF16) for i in range(2)]
    ps = P.psum("ps", [128, 8, 512], F32)

    state = {"w": 0, "stg": 0, "tmpf": 0, "pt": 0, "sql": 0, "xres": 0, "rstd": 0, "ev": 0, "apair": 0,
             "xsi": 0, "hb": 0, "grp": 0}

    def rot(key, n):
        v = state[key]
        state[key] = (v + 1) % n
        return v

    def gbank(allowed):
        k = "bank_" + str(allowed)
        i = state.get(k, 0)
        state[k] = i + 1
        return allowed[i % len(allowed)]

    ALLB = [0, 1, 2, 3, 4, 5, 6, 7]
    LOWB = [0, 1, 2, 3]

    def ar(lo, hi):
        return [f"ar{g}" for g in range(lo // 512, (hi + 511) // 512)]

    def ainq(q):
        return [f"ain{q}a", f"ain{q}b"]

    AIN_H = [[f"ain{q}a" for q in range(4)], [f"ain{q}b" for q in range(4)]]
    AIN_ALL = AIN_H[0] + AIN_H[1]

    def xnames(t0, ntok, pref="X"):
        tiles = sorted(set([t0 // 512, (t0 + ntok - 1) // 512]))
        return [f"{pref}{c}_{t}" for c in range(16) for t in tiles]

    def dma(eng, out, in_, reads, writes):
        return P.op(eng, lambda e: e.dma_start(out=out, in_=in_), reads=reads, writes=writes, dma=True)

    def evac_copy(out, in_, reads, writes, scale=None, eng=None):
        e_ = eng or ("act" if rot("ev", 2) == 0 else "dve")
        if e_ == "act":
            if scale is None:
                return P.op("act", lambda e: e.copy(out=out, in_=in_), reads, writes)
            return P.op("act", lambda e: e.mul(out=out, in_=in_, mul=scale), reads, writes)
        if scale is None:
            return P.op("dve", lambda e: e.tensor_copy(out=out, in_=in_), reads, writes)
        return P.op("dve", lambda e: e.tensor_scalar(out=out, in0=in_, scalar1=scale, scalar2=None,
                                                    op0=ALU.mult), reads, writes)

    bgq = []
    BG_EVERY = 4

    def bg_run():
        it = bgq.pop(0)
        (it[1] if isinstance(it, tuple) else it)()

    def bg_step():
        state["grp"] += 1
        if bgq and state["grp"] % BG_EVERY == 0:
            bg_run()

    def bg_flush():
        while bgq:
            bg_run()

    def bg_flush_tag(tag):
        while any(isinstance(it, tuple) and it[0] == tag for it in bgq):
            bg_run()

    def wgroups(KC, c0, ncols, cap=512):
        gw_max = min(cap, (WS // KC) // 128 * 128)
        out = []
        c = c0
        while c < c0 + ncols:
            gw = min(gw_max, c0 + ncols - c)
            out.append((c, gw))
            c += gw
        return out

    pending_cc = []

    def load_w(W_ap, row0, KC, c, gw):
        state["nload"] = state.get("nload", 0) + 1
        if pending_cc and state["nload"] % 2 == 0:
            pending_cc.pop(0)()
        s = rot("w", 4)
        dst = wslot[s][:, 0:KC * gw].rearrange("p (k f) -> p k f", k=KC)
        src = W_ap[row0:row0 + KC * 128, c:c + gw].rearrange("(k p) f -> p k f", p=128)
        dma("pool", dst, src, reads=[], writes=[f"w{s}"])
        return s, dst

    def linear_fm(a_view, a_res, KC, tw, nt, W_ap, row0, c0, ncols, consume, banks=ALLB, pre=None, PF=2):
        groups = [(c, gw, f, j) for (c, gw) in wgroups(KC, c0, ncols) for f in range(0, gw, 128) for j in range(nt)]
        if pre:
            for g in groups[:PF]:
                pre(g[0] + g[2], g[3])
        cur = None
        for gi, (c, gw, f, j) in enumerate(groups):
            if cur is None or cur[0] != c:
                cur = (c,) + load_w(W_ap, row0, KC, c, gw)
            s, wv = cur[1], cur[2]
            if pre and gi + PF < len(groups):
                g2 = groups[gi + PF]
                pre(g2[0] + g2[2], g2[3])
            b = gbank(banks)
            pv = ps[:, b, 0:tw]
            for k in range(KC):
                P.op("pe", (lambda e, pv=pv, wv=wv, k=k, f=f, j=j:
                            e.matmul(pv, wv[:, k, f:f + 128], a_view[:, k, j * tw:(j + 1) * tw],
                                     start=(k == 0), stop=(k == KC - 1))),
                     reads=[f"w{s}"] + a_res, writes=[f"ps{b}"])
            consume(c + f, j, pv, f"ps{b}")
            bg_step()

    def linear_tm(a_view, a_res, KC, ntt, W_ap, row0, c0, ncols, consume, banks=ALLB):
        for (c, gw) in wgroups(KC, c0, ncols):
            s, wv = load_w(W_ap, row0, KC, c, gw)
            for tt in range(ntt):
                b = gbank(banks)
                pv = ps[:, b, 0:gw]
                for k in range(KC):
                    P.op("pe", (lambda e, pv=pv, wv=wv, k=k, tt=tt:
                                e.matmul(pv, a_view[:, k, tt * 128:(tt + 1) * 128], wv[:, k, :],
                                         start=(k == 0), stop=(k == KC - 1))),
                         reads=[f"w{s}"] + a_res, writes=[f"ps{b}"])
                consume(tt, c, gw, pv, f"ps{b}")
                bg_step()

    def rstd_from_psum(pv, psres, n_feat, w):
        r = rot("rstd", 2)
        rv = rstd[r][:, 0:w]
        P.op("act", lambda e: e.activation(out=rv, in_=pv, func=AF.Sqrt, bias=consts[:, 5:6], scale=1.0 / n_feat),
             reads=[psres, "consts"], writes=[f"rstd{r}"])
        P.op("dve", lambda e: e.reciprocal(out=rv, in_=rv), reads=[f"rstd{r}"], writes=[f"rstd{r}"])
        return rv, f"rstd{r}"

    final_ops = []

    def norm_pieces(x_ap, t0, ntok, gcol, dst_fn, dst_res_fn, out_f32_dma=None, pref="X"):
        pieces = []
        for sub in range(ntok // NSUB):
            def piece(sub=sub):
                xi = rot("xsi", 2)
                xv = xs[xi][:].rearrange("p (c t) -> p c t", c=16)
                a0 = t0 + sub * NSUB
                src = x_ap[:, a0:a0 + NSUB].rearrange("(c p) t -> p c t", p=128)
                dma("sp", xv, src, reads=xnames(a0, NSUB, pref), writes=[f"xs{xi}"])
                b = gbank(ALLB)
                pv = ps[:, b, 0:NSUB]
                for c in range(16):
                    qv = sqb[:, c * NSUB:(c + 1) * NSUB]
                    P.op("act", lambda e, qv=qv, c=c: e.activation(out=qv, in_=xv[:, c, :], func=AF.Square),
                         reads=[f"xs{xi}"], writes=[f"sq{c}"])
                    P.op("pe", lambda e, qv=qv, c=c: e.matmul(pv, ones_bf[:], qv, start=(c == 0), stop=(c == 15)),
                         reads=[f"sq{c}", "ones"], writes=[f"ps{b}"])
                rv, rres = rstd_from_psum(pv, f"ps{b}", D_MODEL, NSUB)
                for c in range(16):
                    if out_f32_dma is None:
                        dv = dst_fn(c, sub * NSUB, NSUB)
                        P.op("dve", (lambda e, dv=dv, c=c:
                                     e.scalar_tensor_tensor(out=dv, in0=xv[:, c, :], scalar=gains[:, gcol + c:gcol + c + 1],
                                                            in1=rv, op0=ALU.mult, op1=ALU.mult)),
                             reads=[f"xs{xi}", rres, "gains"], writes=dst_res_fn(c, sub * NSUB))
                    else:
                        P.op("dve", (lambda e, c=c:
                                     e.scalar_tensor_tensor(out=xv[:, c, :], in0=xv[:, c, :],
                                                            scalar=gains[:, gcol + c:gcol + c + 1],
                                                            in1=rv, op0=ALU.mult, op1=ALU.mult)),
                             reads=[f"xs{xi}", rres, "gains"], writes=[f"xs{xi}"])
                if out_f32_dma is not None:
                    dst = out_f32_dma[:, a0:a0 + NSUB].rearrange("(c p) t -> p c t", p=128)
                    final_ops.append(dma("sp", dst, xv, reads=[f"xs{xi}"], writes=["OUT"]))
            pieces.append(piece)
        return pieces

    ain_v16 = ain[:].rearrange("p (c t) -> p c t", c=16)

    def norm512(x_ap, t0, gcol, hb):
        return norm_pieces(x_ap, t0, 512, gcol,
                           lambda c, s0, w: ain_v16[:, c, hb * 512 + s0:hb * 512 + s0 + w],
                           lambda c, s0: [f"ain{c // 4}{'ab'[hb]}"])

    def norm1024(x_ap, t0, gcol):
        return norm_pieces(x_ap, t0, 1024, gcol,
                           lambda c, s0, w: ain_v16[:, c, s0:s0 + w],
                           lambda c, s0: [f"ain{c // 4}{'ab'[s0 // 512]}"])

    dma("sp", gains[:], gains_in, [], ["gains"])
    dma("sp", consts[:], consts_in, [], ["consts"])
    dma("sp", flag[:], flag_in, [], ["flag"])
    dma("sp", mask01[:], mask01_in, [], ["mask01"])
    dma("pool", mlamask[:], mlamask_in, [], ["mlamask"])
    P.op("dve", lambda e: e.memset(ones_bf[:], 1.0), [], ["ones"])
    P.op("dve", lambda e: e.tensor_scalar(out=flagones[:], in0=ones_bf[:], scalar1=flag[:, 0:1], scalar2=None,
                                          op0=ALU.mult), ["ones", "flag"], ["flagones"])
    QSC_B = (128 + 64) ** -0.5
    QSC_A = 128.0 ** -0.5
    ROPET0 = 8704
    ropet = arena[:, ROPET0:ROPET0 + 2048]
    ROPET_RES = ar(ROPET0, ROPET0 + 2048)
    PR = ar(0, 2560)
    for t0 in range(0, T, 512):
        posi = arena[:, 0:512].bitcast(I32)
        posf = arena[:, 512:1024]
        ang = arena[:, 1024:1536]
        fr = arena[:, 1536:2048]
        dma("sp", posi, pos_in[0:1, t0:t0 + 512].partition_broadcast(128), [], PR)
        P.op("dve", lambda e: e.tensor_copy(out=posf, in_=posi), PR, PR)
        P.op("dve", lambda e: e.tensor_scalar(out=ang, in0=posf, scalar1=consts[:, 0:1], scalar2=None, op0=ALU.mult),
             PR + ["consts"], PR)
        for which in range(2):
            off = 0.25 if which == 0 else 0.0
            zi = arena[:, 0:512].bitcast(I32)
            zf = arena[:, 512:1024]
            mk = arena[:, 2048:2560]
            P.op("dve", lambda e, off=off: e.tensor_scalar(out=fr, in0=ang, scalar1=1.0 / (2 * math.pi), scalar2=off,
                                                           op0=ALU.mult, op1=ALU.add), PR, PR)
            P.op("dve", lambda e: e.tensor_copy(out=zi, in_=fr), PR, PR)
            P.op("dve", lambda e: e.tensor_copy(out=zf, in_=zi), PR, PR)
            P.op("dve", lambda e: e.tensor_tensor(out=fr, in0=fr, in1=zf, op=ALU.subtract), PR, PR)
            P.op("dve", lambda e: e.tensor_scalar(out=mk, in0=fr, scalar1=0.5, scalar2=None, op0=ALU.is_gt), PR, PR)
            P.op("dve", lambda e: e.tensor_tensor(out=fr, in0=fr, in1=mk, op=ALU.subtract), PR, PR)
            P.op("dve", lambda e: e.tensor_scalar(out=mk, in0=fr, scalar1=-0.5, scalar2=None, op0=ALU.is_lt), PR, PR)
            P.op("dve", lambda e: e.tensor_tensor(out=fr, in0=fr, in1=mk, op=ALU.add), PR, PR)
            tv = ropet[:, which * 512:(which + 1) * 512]
            P.op("act", lambda e, tv=tv: e.activation(out=tv, in_=fr, func=AF.Sin, scale=2 * math.pi * 0.999999),
                 PR + ["consts"], ROPET_RES)
            if which == 1:
                P.op("dve", lambda e, tv=tv: e.tensor_scalar(out=tv, in0=tv, scalar1=consts[:, 1:2], scalar2=None,
                                                             op0=ALU.mult), ROPET_RES + ["consts"], ROPET_RES)
            tq = ropet[:, (2 + which) * 512:(3 + which) * 512]
            P.op("dve", lambda e, tv=tv, tq=tq: e.tensor_scalar(out=tq, in0=tv, scalar1=QSC_B, scalar2=None,
                                                                op0=ALU.mult), ROPET_RES, ROPET_RES)
        for r in range(4):
            dma("sp", ROPE[r * 128:(r + 1) * 128, t0:t0 + 512], ropet[:, r * 512:(r + 1) * 512], ROPET_RES, ["ROPE"])
    GM = NG_L * DEPTH
    memn_v = memn[:].rearrange("p (c t) -> p c t", c=16)
    for pc in norm_pieces(memT_in, 0, N_MEM, GM, lambda c, s0, w: memn_v[:, c, s0:s0 + w], lambda c, s0: ["memn"], pref="M"):
        pc()

    def load_ropet(t0):
        dma("sp", ropet.rearrange("p (r t) -> p r t", r=4),
            ROPE[:, t0:t0 + 512].rearrange("(r p) t -> p r t", p=128), ["ROPE"], ROPET_RES)
    CS2 = ropet[:, 0:512]
    SS2 = ropet[:, 512:1024]
    CSq = ropet[:, 1024:1536]
    SSq = ropet[:, 1536:2048]

    HN = dscr("HN", [D_MODEL, T])
    NSTT = T // 1024
    ain_full = ain_v16
    bgq.extend(norm1024(xT_in, 0, 0))
    bg_flush()

    for l in range(DEPTH):
        G0 = NG_L * l
        x_src = xT_in if l == 0 else XR

        kx_v = kx[:].rearrange("p (h m) -> p h m", h=4)
        vx_v = vx[:].rearrange("p (t c) -> p t c", t=2)

        def cons_kx(col, j, pv, pres):
            evac_copy(kx_v[:, col // 128, :], pv, [pres], ["kx"])
        linear_fm(memn_v, ["memn"], 16, 256, 1, w_xkv_d, l * D_MODEL, 0, 512, cons_kx)

        def cons_vx(tt, c, gw, pv, pres):
            evac_copy(vx_v[:, tt, c - 512:c - 512 + gw], pv, [pres], ["vx"])
        linear_tm(memn_v, ["memn"], 16, 2, w_xkv_d, l * D_MODEL, 512, 512, cons_vx)

        def latent_norm(lat, latres, nch, gcol, dstv, dres, j):
            b = gbank(ALLB)
            pv = ps[:, b, :]
            sl = slice(j * 512, (j + 1) * 512)
            for c in range(nch):
                q = rot("sql", 3)
                P.op("act", lambda e, q=q, c=c: e.activation(out=sql[q][:], in_=lat[:, c, sl], func=AF.Square),
                     reads=latres, writes=[f"sql{q}"])
                P.op("pe", lambda e, q=q, c=c: e.matmul(pv, ones_bf[:], sql[q][:], start=(c == 0), stop=(c == nch - 1)),
                     reads=[f"sql{q}", "ones"], writes=[f"ps{b}"])
            rv, rres = rstd_from_psum(pv, f"ps{b}", nch * 128, 512)
            for c in range(nch):
                P.op("dve", (lambda e, c=c: e.scalar_tensor_tensor(out=dstv[:, c, sl], in0=lat[:, c, sl],
                                                                   scalar=gains[:, gcol + c:gcol + c + 1], in1=rv,
                                                                   op0=ALU.mult, op1=ALU.mult)),
                     latres + [rres, "gains"], dres)

        kvlat = arena[:, 0:4096].rearrange("p (c t) -> p c t", c=4)
        kvn_b = arena[:, 4096:6144].bitcast(BF16).rearrange("p (c t) -> p c t", c=4)
        krA = arena[:, 6144:7168]
        krB = arena[:, 7168:8192]
        rope1 = arena[:, 8192:10240]
        R_KVLAT, R_KVN, R_KRA, R_KRB, R_ROPE1 = ar(0, 4096), ar(4096, 6144), ar(6144, 7168), ar(7168, 8192), ar(8192, 10240)
        CS2, SS2 = rope1[:, 0:1024], rope1[:, 1024:2048]
        for st in range(NSTT):
            t0 = st * 1024
            last = (st == NSTT - 1)
            dma("sp", HN[:, t0:t0 + 1024].rearrange("(c p) t -> p c t", p=128), ain_full, AIN_ALL, ["HN"])
            dma("sp", rope1.rearrange("p (r t) -> p r t", r=2),
                ROPE[0:256, t0:t0 + 1024].rearrange("(r p) t -> p r t", p=128), ["ROPE"], R_ROPE1)

            def cons_in1(col, j, pv, pres, t0=t0, last=last):
                tt0 = t0 + j * 512
                if col < 2048:
                    s = rot("stg", NSTG)
                    evac_copy(stg[s][:], pv, [pres], [f"stg{s}"])
                    dma("sp", KA[col - 1024:col - 1024 + 128, tt0:tt0 + 512], stg[s][:], [f"stg{s}"], ["KA"])
                    if last and j == 1 and CTX:
                        dma("sp", xKAT("own")[col - 1024:col - 1024 + 128, :], stg[s][:], [f"stg{s}"], ["KAT"])
                elif col < 4352:
                    ci = (col - 3840) // 128
                    evac_copy(kvlat[:, ci, j * 512:(j + 1) * 512], pv, [pres], ar(ci * 1024 + j * 512, ci * 1024 + j * 512 + 512))
                elif col < 4480:
                    evac_copy(krA[:, j * 512:(j + 1) * 512], pv, [pres], ar(6144 + j * 512, 6144 + j * 512 + 512))
                else:
                    evac_copy(krB[:, j * 512:(j + 1) * 512], pv, [pres], ar(7168 + j * 512, 7168 + j * 512 + 512))

            linear_fm(ain_full, AIN_ALL, 16, 512, 2, w_in_d, l * D_MODEL, 3840, W_IN_R - 3840, cons_in1)

            def krope_piece(j, t0=t0):
                f0 = rot("tmpf", NTMP)
                f1 = rot("tmpf", NTMP)
                sl = slice(j * 512, (j + 1) * 512)
                P.op("dve", lambda e, f0=f0, sl=sl: e.tensor_tensor(out=tmpf[f0][:], in0=krA[:, sl], in1=CS2[:, sl], op=ALU.mult),
                     R_KRA + R_ROPE1, [f"tmpf{f0}"])
                P.op("dve", lambda e, f1=f1, sl=sl: e.tensor_tensor(out=tmpf[f1][:], in0=krB[:, sl], in1=SS2[:, sl], op=ALU.mult),
                     R_KRB + R_ROPE1, [f"tmpf{f1}"])
                s = rot("stg", NSTG)
                P.op("dve", lambda e, s=s, f0=f0, f1=f1: e.tensor_tensor(out=stg[s][:], in0=tmpf[f0][:], in1=tmpf[f1][:],
                                                                         op=ALU.add),
                     [f"tmpf{f0}", f"tmpf{f1}"], [f"stg{s}"])
                dma("sp", xKPE("own")[:, t0 + j * 512:t0 + (j + 1) * 512], stg[s][:], [f"stg{s}"], ["KPE"])
            for j in range(2):
                bgq.append(("lat", lambda j=j: krope_piece(j)))
                bgq.append(("lat", lambda j=j: latent_norm(kvlat, R_KVLAT, 4, G0 + 54, kvn_b, R_KVN, j)))
            state["grp"] = 0
            linear_fm(ain_full, AIN_ALL, 16, 512, 2, w_in_d, l * D_MODEL, 1024, 1024, cons_in1)

            def cons_va(tt, c, gw, pv, pres, t0=t0, last=last):
                s = rot("stg", NSTG)
                evac_copy(stg[s][:, 0:gw], pv, [pres], [f"stg{s}"])
                r0 = t0 + tt * 128
                dma("sp", VA[r0:r0 + 128, c - 2048:c - 2048 + gw], stg[s][:, 0:gw], [f"stg{s}"], ["VA"])
                if last and tt >= 4 and CTX:
                    dma("sp", xVAT("own")[(tt - 4) * 128:(tt - 3) * 128, c - 2048:c - 2048 + gw], stg[s][:, 0:gw],
                        [f"stg{s}"], ["VAT"])
            linear_tm(ain_full, AIN_ALL, 16, 8, w_in_d, l * D_MODEL, 2048, 1024, cons_va)
            bg_flush_tag("lat")
            if not last:
                bgq.extend(norm1024(x_src, t0 + 1024, G0 + 0))
            else:
                dma("sp", ain_full, HN[:, 0:1024].rearrange("(c p) t -> p c t", p=128), ["HN"], AIN_ALL)
            state["grp"] = 0

            def cons_kn(col, j, pv, pres, t0=t0):
                s = rot("stg", NSTG)
                evac_copy(stg[s][:], pv, [pres], [f"stg{s}"])
                dma("sp", xKN("own", col // 128)[:, t0 + j * 512:t0 + (j + 1) * 512], stg[s][:], [f"stg{s}"], ["KN"])
            linear_fm(kvn_b, R_KVN, 4, 512, 2, w_ukv_d, l * KV_LORA, 0, 1024, cons_kn)

            def cons_vb(tt, c, gw, pv, pres, t0=t0):
                s = rot("stg", NSTG)
                evac_copy(stg[s][:, 0:gw], pv, [pres], [f"stg{s}"])
                r0 = t0 + tt * 128
                dma("sp", xVB("own", r0, 128)[:, c - 1024:c - 1024 + gw], stg[s][:, 0:gw], [f"stg{s}"], ["VB"])
            linear_tm(kvn_b, R_KVN, 4, 8, w_ukv_d, l * KV_LORA, 1024, 1024, cons_vb)
            bg_flush()
        if CTX:
            groups_ = [[2 * i, 2 * i + 1] for i in range(cfg.n_cores // 2)]
            def issue_cc(ci):
                P.op("pool", lambda e, ci=ci: e.collective_compute("AllGather", ALU.bypass, replica_groups=groups_,
                                                                   ins=[S_t[ci]], outs=[G_t[ci]]),
                     ["KN", "VB", "KPE", "KAT", "VAT"], ["GATH"], dma="cc")
            issue_cc(0)
            for ci in range(1, len(chunks)):
                pending_cc.append(lambda ci=ci: issue_cc(ci))

        qlat = arena[:, 0:6144].rearrange("p (c t) -> p c t", c=6)
        qn_b = arena[:, 6144:9216].bitcast(BF16).rearrange("p (c t) -> p c t", c=6)
        rope2 = arena[:, 9216:11264]
        R_QLAT, R_QN, R_ROPE2 = ar(0, 6144), ar(6144, 9216), ar(9216, 11264)
        CSq, SSq = rope2[:, 0:1024], rope2[:, 1024:2048]
        for st in range(NSTT):
            t0 = st * 1024
            last = (st == NSTT - 1)
            dma("sp", rope2.rearrange("p (r t) -> p r t", r=2),
                ROPE[256:512, t0:t0 + 1024].rearrange("(r p) t -> p r t", p=128), ["ROPE"], R_ROPE2)

            def cons_in2(col, j, pv, pres, t0=t0):
                tt0 = t0 + j * 512
                if col < 1024:
                    s = rot("stg", NSTG)
                    evac_copy(stg[s][:], pv, [pres], [f"stg{s}"], scale=QSC_A)
                    dma("sp", QA[col:col + 128, tt0:tt0 + 512], stg[s][:], [f"stg{s}"], ["QA"])
                else:
                    ci = (col - 3072) // 128
                    evac_copy(qlat[:, ci, j * 512:(j + 1) * 512], pv, [pres], ar(ci * 1024 + j * 512, ci * 1024 + j * 512 + 512))
            linear_fm(ain_full, AIN_ALL, 16, 512, 2, w_in_d, l * D_MODEL, 3072, 768, cons_in2)
            for j in range(2):
                bgq.append(("lat", lambda j=j: latent_norm(qlat, R_QLAT, 6, G0 + 48, qn_b, R_QN, j)))
            state["grp"] = 0
            linear_fm(ain_full, AIN_ALL, 16, 512, 2, w_in_d, l * D_MODEL, 0, 1024, cons_in2)
            bg_flush_tag("lat")
            if not last:
                dma("sp", ain_full, HN[:, t0 + 1024:t0 + 2048].rearrange("(c p) t -> p c t", p=128), ["HN"], AIN_ALL)
            qra_slot = {}

            def cons_uq(col, j, pv, pres, t0=t0):
                tt0 = t0 + j * 512
                sl = slice(j * 512, (j + 1) * 512)
                if col < 1024:
                    s = rot("stg", NSTG)
                    evac_copy(stg[s][:], pv, [pres], [f"stg{s}"], scale=QSC_B)
                    dma("sp", QN[col:col + 128, tt0:tt0 + 512], stg[s][:], [f"stg{s}"], ["QN"])
                    return
                p_ = (col - 1024) // 256
                if ((col - 1024) // 128) % 2 == 0:
                    fa = rot("tmpf", NTMP)
                    qra_slot[(p_, j)] = fa
                    P.op("dve", lambda e, fa=fa, pv=pv: e.tensor_tensor(out=tmpf[fa][:], in0=pv, in1=CSq[:, sl], op=ALU.mult),
                         [pres] + R_ROPE2, [f"tmpf{fa}"])
                else:
                    fa = qra_slot.pop((p_, j))
                    f0 = rot("tmpf", NTMP)
                    P.op("dve", lambda e, f0=f0, pv=pv: e.tensor_tensor(out=tmpf[f0][:], in0=pv, in1=SSq[:, sl], op=ALU.mult),
                         [pres] + R_ROPE2, [f"tmpf{f0}"])
                    s = rot("stg", NSTG)
                    P.op("dve", lambda e, s=s, f0=f0, fa=fa: e.tensor_tensor(out=stg[s][:], in0=tmpf[f0][:],
                                                                             in1=tmpf[fa][:], op=ALU.add),
                         [f"tmpf{f0}", f"tmpf{fa}"], [f"stg{s}"])
                    dma("sp", QPE[p_ * 128:(p_ + 1) * 128, tt0:tt0 + 512], stg[s][:], [f"stg{s}"], ["QPE"])
            linear_fm(qn_b, R_QN, 6, 512, 2, w_uq_d, l * Q_LORA, 0, 2048, cons_uq)
            bg_flush()

        while pending_cc:
            pending_cc.pop(0)()
        NKA = (CA + T) // 128
        NC4 = CA // 128
        ebf = arena[:, 0:640]
        R_EBF = ar(0, 640)
        OT_ROW = lambda r: [f"OT{r}_{i}" for i in range(NST)]
        A0 = 1024
        bsets = [
            dict(q=ain[:, 0:T], k=ain[:, 4096:4096 + CA + T],
                 v=ain[:, 8192:8192 + NKA * 128].rearrange("p (k d) -> p k d", d=128), o=ain[:, 12288:12288 + T],
                 rq=ainq(0), rk=ainq(1), rv=ainq(2), ro=ainq(3)),
            dict(q=arena[:, A0:A0 + T // 2].bitcast(BF16), k=arena[:, A0 + 2048:A0 + 2048 + (CA + T) // 2].bitcast(BF16),
                 v=arena[:, A0 + 4096:A0 + 4096 + NKA * 64].bitcast(BF16).rearrange("p (k d) -> p k d", d=128),
                 o=arena[:, A0 + 6144:A0 + 6144 + T // 2].bitcast(BF16),
                 rq=ar(A0, A0 + 2048), rk=ar(A0 + 2048, A0 + 4096), rv=ar(A0 + 4096, A0 + 6144), ro=ar(A0 + 6144, A0 + 8192)),
        ]

        def b_load(h):
            S_ = bsets[h % 2]
            dma("sp", S_["q"], QA[h * 128:(h + 1) * 128, :], ["QA"], S_["rq"])
            if CA:
                dma("sp", S_["k"][:, 0:CA], xKAT("ctx")[h * 128:(h + 1) * 128, :], ["GATH"], S_["rk"])
                dma("sp", S_["v"][:, 0:NC4, :], xVAT("ctx")[:, h * 128:(h + 1) * 128].rearrange("(k p) d -> p k d", p=128),
                    ["GATH"], S_["rv"])
            dma("sp", S_["k"][:, CA:CA + T], KA[h * 128:(h + 1) * 128, :], ["KA"], S_["rk"])
            dma("sp", S_["v"][:, NC4:NKA, :], VA[:, h * 128:(h + 1) * 128].rearrange("(k p) d -> p k d", p=128),
                ["VA"], S_["rv"])
            if CA:
                P.op("dve", lambda e, S_=S_: e.tensor_scalar(out=S_["v"][:, 0:NC4, :], in0=S_["v"][:, 0:NC4, :],
                                                            scalar1=flag[:, 0:1], scalar2=None, op0=ALU.mult),
                     S_["rv"] + ["flag"], S_["rv"])
            eb = ebias[h % 2]
            r0 = (l * 8 + h) * 128
            dma("sp", ebf, biasT_in[r0:r0 + 128, :], [], R_EBF)
            P.op("act", lambda e: e.activation(out=ebf, in_=ebf, func=AF.Exp), R_EBF, R_EBF)
            P.op("dve", lambda e, eb=eb: e.tensor_tensor(out=eb[:], in0=ebf, in1=mask01[:], op=ALU.mult),
                 R_EBF + ["mask01"], [f"ebias{h % 2}"])

        DEPTH_P = 2
        b_load(0)
        items = []
        for h in range(A_HEADS):
            for j in range(T // 128):
                items.append((h, j))
        binfo = {}

        def b_qk(idx):
            h, j = items[idx]
            S_ = bsets[h % 2]
            if j == 2 * DEPTH_P and h + 1 < A_HEADS:
                b_load(h + 1)
            qv = S_["q"][:, j * 128:(j + 1) * 128]
            kts = [kt for kt in range(5) if j + NC4 - 4 + kt >= 0]
            pair = rot("apair", 2)
            bA, bB = 2 * pair, 2 * pair + 1
            for kt in kts:
                g = j + NC4 - 4 + kt
                pv = ps[:, bA, kt * 128:(kt + 1) * 128] if kt < 4 else ps[:, bB, 0:128]
                pres = f"ps{bA}" if kt < 4 else f"ps{bB}"
                P.op("pe", lambda e, pv=pv, g=g, qv=qv, S_=S_: e.matmul(pv, S_["k"][:, g * 128:(g + 1) * 128], qv,
                                                                        start=True, stop=True),
                     S_["rk"] + S_["rq"], [pres])
            pi = rot("pt", NPT)
            lo = kts[0] * 128
            eb = ebias[h % 2]
            if lo < 512:
                P.op("act", lambda e, pi=pi, lo=lo, bA=bA:
                     e.activation(out=pt[pi][:, lo:512], in_=ps[:, bA, lo:512], func=AF.Exp),
                     [f"ps{bA}"], [f"pt{pi}"])
            P.op("act", lambda e, pi=pi, bB=bB: e.activation(out=pt[pi][:, 512:640], in_=ps[:, bB, 0:128],
                                                             func=AF.Exp), [f"ps{bB}"], [f"pt{pi}"])
            P.op("dve", lambda e, pi=pi, lo=lo, eb=eb: e.tensor_tensor(out=pt[pi][:, lo:640], in0=pt[pi][:, lo:640],
                                                                       in1=eb[:, lo:640], op=ALU.mult),
                 [f"pt{pi}", f"ebias{h % 2}"], [f"pt{pi}"])
            binfo[idx] = (pi, kts)

        def b_pv(idx):
            h, j = items[idx]
            S_ = bsets[h % 2]
            pi, kts = binfo.pop(idx)
            jg, jj = j // 4, j % 4
            par = (h * (T // 512) + jg) % 2
            bo, bd = 4 + par, 6 + par
            for i_, kt in enumerate(kts):
                g = j + NC4 - 4 + kt
                isctx = g < NC4
                pcol = pt[pi][:, kt * 128:(kt + 1) * 128]
                P.op("pe", lambda e, g=g, pcol=pcol, i_=i_, n=len(kts), S_=S_:
                     e.matmul(ps[:, bo, jj * 128:(jj + 1) * 128], S_["v"][:, g, :], pcol,
                              start=(i_ == 0), stop=(i_ == n - 1)),
                     S_["rv"] + [f"pt{pi}"], [f"ps{bo}"])
                P.op("pe", lambda e, pcol=pcol, i_=i_, n=len(kts), isctx=isctx:
                     e.matmul(ps[:, bd, jj * 128:(jj + 1) * 128], (flagones if isctx else ones_bf)[:], pcol,
                              start=(i_ == 0), stop=(i_ == n - 1)),
                     ["ones", "flagones", f"pt{pi}"], [f"ps{bd}"])
            if jj == 3:
                f0 = rot("tmpf", NTMP)
                P.op("act", lambda e, f0=f0: e.activation(out=tmpf[f0][:], in_=ps[:, bd, :], func=AF.Ln),
                     [f"ps{bd}"], [f"tmpf{f0}"])
                P.op("act", lambda e, f0=f0: e.activation(out=tmpf[f0][:], in_=tmpf[f0][:], func=AF.Exp, scale=-1.0),
                     [f"tmpf{f0}"], [f"tmpf{f0}"])
                P.op("dve", lambda e, f0=f0, S_=S_: e.tensor_tensor(out=S_["o"][:, jg * 512:(jg + 1) * 512],
                                                                    in0=ps[:, bo, :], in1=tmpf[f0][:], op=ALU.mult),
                     [f"ps{bo}", f"tmpf{f0}"], S_["ro"])
                if jg == T // 512 - 1:
                    dma("sp", OT[h * 128:(h + 1) * 128, :], S_["o"], S_["ro"], OT_ROW(h))

        for step in range(len(items) + DEPTH_P):
            if step < len(items):
                b_qk(step)
            if step >= DEPTH_P:
                b_pv(step - DEPTH_P)

        NKB = (CTX + T) // 128
        NCC = CTX // 128
        SK = CTX + T
        kpeA = ain[:, 0:SK]
        kpeB = ain[:, 4096:4096 + SK]
        C0 = 1024
        csets = [
            dict(k=ain[:, 8192:8192 + SK], v=ain[:, 12288:12288 + NKB * 128].rearrange("p (k d) -> p k d", d=128),
                 rk=ainq(2), rv=ainq(3)),
            dict(k=arena[:, C0:C0 + SK // 2].bitcast(BF16),
                 v=arena[:, C0 + 2048:C0 + 2048 + NKB * 64].bitcast(BF16).rearrange("p (k d) -> p k d", d=128),
                 rk=ar(C0, C0 + 2048), rv=ar(C0 + 2048, C0 + 4096)),
        ]
        qnt = [arena[:, 0:256].bitcast(BF16), arena[:, 256:512].bitcast(BF16)]
        qpt = [arena[:, 512:768].bitcast(BF16), arena[:, 768:1024].bitcast(BF16)]
        R_QNT = [["ar0q0"], ["ar0q1"]]
        R_QPT = [["ar1q0"], ["ar1q1"]]
        kpe2 = arena[:, C0 + 4096:C0 + 4096 + SK // 2].bitcast(BF16)
        R_KPE2 = ar(C0 + 4096, C0 + 4096 + SK // 2)
        R_C_AR = ar(0, 1024)
        SUBN = R_QNT[0] + R_QNT[1] + R_QPT[0] + R_QPT[1]
        P.op("dve", lambda e: e.memset(arena[:, 0:8], 0.0), [], R_C_AR + SUBN)
        if CTX:
            dma("sp", kpe2[:, 0:CTX], xKPE("ctx"), ["GATH"], R_KPE2)
        dma("sp", kpe2[:, CTX:SK], xKPE("own"), ["KPE"], R_KPE2)
        P.op("dve", lambda e: e.tensor_scalar(out=kpeA, in0=kpe2, scalar1=consts[:, 2:3], scalar2=None, op0=ALU.mult),
             R_KPE2 + ["consts"], ainq(0))
        P.op("dve", lambda e: e.tensor_scalar(out=kpeB, in0=kpe2, scalar1=consts[:, 3:4], scalar2=None, op0=ALU.mult),
             R_KPE2 + ["consts"], ainq(1))

        def c_load(h):
            S_ = csets[h % 2]
            if CTX:
                dma("sp", S_["k"][:, 0:CTX], xKN("ctx", h), ["GATH"], S_["rk"])
                for t_ in range(0, T, 1024):
                    dma("sp", S_["v"][:, t_ // 128:t_ // 128 + 8, :],
                        xVB("ctx", t_, 1024)[:, h * 128:(h + 1) * 128].rearrange("(k p) d -> p k d", p=128),
                        ["GATH"], S_["rv"])
            dma("sp", S_["k"][:, CTX:SK], xKN("own", h), ["KN"], S_["rk"])
            for t_ in range(0, T, 1024):
                dma("sp", S_["v"][:, NCC + t_ // 128:NCC + t_ // 128 + 8, :],
                    xVB("own", t_, 1024)[:, h * 128:(h + 1) * 128].rearrange("(k p) d -> p k d", p=128),
                    ["VB"], S_["rv"])
            if CTX:
                P.op("dve", lambda e, S_=S_: e.tensor_scalar(out=S_["v"][:, 0:NCC, :], in0=S_["v"][:, 0:NCC, :],
                                                            scalar1=flag[:, 0:1], scalar2=None, op0=ALU.mult),
                     S_["rv"] + ["flag"], S_["rv"])

        def c_qload(h, i):
            qi = (h * NST + i) % 2
            dma("sp", qnt[qi], QN[h * 128:(h + 1) * 128, i * 512:(i + 1) * 512], ["QN"], R_QNT[qi])
            dma("sp", qpt[qi], QPE[(h // 2) * 128:(h // 2 + 1) * 128, i * 512:(i + 1) * 512], ["QPE"], R_QPT[qi])

        c_load(0)
        c_qload(0, 0)
        citems = []
        for h in range(B_HEADS):
            for i in range(NST):
                ktl = list(range(NCC)) + [NCC + o for o in range(4 * i + 4)]
                for i_, g in enumerate(ktl):
                    citems.append((h, i, g, i_, len(ktl)))
        cinfo = {}

        def c_qk(idx):
            h, i, g, i_, n = citems[idx]
            S_ = csets[h % 2]
            if i_ == 0:
                if i + 1 < NST:
                    c_qload(h, i + 1)
                elif h + 1 < B_HEADS:
                    c_qload(h + 1, 0)
            if i == 0 and i_ == DEPTH_P + 1 and h + 1 < B_HEADS:
                c_load(h + 1)
            qi = (h * NST + i) % 2
            kpeX, kpres = (kpeA, ainq(0)) if h % 2 == 0 else (kpeB, ainq(1))
            b = gbank(LOWB)
            P.op("pe", lambda e, b=b, g=g, qi=qi, S_=S_: e.matmul(ps[:, b, :], S_["k"][:, g * 128:(g + 1) * 128], qnt[qi],
                                                                  start=True, stop=False),
                 S_["rk"] + R_QNT[qi], [f"ps{b}"])
            P.op("pe", lambda e, b=b, g=g, qi=qi, kpeX=kpeX: e.matmul(ps[:, b, :], kpeX[:, g * 128:(g + 1) * 128],
                                                                      qpt[qi], start=False, stop=True),
                 kpres + R_QPT[qi], [f"ps{b}"])
            pi = rot("pt", NPT)
            pv = pt[pi][:, 0:512]
            P.op("act", lambda e, pv=pv, b=b: e.activation(out=pv, in_=ps[:, b, :], func=AF.Exp),
                 [f"ps{b}"], [f"pt{pi}"])
            own = g - NCC
            if own >= 4 * i:
                r = own - 4 * i
                P.op("dve", lambda e, pv=pv, r=r: e.tensor_tensor(out=pv, in0=pv, in1=mlamask[:, r * 512:(r + 1) * 512],
                                                                  op=ALU.mult),
                     [f"pt{pi}", "mlamask"], [f"pt{pi}"])
            cinfo[idx] = pi

        def c_pv(idx):
            h, i, g, i_, n = citems[idx]
            S_ = csets[h % 2]
            pi = cinfo.pop(idx)
            pv = pt[pi][:, 0:512]
            qi = (h * NST + i) % 2
            bo, bd = 4 + qi, 6 + qi
            isctx = g < NCC
            P.op("pe", lambda e, g=g, pv=pv, S_=S_: e.matmul(ps[:, bo, :], S_["v"][:, g, :], pv,
                                                             start=(i_ == 0), stop=(i_ == n - 1)),
                 S_["rv"] + [f"pt{pi}"], [f"ps{bo}"])
            P.op("pe", lambda e, pv=pv: e.matmul(ps[:, bd, :], (flagones if isctx else ones_bf)[:], pv,
                                                 start=(i_ == 0), stop=(i_ == n - 1)),
                 ["ones", "flagones", f"pt{pi}"], [f"ps{bd}"])
            if i_ == n - 1:
                f0 = rot("tmpf", NTMP)
                P.op("act", lambda e, f0=f0: e.activation(out=tmpf[f0][:], in_=ps[:, bd, :], func=AF.Ln),
                     [f"ps{bd}"], [f"tmpf{f0}"])
                P.op("act", lambda e, f0=f0: e.activation(out=tmpf[f0][:], in_=tmpf[f0][:], func=AF.Exp, scale=-1.0),
                     [f"tmpf{f0}"], [f"tmpf{f0}"])
                s = rot("stg", NSTG)
                P.op("dve", lambda e, f0=f0, s=s: e.tensor_tensor(out=stg[s][:], in0=ps[:, bo, :], in1=tmpf[f0][:],
                                                                  op=ALU.mult),
                     [f"ps{bo}", f"tmpf{f0}"], [f"stg{s}"])
                dma("sp", OT[(8 + h) * 128:(9 + h) * 128, i * 512:(i + 1) * 512], stg[s][:], [f"stg{s}"], [f"OT{8 + h}_{i}"])

        for step in range(len(citems) + DEPTH_P):
            if step < len(citems):
                c_qk(step)
            if step >= DEPTH_P:
                c_pv(step - DEPTH_P)
        P.op("dve", lambda e: e.memset(arena[:, 0:8], 0.0), SUBN, R_C_AR)

        def make_resid(x_read, t0):
            slots = {}

            def pre(col, j):
                xr = rot("xres", NXR)
                tt0 = t0 + j * 512
                nm = f"X{col // 128}_{tt0 // 512}"
                dma("sp", xres[xr][:], x_read[col:col + 128, tt0:tt0 + 512], [nm], [f"xres{xr}"])
                slots[(col, j)] = xr

            def cons(col, j, pv, pres):
                xr = slots.pop((col, j))
                tt0 = t0 + j * 512
                nm = f"X{col // 128}_{tt0 // 512}"
                P.op("dve", lambda e, xr=xr, pv=pv: e.tensor_tensor(out=xres[xr][:], in0=pv, in1=xres[xr][:], op=ALU.add),
                     [pres, f"xres{xr}"], [f"xres{xr}"])
                dma("sp", XR[col:col + 128, tt0:tt0 + 512], xres[xr][:], [f"xres{xr}"], [nm])
            return cons, pre

        ain2 = arena[:, 0:8192].bitcast(BF16).rearrange("p (c t) -> p c t", c=16)
        dbufs = [(ain_full, AIN_ALL), (ain2, ar(0, 8192))]

        def d_load(st, k):
            dma("sp", dbufs[k][0], OT[:, st * 1024:(st + 1) * 1024].rearrange("(c p) t -> p c t", p=128),
                [f"OT{c}_{i}" for c in range(16) for i in (2 * st, 2 * st + 1)], dbufs[k][1])
        d_load(0, 0)
        e0_queued = False
        for st in range(NSTT):
            t0 = st * 1024
            k = st % 2
            if st + 1 < NSTT:
                d_load(st + 1, 1 - k)
            elif k == 1:
                bgq.extend(norm1024(XR, 0, G0 + 16))
                e0_queued = True
                state["grp"] = 0
            cons, pre = make_resid(x_src, t0)
            linear_fm(dbufs[k][0], dbufs[k][1], 16, 512, 2, w_out_d, l * D_MODEL, 0, D_MODEL, cons, pre=pre)
            bg_flush()
        if not e0_queued:
            bgq.extend(norm1024(XR, 0, G0 + 16))
            bg_flush()

        qx = arena[:, 8192:10240].bitcast(BF16).rearrange("p (h t) -> p h t", h=4)
        ox = arena[:, 0:2048].bitcast(BF16).rearrange("p (h t) -> p h t", h=4)
        R_QX, R_OX = ar(8192, 10240), ar(0, 2048)
        for st in range(NSTT):
            t0 = st * 1024

            def cons_qx(col, j, pv, pres):
                evac_copy(qx[:, col // 128, j * 512:(j + 1) * 512], pv, [pres], R_QX, scale=QSC_A)
            linear_fm(ain_full, AIN_ALL, 16, 512, 2, w_xq_d, l * D_MODEL, 0, 512, cons_qx, banks=LOWB)
            defer = None
            if st + 1 < NSTT:
                bgq.extend(norm1024(XR, t0 + 1024, G0 + 16))
            elif NSTT > 1:
                bgq.extend(norm1024(XR, 0, G0 + 32))
            else:
                defer = norm1024(XR, 0, G0 + 32)
            state["grp"] = 0
            xinfo = {}

            def x_qk(idx):
                j, h, mt = idx // 8, (idx // 2) % 4, idx % 2
                b = gbank(LOWB)
                P.op("pe", lambda e, b=b, h=h, mt=mt, j=j: e.matmul(ps[:, b, :], kx_v[:, h, mt * 128:(mt + 1) * 128],
                                                                    qx[:, h, j * 512:(j + 1) * 512], start=True, stop=True),
                     ["kx"] + R_QX, [f"ps{b}"])
                pi = rot("pt", NPT)
                pv = pt[pi][:, 0:512]
                P.op("act", lambda e, pv=pv, b=b: e.activation(out=pv, in_=ps[:, b, :], func=AF.Exp),
                     [f"ps{b}"], [f"pt{pi}"])
                xinfo[idx] = pi

            def x_pv(idx):
                j, h, mt = idx // 8, (idx // 2) % 4, idx % 2
                pi = xinfo.pop(idx)
                pv = pt[pi][:, 0:512]
                par = (j * 4 + h) % 2
                bo, bd = 4 + par, 6 + par
                P.op("pe", lambda e, pv=pv, h=h, mt=mt: e.matmul(ps[:, bo, :], vx_v[:, mt, h * 128:(h + 1) * 128],
                                                                 pv, start=(mt == 0), stop=(mt == 1)),
                     ["vx", f"pt{pi}"], [f"ps{bo}"])
                P.op("pe", lambda e, pv=pv, mt=mt: e.matmul(ps[:, bd, :], ones_bf[:], pv, start=(mt == 0), stop=(mt == 1)),
                     ["ones", f"pt{pi}"], [f"ps{bd}"])
                if mt == 1:
                    f0 = rot("tmpf", NTMP)
                    P.op("act", lambda e, f0=f0: e.activation(out=tmpf[f0][:], in_=ps[:, bd, :], func=AF.Ln),
                         [f"ps{bd}"], [f"tmpf{f0}"])
                    P.op("act", lambda e, f0=f0: e.activation(out=tmpf[f0][:], in_=tmpf[f0][:], func=AF.Exp, scale=-1.0),
                         [f"tmpf{f0}"], [f"tmpf{f0}"])
                    P.op("dve", lambda e, f0=f0, h=h, j=j: e.tensor_tensor(out=ox[:, h, j * 512:(j + 1) * 512], in0=ps[:, bo, :],
                                                                           in1=tmpf[f0][:], op=ALU.mult),
                         [f"ps{bo}", f"tmpf{f0}"], R_OX)
            for step in range(16 + DEPTH_P):
                if step < 16:
                    x_qk(step)
                if step >= DEPTH_P:
                    x_pv(step - DEPTH_P)
            cons, pre = make_resid(XR, t0)
            linear_fm(ox, R_OX, 4, 512, 2, w_xo_d, l * 512, 0, D_MODEL, cons, pre=pre, PF=6)
            if defer:
                bgq.extend(defer)
            bg_flush()

        NFH = D_FF // 2 // 128
        actv = arena[:, 0:11264].bitcast(BF16).rearrange("p (c t) -> p c t", c=NFH)
        R_ACT = ar(0, 11264)
        NF = T // 1024
        deferF = None
        for st in range(NF):
            t0 = st * 1024
            for half in range(2):
                c_base = half * (D_FF // 2)
                for (c, gw) in wgroups(16, c_base, D_FF // 2, cap=384):
                    sg, wg = load_w(w_gate_d, l * D_MODEL, 16, c, gw)
                    su, wu = load_w(w_up_d, l * D_MODEL, 16, c, gw)
                    for f in range(0, gw, 128):
                        fc = (c + f - c_base) // 128
                        for j in range(2):
                            bg = gbank(ALLB)
                            bu = gbank(ALLB)
                            for k in range(16):
                                P.op("pe", lambda e, bg=bg, wg=wg, k=k, f=f, j=j:
                                     e.matmul(ps[:, bg, :], wg[:, k, f:f + 128], ain_v16[:, k, j * 512:(j + 1) * 512],
                                              start=(k == 0), stop=(k == 15)),
                                     [f"w{sg}"] + AIN_H[j], [f"ps{bg}"])
                            for k in range(16):
                                P.op("pe", lambda e, bu=bu, wu=wu, k=k, f=f, j=j:
                                     e.matmul(ps[:, bu, :], wu[:, k, f:f + 128], ain_v16[:, k, j * 512:(j + 1) * 512],
                                              start=(k == 0), stop=(k == 15)),
                                     [f"w{su}"] + AIN_H[j], [f"ps{bu}"])
                            f0 = rot("tmpf", NTMP)
                            P.op("act", lambda e, f0=f0, bg=bg: e.activation(out=tmpf[f0][:], in_=ps[:, bg, :], func=AF.Silu),
                                 [f"ps{bg}"], [f"tmpf{f0}"])
                            P.op("dve", lambda e, f0=f0, bu=bu, fc=fc, j=j:
                                 e.tensor_tensor(out=actv[:, fc, j * 512:(j + 1) * 512], in0=ps[:, bu, :], in1=tmpf[f0][:],
                                                 op=ALU.mult),
                                 [f"ps{bu}", f"tmpf{f0}"], [f"ar{fc}"])
                if half == 1:
                    if st + 1 < NF:
                        bgq.extend(norm1024(XR, t0 + 1024, G0 + 32))
                    elif l + 1 < DEPTH:
                        if NF > 1:
                            bgq.extend(norm1024(XR, 0, NG_L * (l + 1)))
                        else:
                            deferF = norm1024(XR, 0, NG_L * (l + 1))
                    state["grp"] = 0
                cons, pre = make_resid(XR, t0)
                linear_fm(actv, R_ACT, NFH, 512, 2, w_down_d, l * D_FF + c_base, 0, D_MODEL, cons, pre=pre)
            if deferF:
                bgq.extend(deferF)
                deferF = None
            bg_flush()

    GF = NG_L * DEPTH + 16
    for t0 in range(0, T, 512):
        for pc in norm_pieces(XR, t0, 512, GF, None, None, out_f32_dma=outT):
            pc()

    P.emit(final_waits=final_ops)
    P.close()
    return nc, P


def _fm_cols(g):
    return np.ascontiguousarray(g.reshape(-1, 128).T)


def prepare_shared(inputs, depth, layer0=0):
    f32 = np.float32
    L = slice(layer0, layer0 + depth)
    w_in = np.asarray(inputs["w_in"])[L]
    kr = w_in[:, :, 4352:4416]
    kr_sw = np.concatenate([kr[:, :, 32:], kr[:, :, :32]], axis=-1)
    w_in_r = np.concatenate([w_in[:, :, :4352], kr, kr, kr_sw, kr_sw], axis=-1)
    w_uq = np.asarray(inputs["w_uq"])[L].reshape(depth, Q_LORA, 8, 192)
    nope = w_uq[..., :128].reshape(depth, Q_LORA, 1024)
    rope = w_uq[..., 128:]
    rope_sw = np.concatenate([rope[..., 32:], rope[..., :32]], axis=-1)
    ra = rope.reshape(depth, Q_LORA, 4, 128)
    rb_ = rope_sw.reshape(depth, Q_LORA, 4, 128)
    w_uq_r = np.concatenate([nope, np.stack([ra, rb_], axis=3).reshape(depth, Q_LORA, 1024)], axis=-1)
    w_ukv = np.asarray(inputs["w_ukv"])[L].reshape(depth, KV_LORA, 8, 256)
    w_ukv_r = np.concatenate([w_ukv[..., :128].reshape(depth, KV_LORA, 1024),
                              w_ukv[..., 128:].reshape(depth, KV_LORA, 1024)], axis=-1)
    gcols = []
    for l in range(layer0, layer0 + depth):
        gcols += [_fm_cols(np.asarray(inputs["norm_mix"])[l]), _fm_cols(np.asarray(inputs["norm_mem"])[l]),
                  _fm_cols(np.asarray(inputs["norm_ffn"])[l]), _fm_cols(np.asarray(inputs["q_norm"])[l]),
                  _fm_cols(np.asarray(inputs["kv_norm"])[l])]
    gcols += [_fm_cols(np.asarray(inputs["mem_norm"])), _fm_cols(np.asarray(inputs["norm_final"]))]
    gains = np.concatenate(gcols, axis=1).astype(f32)
    kk = np.arange(640)[:, None]
    qq = np.arange(128)[None, :]
    rel = np.clip(512 + qq - kk, -REL_CLIP, REL_CLIP) + REL_CLIP
    rb = np.asarray(inputs["rel_bias"])[L]
    bt = rb[:, :, rel]
    bt = bt.reshape(depth, 8, 5, 128, 128).transpose(0, 1, 3, 2, 4).reshape(depth * 8 * 128, 640)
    cq = 8 + qq // 64
    ck = kk // 64
    m01 = ((ck >= cq - 8) & (ck <= cq)).astype(f32)
    m01 = m01.reshape(5, 128, 128).transpose(1, 0, 2).reshape(128, 640)
    kq = (np.arange(128)[:, None] // 64)
    qc = (np.arange(512)[None, :] // 64)
    mm = [((2 * r + kq) <= qc).astype(f32) for r in range(4)]
    mlamask = np.concatenate(mm, axis=1)
    half = 32
    inv = (ROPE_THETA ** (-np.arange(half, dtype=f32) / half)).astype(f32)
    consts = np.zeros((128, 8), f32)
    consts[:, 0] = np.tile(inv, 4)
    consts[:, 1] = np.tile(np.concatenate([-np.ones(32, f32), np.ones(32, f32)]), 2)
    consts[:64, 2] = 1.0
    consts[64:, 3] = 1.0
    consts[:, 4] = -math.pi
    consts[:, 5] = EPS

    def flat(a):
        a = np.asarray(a)
        return np.ascontiguousarray(a.reshape(-1, a.shape[-1]), dtype=f32)

    return {
        "consts": consts, "gains": gains, "mask01": np.ascontiguousarray(m01), "mlamask": np.ascontiguousarray(mlamask),
        "biasT": np.ascontiguousarray(bt, dtype=f32),
        "w_in": flat(w_in_r), "w_uq": flat(w_uq_r), "w_ukv": flat(w_ukv_r),
        "w_out": flat(np.asarray(inputs["w_out"])[L]), "w_xq": flat(np.asarray(inputs["w_xq"])[L]),
        "w_xkv": flat(np.asarray(inputs["w_xkv"])[L]), "w_xo": flat(np.asarray(inputs["w_xo"])[L]),
        "w_gate": flat(np.asarray(inputs["w_gate"])[L]), "w_up": flat(np.asarray(inputs["w_up"])[L]),
        "w_down": flat(np.asarray(inputs["w_down"])[L]),
    }


_CACHE = {}


def kernel(**inputs):
    x = np.asarray(inputs["x"])
    mem = np.asarray(inputs["mem"])
    pos = np.asarray(inputs["positions"])
    B, S, D = x.shape
    NH = 2
    T = S // NH
    n_cores = B * NH
    depth = int(np.asarray(inputs["w_in"]).shape[0])
    cfg = Cfg(T=T, CTX=T, depth=depth, n_cores=n_cores)
    key = (T, T, depth, n_cores)
    if key not in _CACHE:
        _CACHE[key] = build_program(cfg)[0]
    nc = _CACHE[key]
    shared = prepare_shared(inputs, depth)
    in_maps = []
    for c in range(n_cores):
        b, hf = c // NH, c % NH
        m = dict(shared)
        m["xT"] = np.ascontiguousarray(x[b, hf * T:(hf + 1) * T].T)
        m["memT"] = np.ascontiguousarray(mem[b].T)
        m["pos"] = np.ascontiguousarray(pos[b:b + 1, hf * T:(hf + 1) * T]).astype(np.int32)
        m["flag"] = np.full((128, 1), 1.0 if hf > 0 else 0.0, np.float32)
        in_maps.append(m)
    res = run_bass_kernel_spmd(nc, in_maps, core_ids=list(range(n_cores)))
    out = np.empty((B, S, D), np.float32)
    for c in range(n_cores):
        b, hf = c // NH, c % NH
        out[b, hf * T:(hf + 1) * T] = res.results[c]["outT"].T
    return out
```

```python
import contextlib
import math
import numpy as np
import concourse.bass as bass
import concourse.mybir as mybir
from concourse.bass_utils import run_bass_kernel_spmd

F32 = mybir.dt.float32
BF16 = mybir.dt.bfloat16
I32 = mybir.dt.int32
AF = mybir.ActivationFunctionType
ALU = mybir.AluOpType

ENGS = ("pe", "act", "dve", "pool", "sp")

D_MODEL = 2048
CHUNK = 64
A_HEADS = 8
B_HEADS = 8
Q_LORA = 768
KV_LORA = 512
N_MEM = 256
X_HEADS = 4
D_FF = 5632
EPS = 1e-6
REL_CLIP = 128
ROPE_THETA = 10000.0
W_IN_R = 4608
NG_L = 58


class Op:
    __slots__ = ("eng", "fn", "deps", "is_dma", "sig", "sem", "val", "pre_wait", "idx", "inc")

    def __init__(self, eng, fn, is_dma):
        self.eng = eng
        self.fn = fn
        self.is_dma = bool(is_dma)
        self.inc = 1 if is_dma == "cc" else 16
        self.deps = []
        self.sig = False
        self.sem = None
        self.val = 0
        self.pre_wait = None


class Prog:
    def __init__(self, nc, sem_wrap=30000):
        self.nc = nc
        self.ops = {e: [] for e in ENGS}
        self.last_w = {}
        self.readers = {}
        self.n_dma_sems = {"sp": 28, "pool": 20, "act": 8, "dve": 4, "pe": 4}
        self.sem_wrap = sem_wrap
        self.stack = contextlib.ExitStack()

    def sbuf(self, name, shape, dtype):
        return self.stack.enter_context(self.nc.sbuf_tensor("sb_" + name, list(shape), dtype))

    def psum(self, name, shape, dtype):
        return self.stack.enter_context(self.nc.psum_tensor(name, list(shape), dtype))

    def op(self, eng, fn, reads=(), writes=(), dma=False):
        o = Op(eng, fn, dma)
        o.idx = len(self.ops[eng])
        lw = self.last_w
        rd = self.readers
        best = {}
        dmas = {}

        def add(d):
            if d.is_dma:
                dmas[id(d)] = d
            else:
                b = best.get(d.eng)
                if b is None or d.idx > b.idx:
                    best[d.eng] = d

        for r in reads:
            for w in lw.get(r, ()):
                add(w)
        for r in writes:
            ws = lw.get(r)
            rs = rd.get(r)
            if rs:
                for x in rs:
                    add(x)
                if ws:
                    for w in ws:
                        add(w)
                lw[r] = [o]
                rd[r] = []
            elif ws and dma and all(w.is_dma for w in ws):
                ws.append(o)
            else:
                if ws:
                    for w in ws:
                        add(w)
                lw[r] = [o]
                rd[r] = []
        for r in reads:
            rd.setdefault(r, []).append(o)
        o.deps = [d for d in list(best.values()) + list(dmas.values()) if d is not o]
        self.ops[eng].append(o)
        return o

    @staticmethod
    def _skip(d, o):
        return d.eng == o.eng and d.eng == "pe" and not o.is_dma and not d.is_dma

    def emit(self, final_waits=()):
        nc = self.nc
        for e in ENGS:
            for o in self.ops[e]:
                for d in o.deps:
                    if d.is_dma or self._skip(d, o):
                        continue
                    d.sig = True
        sems = {}

        def get_sem(name):
            if name not in sems:
                sems[name] = self.stack.enter_context(nc.semaphore(name))
            return sems[name]

        for e in ENGS:
            cnt = 0
            dma_i = 0
            cc_i = 0
            dma_vals = {}
            dma_last = {}
            for o in self.ops[e]:
                if o.is_dma:
                    if o.inc == 1:
                        name = f"cc_{e}"
                    else:
                        name = f"d_{e}_{dma_i % self.n_dma_sems[e]}"
                        dma_i += 1
                    o.sem = get_sem(name)
                    o.val = dma_vals.get(name, 0) + o.inc
                    dma_vals[name] = o.val
                    o.pre_wait = dma_last.get(name)
                    dma_last[name] = o
                elif o.sig:
                    o.sem = get_sem(f"c_{e}_{cnt // self.sem_wrap}")
                    o.val = cnt % self.sem_wrap + 1
                    cnt += 1
        self.n_waits = 0
        self.n_ins = 0
        with nc.Block() as block:
            def run(e, h):
                waited = {}

                def wait(sem, val):
                    k = id(sem)
                    if waited.get(k, 0) >= val:
                        return
                    waited[k] = val
                    h.wait_ge(sem, val)
                    self.n_waits += 1

                for o in self.ops[e]:
                    if o.pre_wait is not None:
                        wait(o.pre_wait.sem, o.pre_wait.val)
                    for d in o.deps:
                        if d.is_dma:
                            wait(d.sem, d.val)
                        elif d.sig and not self._skip(d, o):
                            wait(d.sem, d.val)
                    ins = o.fn(h)
                    self.n_ins += 1
                    if o.is_dma:
                        ins.then_inc(o.sem, o.inc)
                    elif o.sig:
                        ins.then_inc(o.sem, 1)
                if e == "sp":
                    for o in final_waits:
                        wait(o.sem, o.val)

            if self.ops["pe"]:
                block.tensor(lambda h: run("pe", h))
            if self.ops["act"]:
                block.scalar(lambda h: run("act", h))
            if self.ops["dve"]:
                block.vector(lambda h: run("dve", h))
            if self.ops["pool"]:
                block.gpsimd(lambda h: run("pool", h))
            block.sync(lambda h: run("sp", h))

    def close(self):
        self.stack.close()


class Cfg:
    def __init__(self, T=2048, CTX=2048, depth=4, layer0=0, first=True, last=True, n_cores=8):
        self.T = T
        self.CTX = CTX
        self.CA = min(512, CTX)
        self.depth = depth
        self.n_cores = n_cores


def build_program(cfg):
    T, CTX, CA, DEPTH = cfg.T, cfg.CTX, cfg.CA, cfg.depth
    NSUB = 128
    NST = T // 512
    nc = bass.Bass("TRN2", target_bir_lowering=False)
    P = Prog(nc)

    def din(name, shape, dt=F32):
        return nc.dram_tensor(name, list(shape), dt, kind="ExternalInput").ap()

    def dscr(name, shape, dt=BF16):
        return nc.dram_tensor(name, list(shape), dt, kind="Internal").ap()

    xT_in = din("xT", [D_MODEL, T])
    memT_in = din("memT", [D_MODEL, N_MEM])
    pos_in = din("pos", [1, T], I32)
    consts_in = din("consts", [128, 8])
    gains_in = din("gains", [128, NG_L * DEPTH + 32])
    flag_in = din("flag", [128, 1])
    mask01_in = din("mask01", [128, 640])
    mlamask_in = din("mlamask", [128, 4 * 512])
    biasT_in = din("biasT", [DEPTH * 8 * 128, 640])
    w_in_d = din("w_in", [DEPTH * D_MODEL, W_IN_R])
    w_uq_d = din("w_uq", [DEPTH * Q_LORA, 2048])
    w_ukv_d = din("w_ukv", [DEPTH * KV_LORA, 2048])
    w_out_d = din("w_out", [DEPTH * D_MODEL, D_MODEL])
    w_xq_d = din("w_xq", [DEPTH * D_MODEL, 512])
    w_xkv_d = din("w_xkv", [DEPTH * D_MODEL, 1024])
    w_xo_d = din("w_xo", [DEPTH * 512, D_MODEL])
    w_gate_d = din("w_gate", [DEPTH * D_MODEL, D_FF])
    w_up_d = din("w_up", [DEPTH * D_MODEL, D_FF])
    w_down_d = din("w_down", [DEPTH * D_FF, D_MODEL])
    outT = nc.dram_tensor("outT", [D_MODEL, T], F32, kind="ExternalOutput").ap()

    XR = dscr("XR", [D_MODEL, T], F32)
    ROPE = dscr("ROPE", [4 * 128, T], F32)
    QA = dscr("QA", [8 * 128, T])
    KA = dscr("KA", [8 * 128, T])
    VA = dscr("VA", [T, 1024])
    QN = dscr("QN", [8 * 128, T])
    QPE = dscr("QPE", [4 * 128, T])
    OT = dscr("OT", [D_MODEL, T])
    TB = T // 1024
    chunks = []
    for base in (0, T):
        for s_ in range(0, T, 1024):
            chunks.append((base + s_, 1024))
    chunks.append((2 * T, T // 8 + 512))
    chunks.append((2 * T + T // 8 + 512, 512))
    S_t = [dscr(f"SND{i}", [n_, 1024]) for i, (s_, n_) in enumerate(chunks)]
    G_t = [dscr(f"GTH{i}", [2 * n_, 1024]) for i, (s_, n_) in enumerate(chunks)] if CTX else None

    def xrows(kind, r0, n):
        for i, (s_, n_) in enumerate(chunks):
            if s_ <= r0 and r0 + n <= s_ + n_:
                buf = S_t[i] if kind == "own" else G_t[i]
                return buf[r0 - s_:r0 - s_ + n, :]
        raise AssertionError((r0, n))

    def xKN(kind, h):
        return xrows(kind, h * 128 * TB, 128 * TB).rearrange("(a b) c -> a (b c)", b=TB)

    def xVB(kind, tok0, n):
        return xrows(kind, T + tok0, n)

    def xKPE(kind):
        return xrows(kind, 2 * T, T // 8).rearrange("(a b) c -> a (b c)", b=TB)

    def xKAT(kind):
        return xrows(kind, 2 * T + T // 8, 512).rearrange("r (s c) -> (r s) c", s=2)

    def xVAT(kind):
        return xrows(kind, 2 * T + T // 8 + 512, 512)

    WS = 6144
    wslot = [P.sbuf(f"wslot{i}", [128, WS], BF16) for i in range(4)]
    ain = P.sbuf("ain", [128, 16 * 1024], BF16)
    xs = [P.sbuf(f"xs{i}", [128, 16 * NSUB], F32) for i in range(2)]
    arena = P.sbuf("arena", [128, 11264], F32)
    sqb = P.sbuf("sqb", [128, 16 * NSUB], BF16)
    sql = [P.sbuf(f"sql{i}", [128, 512], BF16) for i in range(3)]
    rstd = [P.sbuf(f"rstd{i}", [128, 512], F32) for i in range(2)]
    NTMP, NPT, NXR, NSTG = 3, 4, 8, 4
    tmpf = [P.sbuf(f"tmpf{i}", [128, 512], F32) for i in range(NTMP)]
    stg = [P.sbuf(f"stg{i}", [128, 512], BF16) for i in range(NSTG)]
    pt = [P.sbuf(f"pt{i}", [128, 640], BF16) for i in range(NPT)]
    xres = [P.sbuf(f"xres{i}", [128, 512], F32) for i in range(NXR)]
    gains = P.sbuf("gains", [128, NG_L * DEPTH + 32], F32)
    consts = P.sbuf("consts", [128, 8], F32)
    flag = P.sbuf("flag", [128, 1], F32)
    ones_bf = P.sbuf("ones_bf", [128, 128], BF16)
    flagones = P.sbuf("flagones", [128, 128], BF16)
    mask01 = P.sbuf("mask01", [128, 640], F32)
    mlamask = P.sbuf("mlamask", [128, 4 * 512], BF16)
    memn = P.sbuf("memn", [128, 16 * 256], BF16)
    kx = P.sbuf("kx", [128, 4 * 256], BF16)
    vx = P.sbuf("vx", [128, 2 * 512], BF16)
    ebias = [P.sbuf(f"ebias{i}", [128, 640], BF16) for i in range(2)]
    ps = P.psum("ps", [128, 8, 512], F32)

    state = {"w": 0, "stg": 0, "tmpf": 0, "pt": 0, "sql": 0, "xres": 0, "rstd": 0, "ev": 0, "apair": 0,
             "xsi": 0, "hb": 0, "grp": 0}

    def rot(key, n):
        v = state[key]
        state[key] = (v + 1) % n
        return v

    def gbank(allowed):
        k = "bank_" + str(allowed)
        i = state.get(k, 0)
        state[k] = i + 1
        return allowed[i % len(allowed)]

    ALLB = [0, 1, 2, 3, 4, 5, 6, 7]
    LOWB = [0, 1, 2, 3]

    def ar(lo, hi):
        return [f"ar{g}" for g in range(lo // 512, (hi + 511) // 512)]

    def ainq(q):
        return [f"ain{q}a", f"ain{q}b"]

    AIN_H = [[f"ain{q}a" for q in range(4)], [f"ain{q}b" for q in range(4)]]
    AIN_ALL = AIN_H[0] + AIN_H[1]

    def xnames(t0, ntok, pref="X"):
        tiles = sorted(set([t0 // 512, (t0 + ntok - 1) // 512]))
        return [f"{pref}{c}_{t}" for c in range(16) for t in tiles]

    def dma(eng, out, in_, reads, writes):
        return P.op(eng, lambda e: e.dma_start(out=out, in_=in_), reads=reads, writes=writes, dma=True)

    def evac_copy(out, in_, reads, writes, scale=None, eng=None):
        e_ = eng or ("act" if rot("ev", 2) == 0 else "dve")
        if e_ == "act":
            if scale is None:
                return P.op("act", lambda e: e.copy(out=out, in_=in_), reads, writes)
            return P.op("act", lambda e: e.mul(out=out, in_=in_, mul=scale), reads, writes)
        if scale is None:
            return P.op("dve", lambda e: e.tensor_copy(out=out, in_=in_), reads, writes)
        return P.op("dve", lambda e: e.tensor_scalar(out=out, in0=in_, scalar1=scale, scalar2=None,
                                                    op0=ALU.mult), reads, writes)

    bgq = []
    BG_EVERY = 4

    def bg_run():
        it = bgq.pop(0)
        (it[1] if isinstance(it, tuple) else it)()

    def bg_step():
        state["grp"] += 1
        if bgq and state["grp"] % BG_EVERY == 0:
            bg_run()

    def bg_flush():
        while bgq:
            bg_run()

    def bg_flush_tag(tag):
        while any(isinstance(it, tuple) and it[0] == tag for it in bgq):
            bg_run()

    def wgroups(KC, c0, ncols, cap=512):
        gw_max = min(cap, (WS // KC) // 128 * 128)
        out = []
        c = c0
        while c < c0 + ncols:
            gw = min(gw_max, c0 + ncols - c)
            out.append((c, gw))
            c += gw
        return out

    pending_cc = []

    def load_w(W_ap, row0, KC, c, gw):
        state["nload"] = state.get("nload", 0) + 1
        if pending_cc and state["nload"] % 2 == 0:
            pending_cc.pop(0)()
        s = rot("w", 4)
        dst = wslot[s][:, 0:KC * gw].rearrange("p (k f) -> p k f", k=KC)
        src = W_ap[row0:row0 + KC * 128, c:c + gw].rearrange("(k p) f -> p k f", p=128)
        dma("pool", dst, src, reads=[], writes=[f"w{s}"])
        return s, dst

    def linear_fm(a_view, a_res, KC, tw, nt, W_ap, row0, c0, ncols, consume, banks=ALLB, pre=None, PF=2):
        groups = [(c, gw, f, j) for (c, gw) in wgroups(KC, c0, ncols) for f in range(0, gw, 128) for j in range(nt)]
        if pre:
            for g in groups[:PF]:
                pre(g[0] + g[2], g[3])
        cur = None
        for gi, (c, gw, f, j) in enumerate(groups):
            if cur is None or cur[0] != c:
                cur = (c,) + load_w(W_ap, row0, KC, c, gw)
            s, wv = cur[1], cur[2]
            if pre and gi + PF < len(groups):
                g2 = groups[gi + PF]
                pre(g2[0] + g2[2], g2[3])
            b = gbank(banks)
            pv = ps[:, b, 0:tw]
            for k in range(KC):
                P.op("pe", (lambda e, pv=pv, wv=wv, k=k, f=f, j=j:
                            e.matmul(pv, wv[:, k, f:f + 128], a_view[:, k, j * tw:(j + 1) * tw],
                                     start=(k == 0), stop=(k == KC - 1))),
                     reads=[f"w{s}"] + a_res, writes=[f"ps{b}"])
            consume(c + f, j, pv, f"ps{b}")
            bg_step()

    def linear_tm(a_view, a_res, KC, ntt, W_ap, row0, c0, ncols, consume, banks=ALLB):
        for (c, gw) in wgroups(KC, c0, ncols):
            s, wv = load_w(W_ap, row0, KC, c, gw)
            for tt in range(ntt):
                b = gbank(banks)
                pv = ps[:, b, 0:gw]
                for k in range(KC):
                    P.op("pe", (lambda e, pv=pv, wv=wv, k=k, tt=tt:
                                e.matmul(pv, a_view[:, k, tt * 128:(tt + 1) * 128], wv[:, k, :],
                                         start=(k == 0), stop=(k == KC - 1))),
                         reads=[f"w{s}"] + a_res, writes=[f"ps{b}"])
                consume(tt, c, gw, pv, f"ps{b}")
                bg_step()

    def rstd_from_psum(pv, psres, n_feat, w):
        r = rot("rstd", 2)
        rv = rstd[r][:, 0:w]
        P.op("act", lambda e: e.activation(out=rv, in_=pv, func=AF.Sqrt, bias=consts[:, 5:6], scale=1.0 / n_feat),
             reads=[psres, "consts"], writes=[f"rstd{r}"])
        P.op("dve", lambda e: e.reciprocal(out=rv, in_=rv), reads=[f"rstd{r}"], writes=[f"rstd{r}"])
        return rv, f"rstd{r}"

    final_ops = []

    def norm_pieces(x_ap, t0, ntok, gcol, dst_fn, dst_res_fn, out_f32_dma=None, pref="X"):
        pieces = []
        nsub_ = ntok // NSUB
        ld_ = {}

        def issue_load(sub):
            xi = rot("xsi", 2)
            xv = xs[xi][:].rearrange("p (c t) -> p c t", c=16)
            a0 = t0 + sub * NSUB
            src = x_ap[:, a0:a0 + NSUB].rearrange("(c p) t -> p c t", p=128)
            dma("sp", xv, src, reads=xnames(a0, NSUB, pref), writes=[f"xs{xi}"])
            ld_[sub] = (xi, xv, a0)

        for sub in range(nsub_):
            def piece(sub=sub):
                if sub not in ld_:
                    issue_load(sub)
                if sub + 1 < nsub_ and (sub + 1) not in ld_:
                    issue_load(sub + 1)
                xi, xv, a0 = ld_.pop(sub)
                b = gbank(ALLB)
                pv = ps[:, b, 0:NSUB]
                for c in range(16):
                    qv = sqb[:, c * NSUB:(c + 1) * NSUB]
                    P.op("act", lambda e, qv=qv, c=c: e.activation(out=qv, in_=xv[:, c, :], func=AF.Square),
                         reads=[f"xs{xi}"], writes=[f"sq{c}"])
                    P.op("pe", lambda e, qv=qv, c=c: e.matmul(pv, ones_bf[:], qv, start=(c == 0), stop=(c == 15)),
                         reads=[f"sq{c}", "ones"], writes=[f"ps{b}"])
                rv, rres = rstd_from_psum(pv, f"ps{b}", D_MODEL, NSUB)
                for c in range(16):
                    if out_f32_dma is None:
                        dv = dst_fn(c, sub * NSUB, NSUB)
                        P.op("dve", (lambda e, dv=dv, c=c:
                                     e.scalar_tensor_tensor(out=dv, in0=xv[:, c, :], scalar=gains[:, gcol + c:gcol + c + 1],
                                                            in1=rv, op0=ALU.mult, op1=ALU.mult)),
                             reads=[f"xs{xi}", rres, "gains"], writes=dst_res_fn(c, sub * NSUB))
                    else:
                        P.op("dve", (lambda e, c=c:
                                     e.scalar_tensor_tensor(out=xv[:, c, :], in0=xv[:, c, :],
                                                            scalar=gains[:, gcol + c:gcol + c + 1],
                                                            in1=rv, op0=ALU.mult, op1=ALU.mult)),
                             reads=[f"xs{xi}", rres, "gains"], writes=[f"xs{xi}"])
                if out_f32_dma is not None:
                    dst = out_f32_dma[:, a0:a0 + NSUB].rearrange("(c p) t -> p c t", p=128)
                    final_ops.append(dma("sp", dst, xv, reads=[f"xs{xi}"], writes=["OUT"]))
            pieces.append(piece)
        return pieces

    ain_v16 = ain[:].rearrange("p (c t) -> p c t", c=16)

    def norm512(x_ap, t0, gcol, hb):
        return norm_pieces(x_ap, t0, 512, gcol,
                           lambda c, s0, w: ain_v16[:, c, hb * 512 + s0:hb * 512 + s0 + w],
                           lambda c, s0: [f"ain{c // 4}{'ab'[hb]}"])

    def norm1024(x_ap, t0, gcol):
        return norm_pieces(x_ap, t0, 1024, gcol,
                           lambda c, s0, w: ain_v16[:, c, s0:s0 + w],
                           lambda c, s0: [f"ain{c // 4}{'ab'[s0 // 512]}"])

    dma("sp", gains[:], gains_in, [], ["gains"])
    dma("sp", consts[:], consts_in, [], ["consts"])
    dma("sp", flag[:], flag_in, [], ["flag"])
    dma("sp", mask01[:], mask01_in, [], ["mask01"])
    dma("pool", mlamask[:], mlamask_in, [], ["mlamask"])
    P.op("dve", lambda e: e.memset(ones_bf[:], 1.0), [], ["ones"])
    P.op("dve", lambda e: e.tensor_scalar(out=flagones[:], in0=ones_bf[:], scalar1=flag[:, 0:1], scalar2=None,
                                          op0=ALU.mult), ["ones", "flag"], ["flagones"])
    QSC_B = (128 + 64) ** -0.5
    QSC_A = 128.0 ** -0.5
    ROPET0 = 8704
    ropet = arena[:, ROPET0:ROPET0 + 2048]
    ROPET_RES = ar(ROPET0, ROPET0 + 2048)
    PR = ar(0, 2560)
    for t0 in range(0, T, 512):
        posi = arena[:, 0:512].bitcast(I32)
        posf = arena[:, 512:1024]
        ang = arena[:, 1024:1536]
        fr = arena[:, 1536:2048]
        dma("sp", posi, pos_in[0:1, t0:t0 + 512].partition_broadcast(128), [], PR)
        P.op("dve", lambda e: e.tensor_copy(out=posf, in_=posi), PR, PR)
        P.op("dve", lambda e: e.tensor_scalar(out=ang, in0=posf, scalar1=consts[:, 0:1], scalar2=None, op0=ALU.mult),
             PR + ["consts"], PR)
        for which in range(2):
            off = 0.25 if which == 0 else 0.0
            zi = arena[:, 0:512].bitcast(I32)
            zf = arena[:, 512:1024]
            mk = arena[:, 2048:2560]
            P.op("dve", lambda e, off=off: e.tensor_scalar(out=fr, in0=ang, scalar1=1.0 / (2 * math.pi), scalar2=off,
                                                           op0=ALU.mult, op1=ALU.add), PR, PR)
            P.op("dve", lambda e: e.tensor_copy(out=zi, in_=fr), PR, PR)
            P.op("dve", lambda e: e.tensor_copy(out=zf, in_=zi), PR, PR)
            P.op("dve", lambda e: e.tensor_tensor(out=fr, in0=fr, in1=zf, op=ALU.subtract), PR, PR)
            P.op("dve", lambda e: e.tensor_scalar(out=mk, in0=fr, scalar1=0.5, scalar2=None, op0=ALU.is_gt), PR, PR)
            P.op("dve", lambda e: e.tensor_tensor(out=fr, in0=fr, in1=mk, op=ALU.subtract), PR, PR)
            P.op("dve", lambda e: e.tensor_scalar(out=mk, in0=fr, scalar1=-0.5, scalar2=None, op0=ALU.is_lt), PR, PR)
            P.op("dve", lambda e: e.tensor_tensor(out=fr, in0=fr, in1=mk, op=ALU.add), PR, PR)
            tv = ropet[:, which * 512:(which + 1) * 512]
            P.op("act", lambda e, tv=tv: e.activation(out=tv, in_=fr, func=AF.Sin, scale=2 * math.pi * 0.999999),
                 PR + ["consts"], ROPET_RES)
            if which == 1:
                P.op("dve", lambda e, tv=tv: e.tensor_scalar(out=tv, in0=tv, scalar1=consts[:, 1:2], scalar2=None,
                                                             op0=ALU.mult), ROPET_RES + ["consts"], ROPET_RES)
            tq = ropet[:, (2 + which) * 512:(3 + which) * 512]
            P.op("dve", lambda e, tv=tv, tq=tq: e.tensor_scalar(out=tq, in0=tv, scalar1=QSC_B, scalar2=None,
                                                                op0=ALU.mult), ROPET_RES, ROPET_RES)
        for r in range(4):
            dma("sp", ROPE[r * 128:(r + 1) * 128, t0:t0 + 512], ropet[:, r * 512:(r + 1) * 512], ROPET_RES, ["ROPE"])
    GM = NG_L * DEPTH
    memn_v = memn[:].rearrange("p (c t) -> p c t", c=16)
    for pc in norm_pieces(memT_in, 0, N_MEM, GM, lambda c, s0, w: memn_v[:, c, s0:s0 + w], lambda c, s0: ["memn"], pref="M"):
        pc()

    def load_ropet(t0):
        dma("sp", ropet.rearrange("p (r t) -> p r t", r=4),
            ROPE[:, t0:t0 + 512].rearrange("(r p) t -> p r t", p=128), ["ROPE"], ROPET_RES)
    CS2 = ropet[:, 0:512]
    SS2 = ropet[:, 512:1024]
    CSq = ropet[:, 1024:1536]
    SSq = ropet[:, 1536:2048]

    HN = dscr("HN", [D_MODEL, T])
    NSTT = T // 1024
    ain_full = ain_v16
    bgq.extend(norm1024(xT_in, 0, 0))
    bg_flush()

    for l in range(DEPTH):
        G0 = NG_L * l
        x_src = xT_in if l == 0 else XR

        kx_v = kx[:].rearrange("p (h m) -> p h m", h=4)
        vx_v = vx[:].rearrange("p (t c) -> p t c", t=2)

        def cons_kx(col, j, pv, pres):
            evac_copy(kx_v[:, col // 128, :], pv, [pres], ["kx"])
        linear_fm(memn_v, ["memn"], 16, 256, 1, w_xkv_d, l * D_MODEL, 0, 512, cons_kx)

        def cons_vx(tt, c, gw, pv, pres):
            evac_copy(vx_v[:, tt, c - 512:c - 512 + gw], pv, [pres], ["vx"])
        linear_tm(memn_v, ["memn"], 16, 2, w_xkv_d, l * D_MODEL, 512, 512, cons_vx)

        def latent_norm(lat, latres, nch, gcol, dstv, dres, j):
            b = gbank(ALLB)
            pv = ps[:, b, :]
            sl = slice(j * 512, (j + 1) * 512)
            for c in range(nch):
                q = rot("sql", 3)
                P.op("act", lambda e, q=q, c=c: e.activation(out=sql[q][:], in_=lat[:, c, sl], func=AF.Square),
                     reads=latres, writes=[f"sql{q}"])
                P.op("pe", lambda e, q=q, c=c: e.matmul(pv, ones_bf[:], sql[q][:], start=(c == 0), stop=(c == nch - 1)),
                     reads=[f"sql{q}", "ones"], writes=[f"ps{b}"])
            rv, rres = rstd_from_psum(pv, f"ps{b}", nch * 128, 512)
            for c in range(nch):
                P.op("dve", (lambda e, c=c: e.scalar_tensor_tensor(out=dstv[:, c, sl], in0=lat[:, c, sl],
                                                                   scalar=gains[:, gcol + c:gcol + c + 1], in1=rv,
                                                                   op0=ALU.mult, op1=ALU.mult)),
                     latres + [rres, "gains"], dres)

        kvlat = arena[:, 0:4096].rearrange("p (c t) -> p c t", c=4)
        kvn_b = arena[:, 4096:6144].bitcast(BF16).rearrange("p (c t) -> p c t", c=4)
        krA = arena[:, 6144:7168]
        krB = arena[:, 7168:8192]
        rope1 = arena[:, 8192:10240]
        R_KVLAT, R_KVN, R_KRA, R_KRB, R_ROPE1 = ar(0, 4096), ar(4096, 6144), ar(6144, 7168), ar(7168, 8192), ar(8192, 10240)
        CS2, SS2 = rope1[:, 0:1024], rope1[:, 1024:2048]
        for st in range(NSTT):
            t0 = st * 1024
            last = (st == NSTT - 1)
            dma("sp", HN[:, t0:t0 + 1024].rearrange("(c p) t -> p c t", p=128), ain_full, AIN_ALL, ["HN"])
            dma("sp", rope1.rearrange("p (r t) -> p r t", r=2),
                ROPE[0:256, t0:t0 + 1024].rearrange("(r p) t -> p r t", p=128), ["ROPE"], R_ROPE1)

            def cons_in1(col, j, pv, pres, t0=t0, last=last):
                tt0 = t0 + j * 512
                if col < 2048:
                    s = rot("stg", NSTG)
                    evac_copy(stg[s][:], pv, [pres], [f"stg{s}"])
                    dma("sp", KA[col - 1024:col - 1024 + 128, tt0:tt0 + 512], stg[s][:], [f"stg{s}"], ["KA"])
                    if last and j == 1 and CTX:
                        dma("sp", xKAT("own")[col - 1024:col - 1024 + 128, :], stg[s][:], [f"stg{s}"], ["KAT"])
                elif col < 4352:
                    ci = (col - 3840) // 128
                    evac_copy(kvlat[:, ci, j * 512:(j + 1) * 512], pv, [pres], ar(ci * 1024 + j * 512, ci * 1024 + j * 512 + 512))
                elif col < 4480:
                    evac_copy(krA[:, j * 512:(j + 1) * 512], pv, [pres], ar(6144 + j * 512, 6144 + j * 512 + 512))
                else:
                    evac_copy(krB[:, j * 512:(j + 1) * 512], pv, [pres], ar(7168 + j * 512, 7168 + j * 512 + 512))

            linear_fm(ain_full, AIN_ALL, 16, 512, 2, w_in_d, l * D_MODEL, 3840, W_IN_R - 3840, cons_in1)

            def krope_piece(j, t0=t0):
                f0 = rot("tmpf", NTMP)
                f1 = rot("tmpf", NTMP)
                sl = slice(j * 512, (j + 1) * 512)
                P.op("dve", lambda e, f0=f0, sl=sl: e.tensor_tensor(out=tmpf[f0][:], in0=krA[:, sl], in1=CS2[:, sl], op=ALU.mult),
                     R_KRA + R_ROPE1, [f"tmpf{f0}"])
                P.op("dve", lambda e, f1=f1, sl=sl: e.tensor_tensor(out=tmpf[f1][:], in0=krB[:, sl], in1=SS2[:, sl], op=ALU.mult),
                     R_KRB + R_ROPE1, [f"tmpf{f1}"])
                s = rot("stg", NSTG)
                P.op("dve", lambda e, s=s, f0=f0, f1=f1: e.tensor_tensor(out=stg[s][:], in0=tmpf[f0][:], in1=tmpf[f1][:],
                                                                         op=ALU.add),
                     [f"tmpf{f0}", f"tmpf{f1}"], [f"stg{s}"])
                dma("sp", xKPE("own")[:, t0 + j * 512:t0 + (j + 1) * 512], stg[s][:], [f"stg{s}"], ["KPE"])
            for j in range(2):
                bgq.append(("lat", lambda j=j: krope_piece(j)))
                bgq.append(("lat", lambda j=j: latent_norm(kvlat, R_KVLAT, 4, G0 + 54, kvn_b, R_KVN, j)))
            state["grp"] = 0
            linear_fm(ain_full, AIN_ALL, 16, 512, 2, w_in_d, l * D_MODEL, 1024, 1024, cons_in1)

            def cons_va(tt, c, gw, pv, pres, t0=t0, last=last):
                s = rot("stg", NSTG)
                evac_copy(stg[s][:, 0:gw], pv, [pres], [f"stg{s}"])
                r0 = t0 + tt * 128
                dma("sp", VA[r0:r0 + 128, c - 2048:c - 2048 + gw], stg[s][:, 0:gw], [f"stg{s}"], ["VA"])
                if last and tt >= 4 and CTX:
                    dma("sp", xVAT("own")[(tt - 4) * 128:(tt - 3) * 128, c - 2048:c - 2048 + gw], stg[s][:, 0:gw],
                        [f"stg{s}"], ["VAT"])
            linear_tm(ain_full, AIN_ALL, 16, 8, w_in_d, l * D_MODEL, 2048, 1024, cons_va)
            bg_flush_tag("lat")
            if not last:
                bgq.extend(norm1024(x_src, t0 + 1024, G0 + 0))
            else:
                dma("sp", ain_full, HN[:, 0:1024].rearrange("(c p) t -> p c t", p=128), ["HN"], AIN_ALL)
            state["grp"] = 0

            def cons_kn(col, j, pv, pres, t0=t0):
                s = rot("stg", NSTG)
                evac_copy(stg[s][:], pv, [pres], [f"stg{s}"])
                dma("sp", xKN("own", col // 128)[:, t0 + j * 512:t0 + (j + 1) * 512], stg[s][:], [f"stg{s}"], ["KN"])
            linear_fm(kvn_b, R_KVN, 4, 512, 2, w_ukv_d, l * KV_LORA, 0, 1024, cons_kn)

            def cons_vb(tt, c, gw, pv, pres, t0=t0):
                s = rot("stg", NSTG)
                evac_copy(stg[s][:, 0:gw], pv, [pres], [f"stg{s}"])
                r0 = t0 + tt * 128
                dma("sp", xVB("own", r0, 128)[:, c - 1024:c - 1024 + gw], stg[s][:, 0:gw], [f"stg{s}"], ["VB"])
            linear_tm(kvn_b, R_KVN, 4, 8, w_ukv_d, l * KV_LORA, 1024, 1024, cons_vb)
            bg_flush()
        if CTX:
            groups_ = [[2 * i, 2 * i + 1] for i in range(cfg.n_cores // 2)]
            def issue_cc(ci):
                P.op("pool", lambda e, ci=ci: e.collective_compute("AllGather", ALU.bypass, replica_groups=groups_,
                                                                   ins=[S_t[ci]], outs=[G_t[ci]]),
                     ["KN", "VB", "KPE", "KAT", "VAT"], ["GATH"], dma="cc")
            issue_cc(0)
            for ci in range(1, len(chunks)):
                pending_cc.append(lambda ci=ci: issue_cc(ci))

        qlat = arena[:, 0:6144].rearrange("p (c t) -> p c t", c=6)
        qn_b = arena[:, 6144:9216].bitcast(BF16).rearrange("p (c t) -> p c t", c=6)
        rope2 = arena[:, 9216:11264]
        R_QLAT, R_QN, R_ROPE2 = ar(0, 6144), ar(6144, 9216), ar(9216, 11264)
        CSq, SSq = rope2[:, 0:1024], rope2[:, 1024:2048]
        for st in range(NSTT):
            t0 = st * 1024
            last = (st == NSTT - 1)
            dma("sp", rope2.rearrange("p (r t) -> p r t", r=2),
                ROPE[256:512, t0:t0 + 1024].rearrange("(r p) t -> p r t", p=128), ["ROPE"], R_ROPE2)

            def cons_in2(col, j, pv, pres, t0=t0):
                tt0 = t0 + j * 512
                if col < 1024:
                    s = rot("stg", NSTG)
                    evac_copy(stg[s][:], pv, [pres], [f"stg{s}"], scale=QSC_A)
                    dma("sp", QA[col:col + 128, tt0:tt0 + 512], stg[s][:], [f"stg{s}"], ["QA"])
                else:
                    ci = (col - 3072) // 128
                    evac_copy(qlat[:, ci, j * 512:(j + 1) * 512], pv, [pres], ar(ci * 1024 + j * 512, ci * 1024 + j * 512 + 512))
            linear_fm(ain_full, AIN_ALL, 16, 512, 2, w_in_d, l * D_MODEL, 3072, 768, cons_in2)
            for j in range(2):
                bgq.append(("lat", lambda j=j: latent_norm(qlat, R_QLAT, 6, G0 + 48, qn_b, R_QN, j)))
            state["grp"] = 0
            linear_fm(ain_full, AIN_ALL, 16, 512, 2, w_in_d, l * D_MODEL, 0, 1024, cons_in2)
            bg_flush_tag("lat")
            if not last:
                dma("sp", ain_full, HN[:, t0 + 1024:t0 + 2048].rearrange("(c p) t -> p c t", p=128), ["HN"], AIN_ALL)
            qra_slot = {}

            def cons_uq(col, j, pv, pres, t0=t0):
                tt0 = t0 + j * 512
                sl = slice(j * 512, (j + 1) * 512)
                if col < 1024:
                    s = rot("stg", NSTG)
                    evac_copy(stg[s][:], pv, [pres], [f"stg{s}"], scale=QSC_B)
                    dma("sp", QN[col:col + 128, tt0:tt0 + 512], stg[s][:], [f"stg{s}"], ["QN"])
                    return
                p_ = (col - 1024) // 256
                if ((col - 1024) // 128) % 2 == 0:
                    fa = rot("tmpf", NTMP)
                    qra_slot[(p_, j)] = fa
                    P.op("dve", lambda e, fa=fa, pv=pv: e.tensor_tensor(out=tmpf[fa][:], in0=pv, in1=CSq[:, sl], op=ALU.mult),
                         [pres] + R_ROPE2, [f"tmpf{fa}"])
                else:
                    fa = qra_slot.pop((p_, j))
                    f0 = rot("tmpf", NTMP)
                    P.op("dve", lambda e, f0=f0, pv=pv: e.tensor_tensor(out=tmpf[f0][:], in0=pv, in1=SSq[:, sl], op=ALU.mult),
                         [pres] + R_ROPE2, [f"tmpf{f0}"])
                    s = rot("stg", NSTG)
                    P.op("dve", lambda e, s=s, f0=f0, fa=fa: e.tensor_tensor(out=stg[s][:], in0=tmpf[f0][:],
                                                                             in1=tmpf[fa][:], op=ALU.add),
                         [f"tmpf{f0}", f"tmpf{fa}"], [f"stg{s}"])
                    dma("sp", QPE[p_ * 128:(p_ + 1) * 128, tt0:tt0 + 512], stg[s][:], [f"stg{s}"], ["QPE"])
            linear_fm(qn_b, R_QN, 6, 512, 2, w_uq_d, l * Q_LORA, 0, 2048, cons_uq)
            bg_flush()

        while pending_cc:
            pending_cc.pop(0)()
        NKA = (CA + T) // 128
        NC4 = CA // 128
        ebf = arena[:, 0:640]
        R_EBF = ar(0, 640)
        OT_ROW = lambda r: [f"OT{r}_{i}" for i in range(NST)]
        A0 = 1024
        bsets = [
            dict(q=ain[:, 0:T], k=ain[:, 4096:4096 + CA + T],
                 v=ain[:, 8192:8192 + NKA * 128].rearrange("p (k d) -> p k d", d=128), o=ain[:, 12288:12288 + T],
                 rq=ainq(0), rk=ainq(1), rv=ainq(2), ro=ainq(3)),
            dict(q=arena[:, A0:A0 + T // 2].bitcast(BF16), k=arena[:, A0 + 2048:A0 + 2048 + (CA + T) // 2].bitcast(BF16),
                 v=arena[:, A0 + 4096:A0 + 4096 + NKA * 64].bitcast(BF16).rearrange("p (k d) -> p k d", d=128),
                 o=arena[:, A0 + 6144:A0 + 6144 + T // 2].bitcast(BF16),
                 rq=ar(A0, A0 + 2048), rk=ar(A0 + 2048, A0 + 4096), rv=ar(A0 + 4096, A0 + 6144), ro=ar(A0 + 6144, A0 + 8192)),
        ]

        def b_load(h):
            S_ = bsets[h % 2]
            dma("sp", S_["q"], QA[h * 128:(h + 1) * 128, :], ["QA"], S_["rq"])
            if CA:
                dma("sp", S_["k"][:, 0:CA], xKAT("ctx")[h * 128:(h + 1) * 128, :], ["GATH"], S_["rk"])
                dma("sp", S_["v"][:, 0:NC4, :], xVAT("ctx")[:, h * 128:(h + 1) * 128].rearrange("(k p) d -> p k d", p=128),
                    ["GATH"], S_["rv"])
            dma("sp", S_["k"][:, CA:CA + T], KA[h * 128:(h + 1) * 128, :], ["KA"], S_["rk"])
            dma("sp", S_["v"][:, NC4:NKA, :], VA[:, h * 128:(h + 1) * 128].rearrange("(k p) d -> p k d", p=128),
                ["VA"], S_["rv"])
            if CA:
                P.op("dve", lambda e, S_=S_: e.tensor_scalar(out=S_["v"][:, 0:NC4, :], in0=S_["v"][:, 0:NC4, :],
                                                            scalar1=flag[:, 0:1], scalar2=None, op0=ALU.mult),
                     S_["rv"] + ["flag"], S_["rv"])
            eb = ebias[h % 2]
            r0 = (l * 8 + h) * 128
            dma("sp", ebf, biasT_in[r0:r0 + 128, :], [], R_EBF)
            P.op("act", lambda e: e.activation(out=ebf, in_=ebf, func=AF.Exp), R_EBF, R_EBF)
            P.op("dve", lambda e, eb=eb: e.tensor_tensor(out=eb[:], in0=ebf, in1=mask01[:], op=ALU.mult),
                 R_EBF + ["mask01"], [f"ebias{h % 2}"])

        DEPTH_P = 2
        b_load(0)
        items = []
        for h in range(A_HEADS):
            for j in range(T // 128):
                items.append((h, j))
        binfo = {}

        def b_qk(idx):
            h, j = items[idx]
            S_ = bsets[h % 2]
            if j == 2 * DEPTH_P and h + 1 < A_HEADS:
                b_load(h + 1)
            qv = S_["q"][:, j * 128:(j + 1) * 128]
            kts = [kt for kt in range(5) if j + NC4 - 4 + kt >= 0]
            pair = rot("apair", 2)
            bA, bB = 2 * pair, 2 * pair + 1
            for kt in kts:
                g = j + NC4 - 4 + kt
                pv = ps[:, bA, kt * 128:(kt + 1) * 128] if kt < 4 else ps[:, bB, 0:128]
                pres = f"ps{bA}" if kt < 4 else f"ps{bB}"
                P.op("pe", lambda e, pv=pv, g=g, qv=qv, S_=S_: e.matmul(pv, S_["k"][:, g * 128:(g + 1) * 128], qv,
                                                                        start=True, stop=True),
                     S_["rk"] + S_["rq"], [pres])
            pi = rot("pt", NPT)
            lo = kts[0] * 128
            eb = ebias[h % 2]
            if lo < 512:
                P.op("act", lambda e, pi=pi, lo=lo, bA=bA:
                     e.activation(out=pt[pi][:, lo:512], in_=ps[:, bA, lo:512], func=AF.Exp),
                     [f"ps{bA}"], [f"pt{pi}"])
            P.op("act", lambda e, pi=pi, bB=bB: e.activation(out=pt[pi][:, 512:640], in_=ps[:, bB, 0:128],
                                                             func=AF.Exp), [f"ps{bB}"], [f"pt{pi}"])
            P.op("dve", lambda e, pi=pi, lo=lo, eb=eb: e.tensor_tensor(out=pt[pi][:, lo:640], in0=pt[pi][:, lo:640],
                                                                       in1=eb[:, lo:640], op=ALU.mult),
                 [f"pt{pi}", f"ebias{h % 2}"], [f"pt{pi}"])
            binfo[idx] = (pi, kts)

        def b_pv(idx):
            h, j = items[idx]
            S_ = bsets[h % 2]
            pi, kts = binfo.pop(idx)
            jg, jj = j // 4, j % 4
            par = (h * (T // 512) + jg) % 2
            bo, bd = 4 + par, 6 + par
            for i_, kt in enumerate(kts):
                g = j + NC4 - 4 + kt
                isctx = g < NC4
                pcol = pt[pi][:, kt * 128:(kt + 1) * 128]
                P.op("pe", lambda e, g=g, pcol=pcol, i_=i_, n=len(kts), S_=S_:
                     e.matmul(ps[:, bo, jj * 128:(jj + 1) * 128], S_["v"][:, g, :], pcol,
                              start=(i_ == 0), stop=(i_ == n - 1)),
                     S_["rv"] + [f"pt{pi}"], [f"ps{bo}"])
                P.op("pe", lambda e, pcol=pcol, i_=i_, n=len(kts), isctx=isctx:
                     e.matmul(ps[:, bd, jj * 128:(jj + 1) * 128], (flagones if isctx else ones_bf)[:], pcol,
                              start=(i_ == 0), stop=(i_ == n - 1)),
                     ["ones", "flagones", f"pt{pi}"], [f"ps{bd}"])
            if jj == 3:
                f0 = rot("tmpf", NTMP)
                P.op("act", lambda e, f0=f0: e.activation(out=tmpf[f0][:], in_=ps[:, bd, :], func=AF.Ln),
                     [f"ps{bd}"], [f"tmpf{f0}"])
                P.op("act", lambda e, f0=f0: e.activation(out=tmpf[f0][:], in_=tmpf[f0][:], func=AF.Exp, scale=-1.0),
                     [f"tmpf{f0}"], [f"tmpf{f0}"])
                P.op("dve", lambda e, f0=f0, S_=S_: e.tensor_tensor(out=S_["o"][:, jg * 512:(jg + 1) * 512],
                                                                    in0=ps[:, bo, :], in1=tmpf[f0][:], op=ALU.mult),
                     [f"ps{bo}", f"tmpf{f0}"], S_["ro"])
                if jg == T // 512 - 1:
                    dma("sp", OT[h * 128:(h + 1) * 128, :], S_["o"], S_["ro"], OT_ROW(h))

        for step in range(len(items) + DEPTH_P):
            if step < len(items):
                b_qk(step)
            if step >= DEPTH_P:
                b_pv(step - DEPTH_P)

        NKB = (CTX + T) // 128
        NCC = CTX // 128
        SK = CTX + T
        kpeA = ain[:, 0:SK]
        kpeB = ain[:, 4096:4096 + SK]
        C0 = 1024
        csets = [
            dict(k=ain[:, 8192:8192 + SK], v=ain[:, 12288:12288 + NKB * 128].rearrange("p (k d) -> p k d", d=128),
                 rk=ainq(2), rv=ainq(3)),
            dict(k=arena[:, C0:C0 + SK // 2].bitcast(BF16),
                 v=arena[:, C0 + 2048:C0 + 2048 + NKB * 64].bitcast(BF16).rearrange("p (k d) -> p k d", d=128),
                 rk=ar(C0, C0 + 2048), rv=ar(C0 + 2048, C0 + 4096)),
        ]
        qnt = [arena[:, 0:256].bitcast(BF16), arena[:, 256:512].bitcast(BF16)]
        qpt = [arena[:, 512:768].bitcast(BF16), arena[:, 768:1024].bitcast(BF16)]
        R_QNT = [["ar0q0"], ["ar0q1"]]
        R_QPT = [["ar1q0"], ["ar1q1"]]
        kpe2 = arena[:, C0 + 4096:C0 + 4096 + SK // 2].bitcast(BF16)
        R_KPE2 = ar(C0 + 4096, C0 + 4096 + SK // 2)
        R_C_AR = ar(0, 1024)
        SUBN = R_QNT[0] + R_QNT[1] + R_QPT[0] + R_QPT[1]
        P.op("dve", lambda e: e.memset(arena[:, 0:8], 0.0), [], R_C_AR + SUBN)
        if CTX:
            dma("sp", kpe2[:, 0:CTX], xKPE("ctx"), ["GATH"], R_KPE2)
        dma("sp", kpe2[:, CTX:SK], xKPE("own"), ["KPE"], R_KPE2)
        P.op("dve", lambda e: e.tensor_scalar(out=kpeA, in0=kpe2, scalar1=consts[:, 2:3], scalar2=None, op0=ALU.mult),
             R_KPE2 + ["consts"], ainq(0))
        P.op("dve", lambda e: e.tensor_scalar(out=kpeB, in0=kpe2, scalar1=consts[:, 3:4], scalar2=None, op0=ALU.mult),
             R_KPE2 + ["consts"], ainq(1))

        def c_load(h):
            S_ = csets[h % 2]
            if CTX:
                dma("sp", S_["k"][:, 0:CTX], xKN("ctx", h), ["GATH"], S_["rk"])
                for t_ in range(0, T, 1024):
                    dma("sp", S_["v"][:, t_ // 128:t_ // 128 + 8, :],
                        xVB("ctx", t_, 1024)[:, h * 128:(h + 1) * 128].rearrange("(k p) d -> p k d", p=128),
                        ["GATH"], S_["rv"])
            dma("sp", S_["k"][:, CTX:SK], xKN("own", h), ["KN"], S_["rk"])
            for t_ in range(0, T, 1024):
                dma("sp", S_["v"][:, NCC + t_ // 128:NCC + t_ // 128 + 8, :],
                    xVB("own", t_, 1024)[:, h * 128:(h + 1) * 128].rearrange("(k p) d -> p k d", p=128),
                    ["VB"], S_["rv"])
            if CTX:
                P.op("dve", lambda e, S_=S_: e.tensor_scalar(out=S_["v"][:, 0:NCC, :], in0=S_["v"][:, 0:NCC, :],
                                                            scalar1=flag[:, 0:1], scalar2=None, op0=ALU.mult),
                     S_["rv"] + ["flag"], S_["rv"])

        def c_qload(h, i):
            qi = (h * NST + i) % 2
            dma("sp", qnt[qi], QN[h * 128:(h + 1) * 128, i * 512:(i + 1) * 512], ["QN"], R_QNT[qi])
            dma("sp", qpt[qi], QPE[(h // 2) * 128:(h // 2 + 1) * 128, i * 512:(i + 1) * 512], ["QPE"], R_QPT[qi])

        c_load(0)
        c_qload(0, 0)
        citems = []
        for h in range(B_HEADS):
            for i in range(NST):
                ktl = list(range(NCC)) + [NCC + o for o in range(4 * i + 4)]
                for i_, g in enumerate(ktl):
                    citems.append((h, i, g, i_, len(ktl)))
        cinfo = {}

        def c_qk(idx):
            h, i, g, i_, n = citems[idx]
            S_ = csets[h % 2]
            if i_ == 0:
                if i + 1 < NST:
                    c_qload(h, i + 1)
                elif h + 1 < B_HEADS:
                    c_qload(h + 1, 0)
            if i == 0 and i_ == DEPTH_P + 1 and h + 1 < B_HEADS:
                c_load(h + 1)
            qi = (h * NST + i) % 2
            kpeX, kpres = (kpeA, ainq(0)) if h % 2 == 0 else (kpeB, ainq(1))
            b = gbank(LOWB)
            P.op("pe", lambda e, b=b, g=g, qi=qi, S_=S_: e.matmul(ps[:, b, :], S_["k"][:, g * 128:(g + 1) * 128], qnt[qi],
                                                                  start=True, stop=False),
                 S_["rk"] + R_QNT[qi], [f"ps{b}"])
            P.op("pe", lambda e, b=b, g=g, qi=qi, kpeX=kpeX: e.matmul(ps[:, b, :], kpeX[:, g * 128:(g + 1) * 128],
                                                                      qpt[qi], start=False, stop=True),
                 kpres + R_QPT[qi], [f"ps{b}"])
            pi = rot("pt", NPT)
            pv = pt[pi][:, 0:512]
            P.op("act", lambda e, pv=pv, b=b: e.activation(out=pv, in_=ps[:, b, :], func=AF.Exp),
                 [f"ps{b}"], [f"pt{pi}"])
            own = g - NCC
            if own >= 4 * i:
                r = own - 4 * i
                P.op("dve", lambda e, pv=pv, r=r: e.tensor_tensor(out=pv, in0=pv, in1=mlamask[:, r * 512:(r + 1) * 512],
                                                                  op=ALU.mult),
                     [f"pt{pi}", "mlamask"], [f"pt{pi}"])
            cinfo[idx] = pi

        def c_pv(idx):
            h, i, g, i_, n = citems[idx]
            S_ = csets[h % 2]
            pi = cinfo.pop(idx)
            pv = pt[pi][:, 0:512]
            qi = (h * NST + i) % 2
            bo, bd = 4 + qi, 6 + qi
            isctx = g < NCC
            P.op("pe", lambda e, g=g, pv=pv, S_=S_: e.matmul(ps[:, bo, :], S_["v"][:, g, :], pv,
                                                             start=(i_ == 0), stop=(i_ == n - 1)),
                 S_["rv"] + [f"pt{pi}"], [f"ps{bo}"])
            P.op("pe", lambda e, pv=pv: e.matmul(ps[:, bd, :], (flagones if isctx else ones_bf)[:], pv,
                                                 start=(i_ == 0), stop=(i_ == n - 1)),
                 ["ones", "flagones", f"pt{pi}"], [f"ps{bd}"])
            if i_ == n - 1:
                f0 = rot("tmpf", NTMP)
                P.op("act", lambda e, f0=f0: e.activation(out=tmpf[f0][:], in_=ps[:, bd, :], func=AF.Ln),
                     [f"ps{bd}"], [f"tmpf{f0}"])
                P.op("act", lambda e, f0=f0: e.activation(out=tmpf[f0][:], in_=tmpf[f0][:], func=AF.Exp, scale=-1.0),
                     [f"tmpf{f0}"], [f"tmpf{f0}"])
                s = rot("stg", NSTG)
                P.op("dve", lambda e, f0=f0, s=s: e.tensor_tensor(out=stg[s][:], in0=ps[:, bo, :], in1=tmpf[f0][:],
                                                                  op=ALU.mult),
                     [f"ps{bo}", f"tmpf{f0}"], [f"stg{s}"])
                dma("sp", OT[(8 + h) * 128:(9 + h) * 128, i * 512:(i + 1) * 512], stg[s][:], [f"stg{s}"], [f"OT{8 + h}_{i}"])

        for step in range(len(citems) + DEPTH_P):
            if step < len(citems):
                c_qk(step)
            if step >= DEPTH_P:
                c_pv(step - DEPTH_P)
        P.op("dve", lambda e: e.memset(arena[:, 0:8], 0.0), SUBN, R_C_AR)

        def make_resid(x_read, t0):
            slots = {}

            def pre(col, j):
                xr = rot("xres", NXR)
                tt0 = t0 + j * 512
                nm = f"X{col // 128}_{tt0 // 512}"
                dma("sp", xres[xr][:], x_read[col:col + 128, tt0:tt0 + 512], [nm], [f"xres{xr}"])
                slots[(col, j)] = xr

            def cons(col, j, pv, pres):
                xr = slots.pop((col, j))
                tt0 = t0 + j * 512
                nm = f"X{col // 128}_{tt0 // 512}"
                P.op("dve", lambda e, xr=xr, pv=pv: e.tensor_tensor(out=xres[xr][:], in0=pv, in1=xres[xr][:], op=ALU.add),
                     [pres, f"xres{xr}"], [f"xres{xr}"])
                dma("sp", XR[col:col + 128, tt0:tt0 + 512], xres[xr][:], [f"xres{xr}"], [nm])
            return cons, pre

        ain2 = arena[:, 0:8192].bitcast(BF16).rearrange("p (c t) -> p c t", c=16)
        dbufs = [(ain_full, AIN_ALL), (ain2, ar(0, 8192))]

        def d_load(st, k):
            dma("sp", dbufs[k][0], OT[:, st * 1024:(st + 1) * 1024].rearrange("(c p) t -> p c t", p=128),
                [f"OT{c}_{i}" for c in range(16) for i in (2 * st, 2 * st + 1)], dbufs[k][1])
        d_load(0, 0)
        e0_queued = False
        for st in range(NSTT):
            t0 = st * 1024
            k = st % 2
            if st + 1 < NSTT:
                d_load(st + 1, 1 - k)
            elif k == 1:
                bgq.extend(norm1024(XR, 0, G0 + 16))
                e0_queued = True
                state["grp"] = 0
            cons, pre = make_resid(x_src, t0)
            linear_fm(dbufs[k][0], dbufs[k][1], 16, 512, 2, w_out_d, l * D_MODEL, 0, D_MODEL, cons, pre=pre)
            bg_flush()
        if not e0_queued:
            bgq.extend(norm1024(XR, 0, G0 + 16))
            bg_flush()

        qx = arena[:, 8192:10240].bitcast(BF16).rearrange("p (h t) -> p h t", h=4)
        ox = arena[:, 0:2048].bitcast(BF16).rearrange("p (h t) -> p h t", h=4)
        R_QX, R_OX = ar(8192, 10240), ar(0, 2048)
        for st in range(NSTT):
            t0 = st * 1024

            def cons_qx(col, j, pv, pres):
                evac_copy(qx[:, col // 128, j * 512:(j + 1) * 512], pv, [pres], R_QX, scale=QSC_A)
            linear_fm(ain_full, AIN_ALL, 16, 512, 2, w_xq_d, l * D_MODEL, 0, 512, cons_qx, banks=LOWB)
            defer = None
            if st + 1 < NSTT:
                bgq.extend(norm1024(XR, t0 + 1024, G0 + 16))
            elif NSTT > 1:
                bgq.extend(norm1024(XR, 0, G0 + 32))
            else:
                defer = norm1024(XR, 0, G0 + 32)
            state["grp"] = 0
            xinfo = {}

            def x_qk(idx):
                j, h, mt = idx // 8, (idx // 2) % 4, idx % 2
                b = gbank(LOWB)
                P.op("pe", lambda e, b=b, h=h, mt=mt, j=j: e.matmul(ps[:, b, :], kx_v[:, h, mt * 128:(mt + 1) * 128],
                                                                    qx[:, h, j * 512:(j + 1) * 512], start=True, stop=True),
                     ["kx"] + R_QX, [f"ps{b}"])
                pi = rot("pt", NPT)
                pv = pt[pi][:, 0:512]
                P.op("act", lambda e, pv=pv, b=b: e.activation(out=pv, in_=ps[:, b, :], func=AF.Exp),
                     [f"ps{b}"], [f"pt{pi}"])
                xinfo[idx] = pi

            def x_pv(idx):
                j, h, mt = idx // 8, (idx // 2) % 4, idx % 2
                pi = xinfo.pop(idx)
                pv = pt[pi][:, 0:512]
                par = (j * 4 + h) % 2
                bo, bd = 4 + par, 6 + par
                P.op("pe", lambda e, pv=pv, h=h, mt=mt: e.matmul(ps[:, bo, :], vx_v[:, mt, h * 128:(h + 1) * 128],
                                                                 pv, start=(mt == 0), stop=(mt == 1)),
                     ["vx", f"pt{pi}"], [f"ps{bo}"])
                P.op("pe", lambda e, pv=pv, mt=mt: e.matmul(ps[:, bd, :], ones_bf[:], pv, start=(mt == 0), stop=(mt == 1)),
                     ["ones", f"pt{pi}"], [f"ps{bd}"])
                if mt == 1:
                    f0 = rot("tmpf", NTMP)
                    P.op("act", lambda e, f0=f0: e.activation(out=tmpf[f0][:], in_=ps[:, bd, :], func=AF.Ln),
                         [f"ps{bd}"], [f"tmpf{f0}"])
                    P.op("act", lambda e, f0=f0: e.activation(out=tmpf[f0][:], in_=tmpf[f0][:], func=AF.Exp, scale=-1.0),
                         [f"tmpf{f0}"], [f"tmpf{f0}"])
                    P.op("dve", lambda e, f0=f0, h=h, j=j: e.tensor_tensor(out=ox[:, h, j * 512:(j + 1) * 512], in0=ps[:, bo, :],
                                                                           in1=tmpf[f0][:], op=ALU.mult),
                         [f"ps{bo}", f"tmpf{f0}"], R_OX)
            for step in range(16 + DEPTH_P):
                if step < 16:
                    x_qk(step)
                if step >= DEPTH_P:
                    x_pv(step - DEPTH_P)
            cons, pre = make_resid(XR, t0)
            linear_fm(ox, R_OX, 4, 512, 2, w_xo_d, l * 512, 0, D_MODEL, cons, pre=pre, PF=6)
            if defer:
                bgq.extend(defer)
            bg_flush()

        NFH = D_FF // 2 // 128
        actv = arena[:, 0:11264].bitcast(BF16).rearrange("p (c t) -> p c t", c=NFH)
        R_ACT = ar(0, 11264)
        NF = T // 1024
        deferF = None
        for st in range(NF):
            t0 = st * 1024
            for half in range(2):
                c_base = half * (D_FF // 2)
                for (c, gw) in wgroups(16, c_base, D_FF // 2, cap=384):
                    sg, wg = load_w(w_gate_d, l * D_MODEL, 16, c, gw)
                    su, wu = load_w(w_up_d, l * D_MODEL, 16, c, gw)
                    for f in range(0, gw, 128):
                        fc = (c + f - c_base) // 128
                        for j in range(2):
                            bg = gbank(ALLB)
                            bu = gbank(ALLB)
                            for k in range(16):
                                P.op("pe", lambda e, bg=bg, wg=wg, k=k, f=f, j=j:
                                     e.matmul(ps[:, bg, :], wg[:, k, f:f + 128], ain_v16[:, k, j * 512:(j + 1) * 512],
                                              start=(k == 0), stop=(k == 15)),
                                     [f"w{sg}"] + AIN_H[j], [f"ps{bg}"])
                            for k in range(16):
                                P.op("pe", lambda e, bu=bu, wu=wu, k=k, f=f, j=j:
                                     e.matmul(ps[:, bu, :], wu[:, k, f:f + 128], ain_v16[:, k, j * 512:(j + 1) * 512],
                                              start=(k == 0), stop=(k == 15)),
                                     [f"w{su}"] + AIN_H[j], [f"ps{bu}"])
                            f0 = rot("tmpf", NTMP)
                            P.op("act", lambda e, f0=f0, bg=bg: e.activation(out=tmpf[f0][:], in_=ps[:, bg, :], func=AF.Silu),
                                 [f"ps{bg}"], [f"tmpf{f0}"])
                            P.op("dve", lambda e, f0=f0, bu=bu, fc=fc, j=j:
                                 e.tensor_tensor(out=actv[:, fc, j * 512:(j + 1) * 512], in0=ps[:, bu, :], in1=tmpf[f0][:],
                                                 op=ALU.mult),
                                 [f"ps{bu}", f"tmpf{f0}"], [f"ar{fc}"])
                if half == 1:
                    if st + 1 < NF:
                        bgq.extend(norm1024(XR, t0 + 1024, G0 + 32))
                    elif l + 1 < DEPTH:
                        if NF > 1:
                            bgq.extend(norm1024(XR, 0, NG_L * (l + 1)))
                        else:
                            deferF = norm1024(XR, 0, NG_L * (l + 1))
                    state["grp"] = 0
                cons, pre = make_resid(XR, t0)
                linear_fm(actv, R_ACT, NFH, 512, 2, w_down_d, l * D_FF + c_base, 0, D_MODEL, cons, pre=pre)
            if deferF:
                bgq.extend(deferF)
                deferF = None
            bg_flush()

    GF = NG_L * DEPTH + 16
    for t0 in range(0, T, 512):
        for pc in norm_pieces(XR, t0, 512, GF, None, None, out_f32_dma=outT):
            pc()

    P.emit(final_waits=final_ops)
    P.close()
    return nc, P


def _fm_cols(g):
    return np.ascontiguousarray(g.reshape(-1, 128).T)


def prepare_shared(inputs, depth, layer0=0):
    f32 = np.float32
    L = slice(layer0, layer0 + depth)
    w_in = np.asarray(inputs["w_in"])[L]
    kr = w_in[:, :, 4352:4416]
    kr_sw = np.concatenate([kr[:, :, 32:], kr[:, :, :32]], axis=-1)
    w_in_r = np.concatenate([w_in[:, :, :4352], kr, kr, kr_sw, kr_sw], axis=-1)
    w_uq = np.asarray(inputs["w_uq"])[L].reshape(depth, Q_LORA, 8, 192)
    nope = w_uq[..., :128].reshape(depth, Q_LORA, 1024)
    rope = w_uq[..., 128:]
    rope_sw = np.concatenate([rope[..., 32:], rope[..., :32]], axis=-1)
    ra = rope.reshape(depth, Q_LORA, 4, 128)
    rb_ = rope_sw.reshape(depth, Q_LORA, 4, 128)
    w_uq_r = np.concatenate([nope, np.stack([ra, rb_], axis=3).reshape(depth, Q_LORA, 1024)], axis=-1)
    w_ukv = np.asarray(inputs["w_ukv"])[L].reshape(depth, KV_LORA, 8, 256)
    w_ukv_r = np.concatenate([w_ukv[..., :128].reshape(depth, KV_LORA, 1024),
                              w_ukv[..., 128:].reshape(depth, KV_LORA, 1024)], axis=-1)
    gcols = []
    for l in range(layer0, layer0 + depth):
        gcols += [_fm_cols(np.asarray(inputs["norm_mix"])[l]), _fm_cols(np.asarray(inputs["norm_mem"])[l]),
                  _fm_cols(np.asarray(inputs["norm_ffn"])[l]), _fm_cols(np.asarray(inputs["q_norm"])[l]),
                  _fm_cols(np.asarray(inputs["kv_norm"])[l])]
    gcols += [_fm_cols(np.asarray(inputs["mem_norm"])), _fm_cols(np.asarray(inputs["norm_final"]))]
    gains = np.concatenate(gcols, axis=1).astype(f32)
    kk = np.arange(640)[:, None]
    qq = np.arange(128)[None, :]
    rel = np.clip(512 + qq - kk, -REL_CLIP, REL_CLIP) + REL_CLIP
    rb = np.asarray(inputs["rel_bias"])[L]
    bt = rb[:, :, rel]
    bt = bt.reshape(depth, 8, 5, 128, 128).transpose(0, 1, 3, 2, 4).reshape(depth * 8 * 128, 640)
    cq = 8 + qq // 64
    ck = kk // 64
    m01 = ((ck >= cq - 8) & (ck <= cq)).astype(f32)
    m01 = m01.reshape(5, 128, 128).transpose(1, 0, 2).reshape(128, 640)
    kq = (np.arange(128)[:, None] // 64)
    qc = (np.arange(512)[None, :] // 64)
    mm = [((2 * r + kq) <= qc).astype(f32) for r in range(4)]
    mlamask = np.concatenate(mm, axis=1)
    half = 32
    inv = (ROPE_THETA ** (-np.arange(half, dtype=f32) / half)).astype(f32)
    consts = np.zeros((128, 8), f32)
    consts[:, 0] = np.tile(inv, 4)
    consts[:, 1] = np.tile(np.concatenate([-np.ones(32, f32), np.ones(32, f32)]), 2)
    consts[:64, 2] = 1.0
    consts[64:, 3] = 1.0
    consts[:, 4] = -math.pi
    consts[:, 5] = EPS

    def flat(a):
        a = np.asarray(a)
        return np.ascontiguousarray(a.reshape(-1, a.shape[-1]), dtype=f32)

    return {
        "consts": consts, "gains": gains, "mask01": np.ascontiguousarray(m01), "mlamask": np.ascontiguousarray(mlamask),
        "biasT": np.ascontiguousarray(bt, dtype=f32),
        "w_in": flat(w_in_r), "w_uq": flat(w_uq_r), "w_ukv": flat(w_ukv_r),
        "w_out": flat(np.asarray(inputs["w_out"])[L]), "w_xq": flat(np.asarray(inputs["w_xq"])[L]),
        "w_xkv": flat(np.asarray(inputs["w_xkv"])[L]), "w_xo": flat(np.asarray(inputs["w_xo"])[L]),
        "w_gate": flat(np.asarray(inputs["w_gate"])[L]), "w_up": flat(np.asarray(inputs["w_up"])[L]),
        "w_down": flat(np.asarray(inputs["w_down"])[L]),
    }


_CACHE = {}


def kernel(**inputs):
    x = np.asarray(inputs["x"])
    mem = np.asarray(inputs["mem"])
    pos = np.asarray(inputs["positions"])
    B, S, D = x.shape
    NH = 2
    T = S // NH
    n_cores = B * NH
    depth = int(np.asarray(inputs["w_in"]).shape[0])
    cfg = Cfg(T=T, CTX=T, depth=depth, n_cores=n_cores)
    key = (T, T, depth, n_cores)
    if key not in _CACHE:
        _CACHE[key] = build_program(cfg)[0]
    nc = _CACHE[key]
    shared = prepare_shared(inputs, depth)
    in_maps = []
    for c in range(n_cores):
        b, hf = c // NH, c % NH
        m = dict(shared)
        m["xT"] = np.ascontiguousarray(x[b, hf * T:(hf + 1) * T].T)
        m["memT"] = np.ascontiguousarray(mem[b].T)
        m["pos"] = np.ascontiguousarray(pos[b:b + 1, hf * T:(hf + 1) * T]).astype(np.int32)
        m["flag"] = np.full((128, 1), 1.0 if hf > 0 else 0.0, np.float32)
        in_maps.append(m)
    res = run_bass_kernel_spmd(nc, in_maps, core_ids=list(range(n_cores)))
    out = np.empty((B, S, D), np.float32)
    for c in range(n_cores):
        b, hf = c // NH, c % NH
        out[b, hf * T:(hf + 1) * T] = res.results[c]["outT"].T
    return out
```
